# Optimizing a Trainium2 kernel written in Bass

```python
import jax
import jax.numpy as jnp
from jax import lax
import numpy as np

D_MODEL = 1024
BATCH = 8
SEQ = 4096
DEPTH = 4

CTX_LEN = 256
GRID_W = 64
N_DIR = 2
MIX_W = 512
N_BRANCH = 3
HG_HEADS = 4
HG_DK = 128
HG_DV = MIX_W // HG_HEADS
ML_HEADS = 4
ML_DV = MIX_W // ML_HEADS
ML_DQK = ML_DV // 2
LRU_W = MIX_W
LRU_BLOCKS = 8
LRU_BD = LRU_W // LRU_BLOCKS
LRU_C = 8.0
CONV_W = 4
CONV_LEFT = 2
FFN_HIDDEN = ((8 * D_MODEL // 3 + 255) // 256) * 256
CHUNK = 64
EPS = 1e-6
NEG_BIG = -1e30
LB_TINY = 1e-30
IN_SIZES = (
    HG_HEADS * HG_DK,
    HG_HEADS * HG_DV,
    HG_HEADS * HG_DV,
    N_DIR * HG_HEADS * HG_DK,
    ML_HEADS * ML_DQK,
    ML_HEADS * ML_DQK,
    ML_HEADS * ML_DV,
    ML_HEADS * ML_DV,
    N_DIR * ML_HEADS,
    N_DIR * ML_HEADS,
    LRU_W,
    LRU_W,
    N_BRANCH * D_MODEL,
)
N_IN = sum(IN_SIZES)
ML_FGATE_IDX = 9
N_MIX_PARTS = 12

kernel_name = 'hybrid_hgrn2_mlstm_rglru_dit_trunk'


def rms_norm(x, w):
    xf = x.astype(jnp.float32)
    y = xf * lax.rsqrt(jnp.mean(xf * xf, axis=-1, keepdims=True) + EPS)
    return (y * w).astype(x.dtype)


def modulate(h, shift, scale):
    return h * (1 + scale) + shift


def grid_transpose(h, rows, cols):
    b, _, d = h.shape
    return h.reshape(b, rows, cols, d).transpose(0, 2, 1, 3).reshape(b, rows * cols, d)


def split_in(u):
    return jnp.split(u, np.cumsum(IN_SIZES)[:-1].tolist(), axis=-1)


def to_heads(a, n_heads):
    b, t, _ = a.shape
    return a.reshape(b, t, n_heads, -1).transpose(0, 2, 1, 3)


def to_chunks(a):
    b, h, t = a.shape[:3]
    return jnp.moveaxis(a.reshape(b, h, t // CHUNK, CHUNK, *a.shape[3:]), 2, 0)


def from_chunks(a):
    a = jnp.moveaxis(a, 0, 2)
    return a.reshape(a.shape[0], a.shape[1], -1, *a.shape[4:])


def head_rms_norm(h, w):
    h = h * lax.rsqrt(jnp.mean(h * h, axis=-1, keepdims=True) + EPS)
    b, nh, t, dh = h.shape
    return h.transpose(0, 2, 1, 3).reshape(b, t, nh * dh) * w


def _orient(a, axis, rev):
    return jnp.flip(a, axis) if rev else a


def hgrn2_chunkwise(q, k, v, log_f, s0):
    mask = jnp.tril(jnp.ones((CHUNK, CHUNK), dtype=bool))[:, :, None]

    def step(s_prev, inp):
        qc, kc, vc, gc = inp
        b = jnp.cumsum(gc, axis=2)
        log_decay = jnp.where(mask, b[:, :, :, None, :] - b[:, :, None, :, :], NEG_BIG)
        scores = jnp.einsum('bhtk,bhsk,bhtsk->bhts', qc, kc, jnp.exp(log_decay))
        o = scores @ vc + (qc * jnp.exp(b)) @ s_prev
        b_last = b[:, :, -1:, :]
        s_new = (jnp.exp(b_last[:, :, 0, :, None]) * s_prev
                 + jnp.einsum('bhsk,bhsv->bhkv', kc * jnp.exp(b_last - b), vc))
        return s_new, o

    s_last, o = lax.scan(step, s0, tuple(to_chunks(a) for a in (q, k, v, log_f)))
    return from_chunks(o), s_last


def mlstm_chunkwise(q, k, v, log_i, log_f, state):
    mask = jnp.tril(jnp.ones((CHUNK, CHUNK), dtype=bool))

    def step(carry, inp):
        c_mem, n_mem, m_prev = carry
        qc, kc, vc, ic, fc = inp
        b = jnp.cumsum(fc, axis=-1)
        log_d = jnp.where(mask, b[..., :, None] - b[..., None, :] + ic[..., None, :], NEG_BIG)
        log_inter = b + m_prev[..., None]
        m_row = jnp.maximum(jnp.max(log_d, axis=-1), log_inter)
        s = jnp.einsum('bhtk,bhsk->bhts', qc, kc) * jnp.exp(log_d - m_row[..., None])
        w_inter = jnp.exp(log_inter - m_row)
        num = s @ vc + w_inter[..., None] * (qc @ c_mem)
        den = jnp.sum(s, axis=-1) + w_inter * jnp.einsum('bhtk,bhk->bht', qc, n_mem)
        h = num / jnp.maximum(jnp.abs(den), jnp.exp(-m_row))[..., None]
        b_last = b[..., -1]
        log_w = b_last[..., None] - b + ic
        m_new = jnp.maximum(b_last + m_prev, jnp.max(log_w, axis=-1))
        w_s = jnp.exp(log_w - m_new[..., None])
        carry_decay = jnp.exp(b_last + m_prev - m_new)
        c_new = carry_decay[..., None, None] * c_mem + jnp.einsum('bhs,bhsk,bhsv->bhkv', w_s, kc, vc)
        n_new = carry_decay[..., None] * n_mem + jnp.einsum('bhs,bhsk->bhk', w_s, kc)
        return (c_new, n_new, m_new), h

    carry, h = lax.scan(step, state, tuple(to_chunks(a) for a in (q, k, v, log_i, log_f)))
    return from_chunks(h), carry


def _linear_combine(left, right):
    a_l, b_l = left
    a_r, b_r = right
    return a_l * a_r, a_r * b_l + b_r


def rglru(xc, r_pre, i_pre, lam, h0):
    log_a = -LRU_C * jax.nn.softplus(-lam) * jax.nn.sigmoid(r_pre)
    a = jnp.exp(log_a)
    u = jnp.sqrt(jnp.maximum(-jnp.expm1(2.0 * log_a), 0.0)) * (jax.nn.sigmoid(i_pre) * xc)
    a_cum, h = lax.associative_scan(_linear_combine, (a, u), axis=1)
    h = h + a_cum * h0[:, None, :]
    return h, h[:, -1]


def depthwise_conv(x, w, b):
    t = x.shape[1]
    xp = jnp.pad(x, ((0, 0), (CONV_LEFT, CONV_W - 1 - CONV_LEFT), (0, 0)))
    return sum(xp[:, j:j + t] * w[j] for j in range(CONV_W)) + b


def zero_states(batch):
    f32 = jnp.float32
    one = (jnp.zeros((batch, HG_HEADS, HG_DK, HG_DV), f32),
           jnp.zeros((batch, ML_HEADS, ML_DQK, ML_DV), f32),
           jnp.zeros((batch, ML_HEADS, ML_DQK), f32),
           jnp.zeros((batch, ML_HEADS), f32),
           jnp.zeros((batch, LRU_W), f32))
    return (one, one)


def token_mixers(parts, lb, hg_norm_w, ml_norm_w, conv_w, conv_b, lru_gate_w, lru_gate_b,
                 lru_lambda, init, with_outputs):
    hg_q, hg_i, hg_g, hg_f, ml_q, ml_k, ml_v, ml_o, ml_ig, ml_fg, lru_x, lru_y = parts
    f32 = jnp.float32
    dt = hg_q.dtype
    bsz, t, _ = hg_q.shape
    q_hg = to_heads(jax.nn.silu(hg_q.astype(f32)), HG_HEADS)
    v_hg = to_heads(hg_i.astype(f32), HG_HEADS)
    f_pre = hg_f.astype(f32).reshape(bsz, t, N_DIR, HG_HEADS * HG_DK)
    log_lb = jnp.log(jnp.maximum(lb, LB_TINY))
    logf_hg = jnp.logaddexp(log_lb, jnp.log1p(-lb) + jax.nn.log_sigmoid(f_pre))
    k_hg = (1.0 - lb) * jax.nn.sigmoid(-f_pre)
    q_ml = to_heads(ml_q.astype(f32), ML_HEADS) * ML_DQK ** -0.5
    k_ml = to_heads(ml_k.astype(f32), ML_HEADS)
    v_ml = to_heads(ml_v.astype(f32), ML_HEADS)
    logi_ml = ml_ig.astype(f32).reshape(bsz, t, N_DIR, ML_HEADS).transpose(2, 0, 3, 1)
    logf_ml = jax.nn.log_sigmoid(ml_fg.astype(f32).reshape(bsz, t, N_DIR, ML_HEADS)).transpose(2, 0, 3, 1)
    xc = depthwise_conv(lru_x, conv_w, conv_b).astype(f32)
    gates = jnp.einsum('btnd,zgnde->zgbtne', xc.reshape(bsz, t, LRU_BLOCKS, LRU_BD),
                       lru_gate_w.astype(f32)).reshape(N_DIR, 2, bsz, t, LRU_W)
    gates = gates + lru_gate_b.astype(f32)[:, :, None, None, :]

    o_hg = 0.0
    h_ml = 0.0
    h_lru = 0.0
    finals = []
    for d in range(N_DIR):
        rev = d == 1
        s_hg, c_ml, n_ml, m_ml, s_lru = init[d]
        o, s_hg = hgrn2_chunkwise(_orient(q_hg, 2, rev),
                                  _orient(to_heads(k_hg[:, :, d], HG_HEADS), 2, rev),
                                  _orient(v_hg, 2, rev),
                                  _orient(to_heads(logf_hg[:, :, d], HG_HEADS), 2, rev), s_hg)
        o_hg = o_hg + _orient(o, 2, rev)
        h, (c_ml, n_ml, m_ml) = mlstm_chunkwise(_orient(q_ml, 2, rev), _orient(k_ml, 2, rev),
                                                _orient(v_ml, 2, rev), _orient(logi_ml[d], 2, rev),
                                                _orient(logf_ml[d], 2, rev), (c_ml, n_ml, m_ml))
        h_ml = h_ml + _orient(h, 2, rev)
        hl, s_lru = rglru(_orient(xc, 1, rev), _orient(gates[d, 0], 1, rev),
                          _orient(gates[d, 1], 1, rev), lru_lambda[d].astype(f32), s_lru)
        h_lru = h_lru + _orient(hl, 1, rev)
        finals.append((s_hg, c_ml, n_ml, m_ml, s_lru))
    if not with_outputs:
        return None, finals
    a_out = head_rms_norm(o_hg, hg_norm_w) * jax.nn.silu(hg_g.astype(f32))
    b_out = head_rms_norm(h_ml, ml_norm_w) * jax.nn.sigmoid(ml_o.astype(f32))
    c_out = h_lru * jax.nn.gelu(lru_y.astype(f32))
    return (a_out.astype(dt), b_out.astype(dt), c_out.astype(dt)), finals


def merge_branches(branches, gate_pre, w_branch, w_out):
    bsz, t, _ = gate_pre.shape
    g = jax.nn.sigmoid(gate_pre.reshape(bsz, t, N_BRANCH, D_MODEL))
    merged = sum(g[:, :, n] * (branches[n] @ w_branch[n]) for n in range(N_BRANCH))
    return merged @ w_out


def swiglu(h, w_in, w_out):
    gate, up = jnp.split(h @ w_in, 2, axis=-1)
    return (jax.nn.silu(gate) * up) @ w_out


def setup_inputs(seed: int = 0) -> dict:
    key = jax.random.key(seed)
    ks = jax.random.split(key, 24)
    f32 = jnp.float32

    def nrm(k, shape, fan_in):
        return jax.random.normal(k, shape, f32) * fan_in ** -0.5

    def small(k, shape):
        return 0.02 * jax.random.normal(k, shape, f32)

    b_in = small(ks[8], (DEPTH, N_IN))
    f_off = int(sum(IN_SIZES[:ML_FGATE_IDX]))
    f_bias = jnp.tile(jnp.linspace(3.0, 6.0, ML_HEADS, dtype=f32), N_DIR)
    b_in = b_in.at[:, f_off:f_off + N_DIR * ML_HEADS].add(f_bias)
    a_base = jax.random.uniform(ks[16], (DEPTH, N_DIR, LRU_W), f32, 0.9, 0.999)
    s_base = a_base ** (1.0 / LRU_C)
    return {
        'x': jax.random.normal(ks[0], (BATCH, SEQ, D_MODEL), f32),
        'c': jax.random.normal(ks[1], (BATCH, D_MODEL), f32),
        'ctx': jax.random.normal(ks[2], (BATCH, CTX_LEN, D_MODEL), f32),
        'c_ctx': jax.random.normal(ks[3], (D_MODEL,), f32),
        'w_ada': 0.5 * nrm(ks[4], (DEPTH, D_MODEL, 6 * D_MODEL), D_MODEL),
        'b_ada': small(ks[5], (DEPTH, 6 * D_MODEL)),
        'ln1': 1.0 + small(ks[6], (DEPTH, D_MODEL)),
        'w_in': nrm(ks[7], (DEPTH, D_MODEL, N_IN), D_MODEL),
        'b_in': b_in,
        'hg_lb_raw': 0.5 * jax.random.normal(ks[9], (DEPTH, N_DIR, HG_HEADS * HG_DK), f32),
        'hg_norm': 1.0 + small(ks[10], (DEPTH, MIX_W)),
        'ml_norm': 1.0 + small(ks[11], (DEPTH, MIX_W)),
        'conv_w': nrm(ks[12], (DEPTH, CONV_W, LRU_W), CONV_W),
        'conv_b': small(ks[13], (DEPTH, LRU_W)),
        'lru_gate_w': nrm(ks[14], (DEPTH, N_DIR, 2, LRU_BLOCKS, LRU_BD, LRU_BD), LRU_BD),
        'lru_gate_b': small(ks[15], (DEPTH, N_DIR, 2, LRU_W)),
        'lru_lambda': jnp.log(s_base) - jnp.log1p(-s_base),
        'w_branch': nrm(ks[17], (DEPTH, N_BRANCH, MIX_W, D_MODEL), MIX_W),
        'w_out': nrm(ks[18], (DEPTH, D_MODEL, D_MODEL), D_MODEL),
        'ln2': 1.0 + small(ks[19], (DEPTH, D_MODEL)),
        'w_ffn_in': nrm(ks[20], (DEPTH, D_MODEL, 2 * FFN_HIDDEN), D_MODEL),
        'w_ffn_out': nrm(ks[21], (DEPTH, FFN_HIDDEN, D_MODEL), FFN_HIDDEN),
        'final_norm': 1.0 + small(ks[22], (D_MODEL,)),
    }


def reference(x, c, ctx, c_ctx, w_ada, b_ada, ln1, w_in, b_in, hg_lb_raw, hg_norm, ml_norm,
              conv_w, conv_b, lru_gate_w, lru_gate_b, lru_lambda, w_branch, w_out, ln2,
              w_ffn_in, w_ffn_out, final_norm):
    bsz, seq, _ = x.shape
    rows = seq // GRID_W
    lb_p = jax.nn.softmax(hg_lb_raw.astype(jnp.float32), axis=0)
    lb_all = jnp.cumsum(lb_p, axis=0) - lb_p[0]
    sc = jax.nn.silu(c)
    sc_ctx = jax.nn.silu(c_ctx)
    for l in range(DEPTH):
        last = l == DEPTH - 1
        col_major = l % 2 == 1
        mod = jnp.split(sc @ w_ada[l] + b_ada[l], 6, axis=-1)
        mod_c = jnp.split(sc_ctx @ w_ada[l] + b_ada[l], 6, axis=-1)
        mix_params = (lb_all[l], hg_norm[l], ml_norm[l], conv_w[l], conv_b[l],
                      lru_gate_w[l], lru_gate_b[l], lru_lambda[l])
        hx = modulate(rms_norm(x, ln1[l]), mod[0][:, None], mod[1][:, None])
        hc = modulate(rms_norm(ctx, ln1[l]), mod_c[0], mod_c[1])
        if col_major:
            hx = grid_transpose(hx, rows, GRID_W)
        px = split_in(hx @ w_in[l] + b_in[l])
        pc = split_in(hc @ w_in[l] + b_in[l])
        br_c, st_c = token_mixers(pc[:N_MIX_PARTS], *mix_params, zero_states(ctx.shape[0]), not last)
        br_x, _ = token_mixers(px[:N_MIX_PARTS], *mix_params, st_c, True)
        yx = merge_branches(br_x, px[N_MIX_PARTS], w_branch[l], w_out[l])
        if col_major:
            yx = grid_transpose(yx, GRID_W, rows)
        x = x + mod[2][:, None] * yx
        x = x + mod[5][:, None] * swiglu(
            modulate(rms_norm(x, ln2[l]), mod[3][:, None], mod[4][:, None]), w_ffn_in[l], w_ffn_out[l])
        if not last:
            ctx = ctx + mod_c[2] * merge_branches(br_c, pc[N_MIX_PARTS], w_branch[l], w_out[l])
            ctx = ctx + mod_c[5] * swiglu(
                modulate(rms_norm(ctx, ln2[l]), mod_c[3], mod_c[4]), w_ffn_in[l], w_ffn_out[l])
    return rms_norm(x, final_norm)
```

```python
import numpy as np
from contextlib import ExitStack
import concourse.bass as bass
import concourse.mybir as mybir
from concourse.bass_utils import run_bass_kernel_spmd

F32 = mybir.dt.float32
BF16 = mybir.dt.bfloat16
AF = mybir.ActivationFunctionType
ALU = mybir.AluOpType
AX = mybir.AxisListType

D = 1024
SEQ = 4096
CTX = 256
T = SEQ + CTX
NT = T // 128
DEPTH = 4
N_IN = 8208
FFN = 2816
EPS = 1e-6
NFM = 32
NTM = 1296
FM_GROUP_COL = [0, 1024, 1536, 2048, 2560, 3584, 4112, 4624]
GATE_COL = 5136

COLS = {}
_o = 0
for _n, _w in [("c", 8), ("cctx", 8), ("fnorm", 8), ("ln1", 32), ("ln2", 32), ("bfm", 4 * 32), ("bgate", 4 * 24),
               ("lbraw", 32), ("hgn", 16), ("mln", 16), ("convw", 64), ("convb", 16), ("lgb", 64), ("lam", 32)]:
    COLS[_n] = (_o, _w)
    _o += _w
NCOLS = _o


class Builder:
    NDSEM = 24

    def __init__(self):
        self.nc = bass.Bass("TRN2", target_bir_lowering=False)
        self.es = ExitStack()
        nc = self.nc
        self.eng = {"pe": nc.tensor, "act": nc.scalar, "dve": nc.vector, "pool": nc.gpsimd, "sp": nc.sync}
        self.esem = {e: self.es.enter_context(nc.semaphore("se_" + e)) for e in self.eng}
        self.ecnt = {e: 0 for e in self.eng}
        self.seen = {e: {} for e in self.eng}
        self.dsem = [self.es.enter_context(nc.semaphore("sd%d" % i)) for i in range(self.NDSEM)]
        self.dcum = [0] * self.NDSEM
        self.dnext = 0
        self.buf = {}
        self.scopes = []

    def push(self):
        st = ExitStack()
        self.scopes.append(st)
        return st

    def pop(self):
        self.barrier()
        self.scopes.pop().close()

    def _stack(self):
        return self.scopes[-1] if self.scopes else self.es

    def sb(self, name, shape, dt=F32):
        self.nuid = getattr(self, "nuid", 0) + 1
        name = "%s_%d" % (name, self.nuid)
        return self._stack().enter_context(self.nc.sbuf_tensor(name, list(shape), dt))

    def ps(self, name, shape, dt=F32):
        return self.es.enter_context(self.nc.psum_tensor(name, list(shape), dt))

    def dram(self, name, shape, dt=F32, kind="Internal"):
        return self.nc.dram_tensor(name, list(shape), dt, kind=kind).ap()

    @staticmethod
    def keys(x):
        if x is None or isinstance(x, (int, float)):
            return []
        if isinstance(x, list):
            r = []
            for y in x:
                r += Builder.keys(y)
            return r
        if isinstance(x, (tuple, str)):
            return [x]
        return [x.name]

    def _deps(self, ins, outs):
        deps = []
        for k in self.keys(ins):
            st = self.buf.get(k)
            if st and st["w"]:
                deps.append(st["w"])
        for k in self.keys(outs):
            st = self.buf.get(k)
            if st:
                if st["w"]:
                    deps.append(st["w"])
                deps.extend(st["r"].values())
        return deps

    def _wait(self, e, deps):
        seen = self.seen[e]
        need = {}
        for (sid, sem, val) in deps:
            if seen.get(sid, 0) >= val:
                continue
            if sid not in need or need[sid][1] < val:
                need[sid] = (sem, val)
        for sid, (sem, val) in need.items():
            self.eng[e].wait_ge(sem, val)
            seen[sid] = val

    def _record(self, tok, ins, outs):
        for k in self.keys(outs):
            self.buf[k] = {"w": tok, "r": {}}
        for k in self.keys(ins):
            st = self.buf.setdefault(k, {"w": None, "r": {}})
            st["r"][tok[0]] = tok

    def op(self, e, fn, outs, ins):
        self._wait(e, self._deps(ins, outs))
        inst = fn()
        inst.then_inc(self.esem[e], 1)
        self.ecnt[e] += 1
        tok = ("e_" + e + getattr(self, "sid_suffix", ""), self.esem[e], self.ecnt[e])
        self._record(tok, ins, outs)
        return tok

    def dma(self, q, out, in_, okey=None, ikey=None, **kw):
        ok = okey if okey is not None else out
        ik = ikey if ikey is not None else in_
        i = self.dnext
        self.dnext = (self.dnext + 1) % self.NDSEM
        deps = self._deps([ik], [ok])
        if self.dcum[i] > 0:
            deps.append(("d%d" % i, self.dsem[i], self.dcum[i]))
        self._wait(q, deps)
        self.eng[q].dma_start(out=out, in_=in_, **kw).then_inc(self.dsem[i], 16)
        self.dcum[i] += 16
        tok = ("d%d" % i, self.dsem[i], self.dcum[i])
        self._record(tok, [ik], [ok])
        return tok

    def rotate_sems(self):
        self.barrier()
        self.gen = getattr(self, "gen", 0) + 1
        for e in self.eng:
            self.esem[e] = self.es.enter_context(self.nc.semaphore("se%d_%s" % (self.gen, e)))
            self.ecnt[e] = 0
        self.sid_suffix = "_g%d" % self.gen

    def barrier(self):
        for e in self.eng:
            deps = [("e_" + e2 + getattr(self, "sid_suffix", ""), self.esem[e2], self.ecnt[e2]) for e2 in self.eng if e2 != e and self.ecnt[e2] > 0]
            deps += [("d%d" % i, self.dsem[i], self.dcum[i]) for i in range(self.NDSEM) if self.dcum[i] > 0]
            self._wait(e, deps)

    def mark(self, name):
        if not hasattr(self, "marks"):
            self.marks = []
        self.marks.append((name, getattr(self, "npe", 0)))

    def mm(self, out, pairs):
        self.npe = getattr(self, "npe", 0) + len(pairs)
        ins = []
        for l, r in pairs:
            ins += [l, r]
        n = len(pairs)

        def fn():
            inst = None
            for i, (l, r) in enumerate(pairs):
                inst = self.nc.tensor.matmul(out, lhsT=l, rhs=r, start=(i == 0), stop=(i == n - 1))
            return inst
        return self.op("pe", fn, [out], ins)

    def mm_multi(self, groups):
        self.npe = getattr(self, "npe", 0) + sum(len(p) for _, p in groups)
        ins, outs = [], []
        for out, pairs in groups:
            outs.append(out)
            for l, r in pairs:
                ins += [l, r]

        def fn():
            inst = None
            for out, pairs in groups:
                n = len(pairs)
                for i, (l, r) in enumerate(pairs):
                    inst = self.nc.tensor.matmul(out, lhsT=l, rhs=r, start=(i == 0), stop=(i == n - 1))
            return inst
        return self.op("pe", fn, outs, ins)

    def tr(self, out, in_, ident):
        self.npe = getattr(self, "npe", 0) + 1
        return self.op("pe", lambda: self.nc.tensor.transpose(out, in_, ident), [out], [in_, ident])

    def act(self, out, in_, func, bias=None, scale=None):
        kw = {}
        if bias is not None:
            kw["bias"] = bias
        if scale is not None:
            kw["scale"] = scale
        return self.op("act", lambda: self.nc.scalar.activation(out=out, in_=in_, func=func, **kw), [out], [in_, bias, scale])

    def tt(self, e, out, in0, in1, op):
        return self.op(e, lambda: self.eng[e].tensor_tensor(out=out, in0=in0, in1=in1, op=op), [out], [in0, in1])

    def ts(self, e, out, in0, s1, op0, s2=None, op1=None):
        if op1 is None:
            return self.op(e, lambda: self.eng[e].tensor_scalar(out=out, in0=in0, scalar1=s1, scalar2=None, op0=op0), [out], [in0, s1])
        return self.op(e, lambda: self.eng[e].tensor_scalar(out=out, in0=in0, scalar1=s1, scalar2=s2, op0=op0, op1=op1), [out], [in0, s1, s2])

    def stt(self, e, out, in0, scalar, in1, op0, op1):
        return self.op(e, lambda: self.eng[e].scalar_tensor_tensor(out=out, in0=in0, scalar=scalar, in1=in1, op0=op0, op1=op1), [out], [in0, scalar, in1])

    def copy(self, e, out, in_):
        if e == "act":
            return self.op(e, lambda: self.nc.scalar.copy(out=out, in_=in_), [out], [in_])
        return self.op(e, lambda: self.eng[e].tensor_copy(out=out, in_=in_), [out], [in_])

    def memset(self, e, out, val):
        return self.op(e, lambda: self.eng[e].memset(out, val), [out], [])

    def scan(self, out, d0, d1, init, e="dve"):
        return self.op(e, lambda: self.eng[e].tensor_tensor_scan(out=out, data0=d0, data1=d1, initial=init, op0=ALU.mult, op1=ALU.add), [out], [d0, d1, init])

    def recip(self, out, in_, e="dve"):
        return self.op(e, lambda: self.eng[e].reciprocal(out=out, in_=in_), [out], [in_])

    def reduce(self, out, in_, e="dve"):
        return self.op(e, lambda: self.eng[e].tensor_reduce(out=out, in_=in_, axis=AX.X, op=ALU.add), [out], [in_])

    def aselect(self, out, in_, pattern, cmp, fill, base, cm):
        return self.op("pool", lambda: self.nc.gpsimd.affine_select(out=out, in_=in_, pattern=pattern, compare_op=cmp, fill=fill, base=base, channel_multiplier=cm), [out], [in_])


def softplus_negabs(B, out, x, tmp1, tmp2):
    B.stt("dve", tmp1, x, -1.0, x, ALU.mult, ALU.max)
    B.act(tmp1, tmp1, AF.Exp, scale=-1.0)
    B.ts("dve", tmp2, tmp1, 2.0, ALU.add)
    B.recip(tmp2, tmp2)
    B.tt("dve", tmp1, tmp1, tmp2, ALU.mult)
    B.tt("dve", tmp2, tmp1, tmp1, ALU.mult)
    B.ts("dve", out, tmp2, 1.0 / 11.0, ALU.mult, 1.0 / 9.0, ALU.add)
    for cst in (1.0 / 7.0, 1.0 / 5.0, 1.0 / 3.0, 1.0):
        B.tt("dve", out, out, tmp2, ALU.mult)
        B.ts("dve", out, out, cst, ALU.add)
    B.tt("dve", out, out, tmp1, ALU.mult)
    B.ts("dve", out, out, 2.0, ALU.mult)


def build(depth=DEPTH, debug=False):
    B = Builder()
    nc = B.nc
    EI = "ExternalInput"
    xT_d = B.dram("xT", [8, 128, SEQ], F32, EI)
    cxT_d = B.dram("cxT", [8, 128, CTX], F32, EI)
    cols_d = B.dram("cols", [128, NCOLS], F32, EI)
    btm_d = B.dram("btm", [DEPTH, NTM], F32, EI)
    bada_d = B.dram("b_ada", [1, DEPTH * 6 * D], F32, EI)
    wada_d = B.dram("w_ada", [DEPTH, D, 6 * D], F32, EI)
    win_d = B.dram("w_in", [DEPTH, D, N_IN], F32, EI)
    wbr_d = B.dram("w_branch", [DEPTH, 3, 512, D], F32, EI)
    wout_d = B.dram("w_out", [DEPTH, D, D], F32, EI)
    wfi_d = B.dram("w_ffn_in", [DEPTH, D, 2 * FFN], F32, EI)
    wfo_d = B.dram("w_ffn_out", [DEPTH, FFN, D], F32, EI)
    lgw_d = B.dram("lru_gate_w", [DEPTH, 2, 2, 8, 64, 64], F32, EI)
    y_d = B.dram("yT", [8, 128, SEQ], F32, "ExternalOutput")
    dk = "ExternalOutput" if debug else "Internal"
    XS = [B.dram("XS0", [8, 128, SEQ], F32, dk), B.dram("XS1", [8, 128, SEQ], F32, dk)]
    HXD = B.dram("HXD", [8, 128, T], BF16, "Internal")
    PXF = B.dram("PXF", [NFM, 128, T], F32, dk)
    PXT = B.dram("PXT", [NT, 128, NTM], F32, dk)
    BR = B.dram("BR", [12, 128, T], BF16, "Internal")
    BRdbg = B.dram("BRdbg", [12, 128, T], F32, "ExternalOutput") if debug else None
    WGbf = B.dram("WGbf", [D, 3072], BF16, "Internal")
    WBbf = B.dram("WBbf", [1536, D], BF16, "Internal")
    WObf = B.dram("WObf", [D, D], BF16, "Internal")
    WFIbf = B.dram("WFIbf", [D, 2 * FFN], BF16, "Internal")
    WFObf = B.dram("WFObf", [FFN, D], BF16, "Internal")

    PS = [B.ps("P%d" % i, [128, 512], F32) for i in range(7)]
    PB = B.ps("PB", [128, 1024], BF16)
    psrot = [0]

    def P():
        psrot[0] = (psrot[0] + 1) % 7
        return PS[psrot[0]]

    CC = B.sb("CC", [128, NCOLS])
    IDF = B.sb("IDF", [128, 128])
    IDB = B.sb("IDB", [128, 128], BF16)
    ONESF = B.sb("ONESF", [128, 128])
    ONESB = B.sb("ONESB", [128, 128], BF16)
    TRIF = B.sb("TRIF", [128, 128])
    TRIB = B.sb("TRIB", [128, 128])
    MHF = B.sb("MHF", [128, 128])
    MHB = B.sb("MHB", [128, 128])
    MODC = B.sb("MODC", [128, DEPTH * 96])
    A1 = B.sb("A1", [128, DEPTH * 16])
    A2 = B.sb("A2", [128, DEPTH * 16])
    LB = B.sb("LB", [128, 32])
    OML = B.sb("OML", [128, 32])
    CST = B.sb("CST", [128, 32])
    CST2 = B.sb("CST2", [128, 32])
    BQ8 = B.sb("BQ8", [128, 8])
    CX = B.sb("CX", [128, 8, CTX])
    SMT = [B.sb("SMT%d" % i, [128, 32]) for i in range(4)]

    def col(name, i):
        o = COLS[name][0] + i
        return CC[:, o:o + 1]

    def modc(l, j, k, w):
        o = ((l * 48 + j * 8 + k) * 2 + w)
        return MODC[:, o:o + 1]

    def a1c(l, k, w):
        o = (l * 8 + k) * 2 + w
        return A1[:, o:o + 1]

    def a2c(l, k, w):
        o = (l * 8 + k) * 2 + w
        return A2[:, o:o + 1]

    B.dma("sp", CC[:], cols_d[:, :])
    B.dma("sp", CX[:], cxT_d.rearrange("k p t -> p k t"))
    B.memset("pool", ONESF[:], 1.0)
    B.memset("pool", ONESB[:], 1.0)
    B.aselect(IDF[:], ONESF[:], [[-1, 128]], ALU.is_equal, 0.0, 0, 1)
    B.copy("pool", IDB[:], IDF[:])
    B.aselect(TRIF[:], ONESF[:], [[1, 128]], ALU.is_ge, 0.0, 0, -1)
    B.aselect(TRIB[:], ONESF[:], [[-1, 128]], ALU.is_ge, 0.0, 0, 1)
    RM = B.sb("RM", [128, 4])
    BD = B.sb("BD", [128, 128])
    for q in range(4):
        B.aselect(RM[:, q:q + 1], ONESF[:, 0:1], [[0, 1]], ALU.is_ge, 0.0, -32 * q, 1)
        B.aselect(RM[:, q:q + 1], RM[:, q:q + 1], [[0, 1]], ALU.is_ge, 0.0, 32 * q + 31, -1)
        B.copy("pool", BD[:, q * 32:(q + 1) * 32], RM[:, q:q + 1].to_broadcast([128, 32]))
    MASK32 = B.sb("MASK32", [128, 32])
    B.memset("pool", MASK32[:], 1.0)
    B.memset("pool", MASK32[:, 0:1], 0.0)
    B.tt("pool", MHF[:], TRIF[:], BD[:], ALU.mult)
    B.tt("pool", MHB[:], TRIB[:], BD[:], ALU.mult)

    lo, _ = COLS["lbraw"]
    E_ = SMT[0]
    B.act(E_[:], CC[:, lo:lo + 32], AF.Exp)
    S_ = SMT[1]
    B.tt("dve", S_[:, 0:8], E_[:, 0:8], E_[:, 8:16], ALU.add)
    B.tt("dve", S_[:, 0:8], S_[:, 0:8], E_[:, 16:24], ALU.add)
    B.tt("dve", S_[:, 0:8], S_[:, 0:8], E_[:, 24:32], ALU.add)
    B.recip(S_[:, 0:8], S_[:, 0:8])
    for l in range(1, 4):
        B.tt("dve", E_[:, l * 8:(l + 1) * 8], E_[:, l * 8:(l + 1) * 8], S_[:, 0:8], ALU.mult)
    B.memset("dve", LB[:, 0:8], 0.0)
    B.copy("dve", LB[:, 8:16], E_[:, 8:16])
    B.tt("dve", LB[:, 16:24], LB[:, 8:16], E_[:, 16:24], ALU.add)
    B.tt("dve", LB[:, 24:32], LB[:, 16:24], E_[:, 24:32], ALU.add)
    B.ts("dve", OML[:], LB[:], -1.0, ALU.mult, 1.0, ALU.add)
    lo, _ = COLS["lam"]
    softplus_negabs(B, SMT[0][:], CC[:, lo:lo + 32], SMT[1][:], SMT[2][:])
    B.ts("dve", SMT[1][:], CC[:, lo:lo + 32], -1.0, ALU.mult, 0.0, ALU.max)
    B.tt("dve", SMT[0][:], SMT[0][:], SMT[1][:], ALU.add)
    B.ts("dve", CST[:], SMT[0][:], -8.0, ALU.mult)
    B.ts("dve", CST2[:], SMT[0][:], -16.0, ALU.mult)

    B.push()
    S2 = B.sb("S2", [128, 8, 2], BF16)
    BADA = B.sb("BADA", [2, DEPTH * 6 * D])
    MODR = B.sb("MODR", [2, 6 * D])
    WA = [B.sb("WA%d" % i, [128, 8, 512], BF16) for i in range(2)]
    lo, _ = COLS["c"]
    B.act(S2[:, :, 0], CC[:, lo:lo + 8], AF.Silu)
    lo, _ = COLS["cctx"]
    B.act(S2[:, :, 1], CC[:, lo:lo + 8], AF.Silu)
    B.dma("sp", BADA[:], bada_d[0].partition_broadcast(2))
    for l in range(depth):
        wv = wada_d[l].rearrange("(k p) f -> p k f", p=128)
        for fc in range(12):
            w = WA[fc % 2]
            B.dma("pool", w[:], wv[:, :, fc * 512:(fc + 1) * 512])
            p = P()
            B.mm(p[0:2, :], [(S2[:, k, :], w[:, k, :]) for k in range(8)])
            B.tt("dve", MODR[:, fc * 512:(fc + 1) * 512], p[0:2, :], BADA[:, l * 6144 + fc * 512: l * 6144 + (fc + 1) * 512], ALU.add)
        p = P()
        for f in range(48):
            B.tr(p[:, f * 2:(f + 1) * 2], MODR[0:2, f * 128:(f + 1) * 128], IDF[0:2, 0:2])
        B.copy("dve", MODC[:, l * 96:(l + 1) * 96], p[:, 0:96])
        lo1, _ = COLS["ln1"]
        lo2, _ = COLS["ln2"]
        for (AT_, jv, lo_) in ((A1, 1, lo1), (A2, 4, lo2)):
            src = MODC[:, l * 96 + jv * 16: l * 96 + (jv + 1) * 16].rearrange("p (k w) -> p k w", w=2)
            dst = AT_[:, l * 16:(l + 1) * 16].rearrange("p (k w) -> p k w", w=2)
            lnb = CC[:, lo_ + l * 8: lo_ + (l + 1) * 8].unsqueeze(2).to_broadcast([128, 8, 2])
            B.stt("dve", dst, src, 1.0, lnb, ALU.add, ALU.mult)
    B.pop()

    blocks = [(0, CTX)] + [(CTX + i * 512, 512) for i in range(8)]
    cur = None
    for l in range(depth):
        B.rotate_sems()
        last = (l == DEPTH - 1)
        perm = l >= 1
        if l == 0:
            xs_old, xs_new = xT_d, XS[0]
        else:
            xs_old, xs_new = XS[(l - 1) % 2], XS[l % 2]

        B.mark("L%d_p1" % l)
        B.push()
        HX = [B.sb("HX%d" % k, [128, T], BF16) for k in range(8)]
        B.push()
        XT = [B.sb("XT%d" % i, [128, SEQ]) for i in range(2)]
        ACC = B.sb("ACC", [128, SEQ])
        RSTD = B.sb("RSTD", [128, SEQ])
        XP = B.sb("XP", [128, SEQ])
        CACC = B.sb("CACC", [128, CTX])
        CRS = B.sb("CRS", [128, CTX])
        CT = B.sb("CT", [128, CTX])
        for k in range(8):
            xt = XT[k % 2]
            B.dma("sp", xt[:], xs_old[k], ikey=("XS", id(xs_old), k))
            if k == 0:
                B.act(ACC[:], xt[:], AF.Square)
                B.act(CACC[:], CX[:, k, :], AF.Square)
            else:
                B.act(XP[:], xt[:], AF.Square)
                B.tt("pool", ACC[:], ACC[:], XP[:], ALU.add)
                B.act(CT[:], CX[:, k, :], AF.Square)
                B.tt("dve", CACC[:], CACC[:], CT[:], ALU.add)
        for b8 in range(8):
            p = P()
            B.mm(p[:, :], [(ONESF[:], ACC[:, b8 * 512:(b8 + 1) * 512])])
            B.act(RSTD[:, b8 * 512:(b8 + 1) * 512], p[:, :], AF.Ln, scale=1.0 / D, bias=EPS)
        B.act(RSTD[:], RSTD[:], AF.Exp, scale=-0.5)
        p = P()
        B.mm(p[:, 0:CTX], [(ONESF[:], CACC[:])])
        B.act(CRS[:], p[:, 0:CTX], AF.Ln, scale=1.0 / D, bias=EPS)
        B.act(CRS[:], CRS[:], AF.Exp, scale=-0.5)
        for k in range(8):
            xt = XT[k % 2]
            B.dma("sp", xt[:], xs_old[k], ikey=("XS", id(xs_old), k))
            B.tt("dve", ACC[:], xt[:], RSTD[:], ALU.mult)
            hxo = HX[k][:, CTX:T]
            if perm:
                hxo = hxo.rearrange("p (a b) -> p b a", a=64, b=64)
                src = ACC[:].rearrange("p (a b) -> p a b", a=64)
            else:
                src = ACC[:]
            B.act(hxo, src, AF.Identity, scale=a1c(l, k, 0), bias=modc(l, 0, k, 0))
            if perm:
                B.copy("pool", XP[:].rearrange("p (a b) -> p b a", a=64, b=64), xt[:].rearrange("p (a b) -> p a b", a=64))
                B.dma("sp", xs_new[k], XP[:], okey=("XS", id(xs_new), k))
            else:
                B.dma("sp", xs_new[k], xt[:], okey=("XS", id(xs_new), k))
            B.tt("dve", CT[:], CX[:, k, :], CRS[:], ALU.mult)
            B.act(HX[k][:, 0:CTX], CT[:], AF.Identity, scale=a1c(l, k, 1), bias=modc(l, 0, k, 1))
            B.dma("sp", HXD[k], HX[k][:], okey=("HXD", k))
        B.pop()

        B.mark("L%d_p1c" % l)
        B.push()
        BT = B.sb("BT", [128, NTM])
        WT = B.sb("WT", [128, 8, NTM], BF16)
        WF = [B.sb("WF%d" % i, [128, 8, 512], BF16) for i in range(2)]
        STG = [B.sb("STG%d" % i, [128, T]) for i in range(2)]
        STT = [B.sb("STT%d" % i, [128, NTM]) for i in range(2)]
        wv = win_d[l].rearrange("(k p) f -> p k f", p=128)
        B.dma("sp", BT[:], btm_d[l].partition_broadcast(128))
        B.dma("pool", WT[:, :, 0:512], wv[:, :, 512:1024])
        B.dma("pool", WT[:, :, 512:1024], wv[:, :, 3072:3584])
        B.dma("pool", WT[:, :, 1024:1280], wv[:, :, 2816:3072])
        B.dma("pool", WT[:, :, 1280:1296], wv[:, :, 4096:4112])
        lo, _ = COLS["bfm"]
        B.ts("dve", BQ8[:, 0:2], CC[:, lo + l * 32 + 16: lo + l * 32 + 18], 0.125, ALU.mult)
        for g in range(8):
            w = WF[g % 2]
            B.dma("pool", w[:], wv[:, :, FM_GROUP_COL[g]:FM_GROUP_COL[g] + 512])
            for q in range(4):
                ft = g * 4 + q
                stg = STG[ft % 2]
                bias = col("bfm", l * 32 + ft)
                scale = None
                if ft < 8:
                    func = AF.Silu
                elif 20 <= ft < 24:
                    func = AF.Sigmoid
                else:
                    func = AF.Identity
                if ft in (16, 17):
                    scale = 0.125
                    bias = BQ8[:, ft - 16:ft - 15]
                for (c0, n) in blocks:
                    p = P()
                    B.mm(p[:, 0:n], [(w[:, k, q * 128:(q + 1) * 128], HX[k][:, c0:c0 + n]) for k in range(8)])
                    B.act(stg[:, c0:c0 + n], p[:, 0:n], func, bias=bias, scale=scale)
                B.dma("sp", PXF[ft], stg[:], okey=("PXF", ft))
        for i in range(NT):
            sl = slice(i * 128, (i + 1) * 128)
            pa, pb, pc = P(), P(), P()
            B.mm_multi([
                (pa[:, :], [(HX[k][:, sl], WT[:, k, 0:512]) for k in range(8)]),
                (pb[:, :], [(HX[k][:, sl], WT[:, k, 512:1024]) for k in range(8)]),
                (pc[:, 0:272], [(HX[k][:, sl], WT[:, k, 1024:1296]) for k in range(8)]),
            ])
            st = STT[i % 2]
            B.tt("dve", st[:, 0:512], pa[:, :], BT[:, 0:512], ALU.add)
            B.tt("dve", st[:, 512:1024], pb[:, :], BT[:, 512:1024], ALU.add)
            B.tt("dve", st[:, 1024:1296], pc[:, 0:272], BT[:, 1024:1296], ALU.add)
            B.dma("sp", PXT[i], st[:], okey=("PXT", i))
        B.pop()
        B.pop()

        for r4 in range(4):
            rs = slice(r4 * 256, (r4 + 1) * 256)
            B.dma("pool", WGbf[rs, :], win_d[l][rs, GATE_COL:GATE_COL + 3072], okey=("WGbf", r4))
            B.dma("pool", WObf[rs, :], wout_d[l][rs, :], okey=("WObf", r4))
            B.dma("pool", WFIbf[rs, :], wfi_d[l][rs, :], okey=("WFIbf", r4))
        wb2 = wbr_d[l].rearrange("n r f -> (n r) f")
        for r4 in range(3):
            B.dma("pool", WBbf[r4 * 512:(r4 + 1) * 512, :], wb2[r4 * 512:(r4 + 1) * 512, :], okey=("WBbf", r4))
        for r4 in range(4):
            rs = slice(r4 * 704, (r4 + 1) * 704)
            B.dma("pool", WFObf[rs, :], wfo_d[l][rs, :], okey=("WFObf", r4))
        PXT_ALL = [("PXT", i) for i in range(NT)]
        pxt_v = PXT.rearrange("i p c -> p i c")

        B.mark("L%d_lru" % l)
        HT = T // 2
        HB = [(0, HT), (HT, T)]
        SEGS = ((0, CTX), (CTX, T))
        B.push()
        LXs = [B.sb("LX%d" % i, [128, T]) for i in range(2)]
        GWs = [B.sb("GW%d" % i, [128, 4, 128]) for i in range(2)]
        LYs = [[B.sb("LY%d_%d" % (r, i), [128, HT]) for i in range(2)] for r in range(2)]
        XC = [B.sb("XC%d" % i, [128, HT]) for i in range(2)]
        HL = [B.sb("HL%d" % i, [128, HT]) for i in range(2)]
        RR = [B.sb("RR%d" % i, [128, HT]) for i in range(2)]
        II = [B.sb("II%d" % i, [128, HT]) for i in range(2)]
        AA = [B.sb("AA%d" % i, [128, HT]) for i in range(2)]
        OB = [B.sb("OB%d" % i, [128, HT], BF16) for i in range(2)]
        for r in range(2):
            B.memset("pool", GWs[r][:], 0.0)

        def load_lru(j):
            B.dma("sp", LXs[j % 2][:], PXF[24 + j], ikey=("PXF", 24 + j))
            for hs, (r0, r1) in enumerate(HB):
                B.dma("sp", LYs[j % 2][hs][:], PXF[28 + j][:, r0:r1], ikey=("PXF", 28 + j))
            for z in range(2):
                for g in range(2):
                    for h2 in range(2):
                        B.dma("sp", GWs[j % 2][h2 * 64:(h2 + 1) * 64, z * 2 + g, h2 * 64:(h2 + 1) * 64], lgw_d[l, z, g, 2 * j + h2])

        load_lru(0)
        for j in range(4):
            if j + 1 < 4:
                load_lru(j + 1)
            LX, GW, LY = LXs[j % 2], GWs[j % 2], LYs[j % 2]
            cw = lambda tap: col("convw", l * 16 + tap * 4 + j)
            for hs, (r0, r1) in enumerate(HB):
                B.ts("dve", XC[hs][:], LX[:, r0:r1], cw(2), ALU.mult, col("convb", l * 4 + j), ALU.add)
            for tap, o in ((0, -2), (1, -1), (3, 1)):
                for hs, (r0, r1) in enumerate(HB):
                    for (s0, s1) in SEGS:
                        d0 = max(r0, s0 + max(0, -o))
                        d1 = min(r1, s1 - max(0, o))
                        if d1 > d0:
                            B.stt("dve", XC[hs][:, d0 - r0:d1 - r0], LX[:, d0 + o:d1 + o], cw(tap), XC[hs][:, d0 - r0:d1 - r0], ALU.mult, ALU.add)
            for z in range(2):
                for hs, (r0, r1) in enumerate(HB):
                    c = 0
                    while c < HT:
                        n = min(512, HT - c)
                        p = P()
                        B.mm(p[:, 0:n], [(GW[:, z * 2 + 0, :], XC[hs][:, c:c + n])])
                        B.act(RR[hs][:, c:c + n], p[:, 0:n], AF.Sigmoid, bias=col("lgb", l * 16 + (z * 2 + 0) * 4 + j))
                        p = P()
                        B.mm(p[:, 0:n], [(GW[:, z * 2 + 1, :], XC[hs][:, c:c + n])])
                        B.act(II[hs][:, c:c + n], p[:, 0:n], AF.Sigmoid, bias=col("lgb", l * 16 + (z * 2 + 1) * 4 + j))
                        c += n
                ci = l * 8 + z * 4 + j
                for hs in range(2):
                    B.act(AA[hs][:], RR[hs][:], AF.Exp, scale=CST[:, ci:ci + 1])
                for hs in range(2):
                    B.act(RR[hs][:], RR[hs][:], AF.Exp, scale=CST2[:, ci:ci + 1])
                for hs in range(2):
                    B.act(RR[hs][:], RR[hs][:], AF.Sqrt, scale=-1.0, bias=1.0)
                for hs in range(2):
                    B.tt("dve", II[hs][:], II[hs][:], RR[hs][:], ALU.mult)
                for hs in range(2):
                    B.tt("dve" if hs == 0 else "pool", II[hs][:], II[hs][:], XC[hs][:], ALU.mult)
                if z == 0:
                    B.scan(RR[0][:, 0:CTX], AA[0][:, 0:CTX], II[0][:, 0:CTX], 0.0)
                    B.scan(RR[0][:, CTX:HT], AA[0][:, CTX:HT], II[0][:, CTX:HT], RR[0][:, CTX - 1:CTX])
                    B.scan(RR[1][:], AA[1][:], II[1][:], RR[0][:, HT - 1:HT])
                    for hs in range(2):
                        B.copy("act", HL[hs][:], RR[hs][:])
                else:
                    B.scan(RR[0][:, 0:CTX][:, ::-1], AA[0][:, 0:CTX][:, ::-1], II[0][:, 0:CTX][:, ::-1], 0.0)
                    B.scan(RR[1][:, ::-1], AA[1][:, ::-1], II[1][:, ::-1], RR[0][:, 0:1])
                    B.scan(RR[0][:, CTX:HT][:, ::-1], AA[0][:, CTX:HT][:, ::-1], II[0][:, CTX:HT][:, ::-1], RR[1][:, 0:1])
                    for hs in range(2):
                        B.tt("dve", HL[hs][:], HL[hs][:], RR[hs][:], ALU.add)
            for hs in range(2):
                B.act(AA[hs][:], LY[hs][:], AF.Square)
            for hs in range(2):
                B.ts("dve", AA[hs][:], AA[hs][:], 0.044715 * 1.5957691216, ALU.mult, 1.5957691216, ALU.add)
                B.tt("dve", AA[hs][:], AA[hs][:], LY[hs][:], ALU.mult)
            for hs in range(2):
                B.act(AA[hs][:], AA[hs][:], AF.Sigmoid)
            for hs, (r0, r1) in enumerate(HB):
                B.tt("dve" if hs == 0 else "pool", AA[hs][:], AA[hs][:], LY[hs][:], ALU.mult)
                B.tt("dve", OB[hs][:], AA[hs][:], HL[hs][:], ALU.mult)
                B.dma("sp", BR[8 + j][:, r0:r1], OB[hs][:], okey=("BR", 8 + j))
                if debug:
                    B.tt("dve", II[hs][:], AA[hs][:], HL[hs][:], ALU.mult)
                    B.dma("sp", BRdbg[8 + j][:, r0:r1], II[hs][:])
        B.pop()

        B.mark("L%d_mlstm" % l)
        B.push()
        GT = B.sb("GT", [128, NT, 16])
        LF = B.sb("LF", [128, NT, 8])
        BB = B.sb("BBm", [128, NT, 8])
        BTOT = B.sb("BTOT", [128, NT, 8])
        WP = B.sb("WPm", [128, NT, 8])
        WS = B.sb("WSm", [128, NT, 8])
        EN = B.sb("ENm", [128, NT, 8])
        EBT = B.sb("EBT", [128, NT, 8])
        TM1 = B.sb("TM1", [128, NT, 8])
        TM2 = B.sb("TM2", [128, NT, 8])
        B.dma("sp", GT[:], pxt_v[:, :, 1280:1296], ikey=PXT_ALL)
        softplus_negabs(B, LF[:], GT[:, :, 8:16], TM1[:], TM2[:])
        B.ts("dve", TM1[:], GT[:, :, 8:16], 0.0, ALU.min)
        B.tt("dve", LF[:], TM1[:], LF[:], ALU.subtract)
        p = P()
        pv = p[:, 0:NT * 8].rearrange("p (i c) -> p i c", c=8)
        for i in range(NT):
            B.mm(pv[:, i, 0:4], [(TRIF[:], LF[:, i, 0:4])])
            B.mm(pv[:, i, 4:8], [(TRIB[:], LF[:, i, 4:8])])
        B.copy("dve", BB[:], pv)
        p = P()
        B.mm(p[:, 0:NT * 8], [(ONESF[:], LF[:].rearrange("p i c -> p (i c)"))])
        B.copy("dve", BTOT[:], p[:, 0:NT * 8].rearrange("p (i c) -> p i c", c=8))
        B.tt("dve", TM1[:], GT[:, :, 0:8], BB[:], ALU.subtract)
        B.act(WP[:], TM1[:], AF.Exp)
        B.tt("dve", TM1[:], TM1[:], BTOT[:], ALU.add)
        B.act(WS[:], TM1[:], AF.Exp)
        B.act(EN[:], BB[:], AF.Exp, scale=-1.0)
        B.act(EBT[:], BTOT[:], AF.Exp)
        ord_f = list(range(NT))
        ord_b = [1, 0] + list(range(NT - 1, 1, -1))
        HH = B.sb("HH", [128, NT, 132])
        TMPH = B.sb("TMPH", [128, NT, 132])
        KWA = [B.sb("KWA%d" % z, [128, NT, 64], BF16) for z in range(2)]
        DEN = [B.sb("DENm%d" % z, [128, NT]) for z in range(2)]
        OBm = B.sb("OBm", [128, T], BF16)
        CS = [[B.sb("CSm%d_%d" % (z, r), [64, 132]) for r in range(4)] for z in range(2)]
        CB = [[B.sb("CBm%d_%d" % (z, r), [64, 132], BF16) for r in range(4)] for z in range(2)]
        PT = [[B.sb("PTm%d_%d" % (z, r), [128, 128], BF16) for r in range(3)] for z in range(2)]
        KW = [[B.sb("KWm%d_%d" % (z, r), [128, 64], BF16) for r in range(3)] for z in range(2)]
        RC = [[B.sb("RCm%d_%d" % (z, r), [128, 2]) for r in range(3)] for z in range(2)]
        SS = B.sb("SSm", [128, NT])
        MIN_ = [dict(QT=B.sb("QTm%d" % r, [64, T], BF16), KT=B.sb("KTm%d" % r, [64, T], BF16),
                     KTK=B.sb("KTK%d" % r, [128, NT, 64]), VA=B.sb("VA%d" % r, [128, NT, 132], BF16),
                     SOG=B.sb("SOG%d" % r, [128, T])) for r in range(2)]
        for r in range(2):
            B.memset("pool", MIN_[r]["VA"][:, :, 128:132], 1.0)

        def load_head(h):
            m = MIN_[h % 2]
            r0 = (h % 2) * 64
            B.dma("pool", m["QT"][:], PXF[16 + h // 2][r0:r0 + 64, :], ikey=("PXF", 16 + h // 2))
            B.dma("pool", m["KT"][:], PXF[18 + h // 2][r0:r0 + 64, :], ikey=("PXF", 18 + h // 2))
            B.dma("sp", m["KTK"][:], pxt_v[:, :, 1024 + h * 64:1024 + (h + 1) * 64], ikey=PXT_ALL)
            B.dma("pool", m["VA"][:, :, 0:128], pxt_v[:, :, 512 + h * 128:512 + (h + 1) * 128], ikey=PXT_ALL)
            B.dma("sp", m["SOG"][:], PXF[20 + h], ikey=("PXF", 20 + h))

        load_head(0)
        for h in range(4):
            if h + 1 < 4:
                load_head(h + 1)
            m = MIN_[h % 2]
            QT, KT, KTK, VA, SOG = m["QT"], m["KT"], m["KTK"], m["VA"], m["SOG"]
            for z in range(2):
                B.memset("pool", CS[z][0][:], 0.0)
                B.memset("pool", CB[z][0][:], 0.0)
            def emit_pt(step, z):
                i = (ord_f if z == 0 else ord_b)[step]
                sl = slice(i * 128, (i + 1) * 128)
                zh = z * 4 + h
                p = P()
                B.mm(p[:, 0:128], [(KT[:, sl], QT[:, sl])])
                B.stt("dve", PT[z][step % 3][:], p[:, 0:128], WP[:, i, zh:zh + 1], (TRIF if z == 0 else TRIB)[:], ALU.mult, ALU.mult)

            for z in range(2):
                zh = z * 4 + h
                B.tt("dve", KWA[z][:], KTK[:], WS[:, :, zh].unsqueeze(2).to_broadcast([128, NT, 64]), ALU.mult)
                emit_pt(0, z)
            for step in range(NT):
                for z in range(2):
                    i = (ord_f if z == 0 else ord_b)[step]
                    sl = slice(i * 128, (i + 1) * 128)
                    zh = z * 4 + h
                    pd = P()
                    B.mm(pd[0:64, 0:132], [(KWA[z][:, i, :], VA[:, i, :])])
                    B.stt("dve", CS[z][(step + 1) % 4][:], CS[z][step % 4][:], EBT[0:64, i, zh:zh + 1], pd[0:64, 0:132], ALU.mult, ALU.add)
                    B.copy("act", CB[z][(step + 1) % 4][:], CS[z][(step + 1) % 4][:])
                    po = P()
                    B.mm(po[:, 0:132], [(PT[z][step % 3][:], VA[:, i, :]), (QT[:, sl], CB[z][step % 4][:])])
                    B.copy("act", (HH if z == 0 else TMPH)[:, i, :], po[:, 0:132])
                    if step + 1 < NT:
                        emit_pt(step + 1, z)
            for z in range(2):
                zh = z * 4 + h
                Hz = HH if z == 0 else TMPH
                B.stt("dve", DEN[z][:], Hz[:, :, 128], -1.0, Hz[:, :, 128], ALU.mult, ALU.max)
                B.tt("dve", DEN[z][:], DEN[z][:], EN[:, :, zh], ALU.max)
                B.act(DEN[z][:], DEN[z][:], AF.Ln)
                B.act(DEN[z][:], DEN[z][:], AF.Exp, scale=-1.0)
                B.tt("dve", Hz[:, :, 0:128], Hz[:, :, 0:128], DEN[z][:].unsqueeze(2).to_broadcast([128, NT, 128]), ALU.mult)
            B.tt("dve", HH[:, :, 0:128], HH[:, :, 0:128], TMPH[:, :, 0:128], ALU.add)
            B.tt("pool", TMPH[:, :, 0:128], HH[:, :, 0:128], HH[:, :, 0:128], ALU.mult)
            B.reduce(SS[:], TMPH[:, :, 0:128])
            B.act(SS[:], SS[:], AF.Ln, scale=1.0 / 128, bias=EPS)
            B.act(SS[:], SS[:], AF.Exp, scale=-0.5)
            B.tt("dve", HH[:, :, 0:128], HH[:, :, 0:128], SS[:].unsqueeze(2).to_broadcast([128, NT, 128]), ALU.mult)
            for i4 in range(0, NT, 4):
                p = P()
                nn = min(4, NT - i4)
                for u in range(nn):
                    B.tr(p[:, u * 128:(u + 1) * 128], HH[:, i4 + u, 0:128], IDF[:])
                B.stt("dve", OBm[:, i4 * 128:(i4 + nn) * 128], p[:, 0:nn * 128], col("mln", l * 4 + h), SOG[:, i4 * 128:(i4 + nn) * 128], ALU.mult, ALU.mult)
            B.dma("sp", BR[4 + h], OBm[:], okey=("BR", 4 + h))
            if debug:
                B.copy("dve", TMPH[:].rearrange("p i c -> p (i c)")[:, 0:T], OBm[:])
                B.dma("sp", BRdbg[4 + h], TMPH[:].rearrange("p i c -> p (i c)")[:, 0:T])
        B.pop()

        B.mark("L%d_hgrn" % l)
        NCH = T // 32
        B.push()
        RMASK = B.sb("RMASK", [128, T])
        B.memset("pool", RMASK[:], 1.0)
        B.memset("pool", RMASK[:].rearrange("p (c t) -> p c t", t=32)[:, :, 0], 0.0)
        for h in range(4):
            B.push()
            QS = B.sb("QS", [128, T])
            VT = B.sb("VTh", [128, NT, 128], BF16)
            OA = B.sb("OA", [128, T])
            FFh = [B.sb("FF%d" % i, [128, T // 2]) for i in range(2)]
            KKh = [B.sb("KK%d" % i, [128, T // 2]) for i in range(2)]
            EEh = [B.sb("EE%d" % i, [128, T // 2]) for i in range(2)]
            QT2 = B.sb("QT2", [128, T], BF16)
            KT2 = B.sb("KT2", [128, T], BF16)
            KTO = B.sb("KTO", [128, NT, 128], BF16)
            BMID = B.sb("BMID", [128, NCH])
            BLS = B.sb("BLS", [128, NCH])
            EM = B.sb("EMh", [128, NCH])
            EL2 = B.sb("EL2", [128, NCH])
            ELM = B.sb("ELM", [128, NCH])
            RING = 6
            SR = [B.sb("SR%d" % r, [128, 128]) for r in range(RING)]
            PDS = [None, None, None]
            SPR = [B.sb("SPR%d" % r, [128, 128], BF16) for r in range(RING)]
            ATR = [B.sb("ATR%d" % r, [128, 128], BF16) for r in range(3)]
            cidx = 0
            B.dma("sp", QS[:], PXF[h], ikey=("PXF", h))
            B.dma("pool", VT[:], pxt_v[:, :, h * 128:(h + 1) * 128], ikey=PXT_ALL)
            VTM = [B.sb("VTM%d" % q, [128, NT, 128], BF16) for q in range(4)]
            for q in range(4):
                B.ts("dve", VTM[q][:], VT[:], RM[:, q:q + 1], ALU.mult)
            for z in range(2):
                zh = l * 8 + z * 4 + h
                mid = 15 if z == 0 else 16
                lastp = 31 if z == 0 else 0
                HT = T // 2
                NCH2 = NCH // 2

                def steps(hs):
                    r = slice(hs * HT, (hs + 1) * HT)
                    cr = slice(hs * NCH2, (hs + 1) * NCH2)
                    F_, K_, E_ = FFh[hs], KKh[hs], EEh[hs]
                    ev = E_[:].rearrange("p (c t) -> p c t", t=32)
                    fv = F_[:].rearrange("p (c t) -> p c t", t=32)
                    KT3 = E_[:].bitcast(BF16)[:, 0:HT]
                    yield lambda: B.dma("sp", F_[:], PXF[8 + z * 4 + h][:, r], ikey=("PXF", 8 + z * 4 + h))
                    yield lambda: B.act(F_[:], F_[:], AF.Sigmoid)
                    yield lambda: B.ts("dve", F_[:], F_[:], OML[:, zh:zh + 1], ALU.mult, LB[:, zh:zh + 1], ALU.add)
                    yield lambda: B.act(K_[:], F_[:], AF.Identity, scale=-1.0, bias=1.0)
                    yield lambda: B.act(F_[:], F_[:], AF.Ln)
                    yield lambda: B.scan(E_[:], RMASK[:, r], F_[:], 0.0)
                    if z == 1:
                        yield lambda: B.copy("dve", BLS[:, cr], ev[:, :, 31])
                        yield lambda: B.tt("dve", ev, BLS[:, cr].unsqueeze(2).to_broadcast([128, NCH2, 32]), ev, ALU.subtract)
                        yield lambda: B.tt("dve", E_[:], E_[:], F_[:], ALU.add)
                    yield lambda: B.copy("pool", BMID[:, cr], ev[:, :, mid])
                    yield lambda: B.copy("pool", BLS[:, cr], ev[:, :, lastp])
                    yield lambda: B.act(EM[:, cr], BMID[:, cr], AF.Exp)
                    yield lambda: B.act(EL2[:, cr], BLS[:, cr], AF.Exp)
                    yield lambda: B.tt("pool", BLS[:, cr], BLS[:, cr], BMID[:, cr], ALU.subtract)
                    yield lambda: B.act(ELM[:, cr], BLS[:, cr], AF.Exp)
                    yield lambda: B.tt("dve", ev, ev, BMID[:, cr].unsqueeze(2).to_broadcast([128, NCH2, 32]), ALU.subtract)
                    yield lambda: B.act(F_[:], E_[:], AF.Exp)
                    yield lambda: B.tt("dve", QT2[:, r], QS[:, r], F_[:], ALU.mult)
                    yield lambda: B.act(F_[:], E_[:], AF.Exp, scale=-1.0)
                    yield lambda: B.tt("dve", KT2[:, r], K_[:], F_[:], ALU.mult)
                    yield lambda: B.tt("dve", fv, fv, ELM[:, cr].unsqueeze(2).to_broadcast([128, NCH2, 32]), ALU.mult)
                    yield lambda: B.tt("dve", KT3, K_[:], F_[:], ALU.mult)
                    nth = NT // 2
                    for i4 in range(0, nth, 4):
                        nn = min(4, nth - i4)

                        def trs(i4=i4, nn=nn):
                            for u in range(nn):
                                B.tr(PB[:, u * 128:(u + 1) * 128], KT3[:, (i4 + u) * 128:(i4 + u + 1) * 128], IDB[:])
                            B.copy("act", KTO[:, hs * nth + i4:hs * nth + i4 + nn, :], PB[:, 0:nn * 128].rearrange("p (u k) -> p u k", k=128))
                        yield trs

                gens = [steps(0), steps(1)]
                live = [True, True]
                while any(live):
                    for hs in range(2):
                        if live[hs]:
                            try:
                                next(gens[hs])()
                            except StopIteration:
                                live[hs] = False
                B.memset("pool", SR[0][:], 0.0)
                order = ord_f if z == 0 else ord_b
                cidx = 0

                def emit_pre(ti):
                    i = order[ti]
                    sl = slice(i * 128, (i + 1) * 128)
                    pa = P()
                    B.mm(pa[:, 0:128], [(KT2[:, sl], QT2[:, sl])])
                    B.tt("dve", ATR[ti % 3][:], pa[:, 0:128], (MHF if z == 0 else MHB)[:], ALU.mult)
                    pd = P()
                    B.mm_multi([(pd[:, hf * 128:(hf + 1) * 128], [(KTO[:, i, :], VTM[hf][:, i, :])]) for hf in range(4)])
                    PDS[ti % 3] = pd

                emit_pre(0)
                for ti, i in enumerate(order):
                    sl = slice(i * 128, (i + 1) * 128)
                    if ti + 1 < NT:
                        emit_pre(ti + 1)
                    ATb = ATR[ti % 3]
                    po = P()
                    halves = (0, 1, 2, 3) if z == 0 else (3, 2, 1, 0)
                    for hf in halves:
                        c = 4 * i + hf
                        lo = hf * 32
                        scur, snext = SR[cidx % RING], SR[(cidx + 1) % RING]
                        spb = SPR[cidx % RING]
                        B.stt("dve", snext[:], scur[:], EL2[:, c:c + 1], PDS[ti % 3][:, hf * 128:(hf + 1) * 128], ALU.mult, ALU.add)
                        B.act(spb[:], scur[:], AF.Identity, scale=EM[:, c:c + 1])
                        B.mm(po[:, lo:lo + 32], [(VT[:, i, :], ATb[:, lo:lo + 32]), (spb[:], QT2[:, c * 32:(c + 1) * 32])])
                        cidx += 1
                    if z == 0:
                        B.copy("act", OA[:, sl], po[:, 0:128])
                    else:
                        B.tt("dve", OA[:, sl], OA[:, sl], po[:, 0:128], ALU.add)
            HT = T // 2
            for hs in range(2):
                r0 = hs * HT
                B.act(EEh[hs][:], OA[:, r0:r0 + HT], AF.Square)
                B.dma("sp", KKh[hs][:], PXF[4 + h][:, r0:r0 + HT], ikey=("PXF", 4 + h))
            for hs in range(2):
                r0 = hs * HT
                c = 0
                while c < HT:
                    n = min(512, HT - c)
                    p = P()
                    B.mm(p[:, 0:n], [(ONESF[:], EEh[hs][:, c:c + n])])
                    B.act(FFh[hs][:, c:c + n], p[:, 0:n], AF.Ln, scale=1.0 / 128, bias=EPS)
                    c += n
                B.act(FFh[hs][:], FFh[hs][:], AF.Exp, scale=-0.5)
            OBh = QT2
            for hs in range(2):
                r0 = hs * HT
                B.tt("dve", OA[:, r0:r0 + HT], OA[:, r0:r0 + HT], FFh[hs][:], ALU.mult)
                B.stt("dve", OBh[:, r0:r0 + HT], OA[:, r0:r0 + HT], col("hgn", l * 4 + h), KKh[hs][:], ALU.mult, ALU.mult)
            B.dma("sp", BR[h], OBh[:], okey=("BR", h))
            if debug:
                B.copy("dve", OA[:], OBh[:])
                B.dma("sp", BRdbg[h], OA[:])
            B.pop()
        B.pop()

        B.mark("L%d_p3" % l)
        xsv = xs_new.rearrange("k p t -> p k t")
        hxv = HXD.rearrange("k p t -> p k t")
        brv = BR.rearrange("k p t -> p k t")
        yv = y_d.rearrange("k p t -> p k t")
        XSK = [("XS", id(xs_new), k) for k in range(8)]
        wv = win_d[l].rearrange("(k p) f -> p k f", p=128)
        B.push()
        WGr = B.sb("WGr", [128, 8, 3072], BF16)
        WBr = B.sb("WBr", [128, 12, 1024], BF16)
        WOr = B.sb("WOr", [128, 8, 1024], BF16)
        XBs = [B.sb("XB%d" % i, [128, 8, 512]) for i in range(2)]
        HXBs = [B.sb("HXB%d" % i, [128, 8, 512], BF16) for i in range(2)]
        BRBs = [B.sb("BRB%d" % i, [128, 12, 512], BF16) for i in range(2)]
        MG = B.sb("MG", [128, 8, 512], BF16)
        GS = [B.sb("GS%d" % i, [128, 512]) for i in range(3)]
        TMP3 = [B.sb("TMP3%d" % i, [128, 512]) for i in range(2)]
        ACC3 = B.sb("ACC3", [128, 512])
        wgv = WGbf.rearrange("(k p) f -> p k f", p=128)
        wbv = WBbf.rearrange("(q p) f -> p q f", p=128)
        wov = WObf.rearrange("(k p) f -> p k f", p=128)
        K4 = lambda nm: [(nm, r4) for r4 in range(4)]
        for nb in range(3):
            for hh in range(2):
                B.dma("sp", WGr[:, hh * 4:(hh + 1) * 4, nb * 1024:(nb + 1) * 1024],
                      wgv[:, hh * 4:(hh + 1) * 4, nb * 1024:(nb + 1) * 1024], ikey=K4("WGbf"))
            B.dma("sp", WBr[:, nb * 4:(nb + 1) * 4, :], wbv[:, nb * 4:(nb + 1) * 4, :], ikey=[("WBbf", r4) for r4 in range(3)])
        for hh in range(2):
            B.dma("sp", WOr[:, hh * 4:(hh + 1) * 4, :], wov[:, hh * 4:(hh + 1) * 4, :], ikey=K4("WObf"))
        for bi, (c0, n) in enumerate(blocks):
            isctx = bi == 0
            if isctx and last:
                continue
            w_ = 1 if isctx else 0
            HXB, BRB = HXBs[bi % 2], BRBs[bi % 2]
            if isctx:
                xb = CX
            else:
                xb = XBs[bi % 2]
                B.dma("sp", xb[:], xsv[:, :, c0 - CTX:c0 - CTX + n], ikey=XSK)
            B.dma("sp", HXB[:, :, 0:n], hxv[:, :, c0:c0 + n], ikey=[("HXD", k) for k in range(8)])
            B.dma("sp", BRB[:, :, 0:n], brv[:, :, c0:c0 + n], ikey=[("BR", k) for k in range(12)])
            for j in range(8):
                js = slice(j * 128, (j + 1) * 128)
                for nb in range(3):
                    pg = P()
                    B.mm(pg[:, 0:n], [(WGr[:, k, nb * 1024 + j * 128: nb * 1024 + (j + 1) * 128], HXB[:, k, 0:n]) for k in range(8)])
                    gs = GS[nb]
                    B.act(gs[:, 0:n], pg[:, 0:n], AF.Sigmoid, bias=col("bgate", l * 24 + nb * 8 + j))
                    pb = P()
                    B.mm(pb[:, 0:n], [(WBr[:, nb * 4 + k, js], BRB[:, nb * 4 + k, 0:n]) for k in range(4)])
                    if nb == 0:
                        B.tt("dve", ACC3[:, 0:n], gs[:, 0:n], pb[:, 0:n], ALU.mult)
                    elif nb == 1:
                        B.tt("dve", TMP3[0][:, 0:n], gs[:, 0:n], pb[:, 0:n], ALU.mult)
                        B.tt("pool", ACC3[:, 0:n], ACC3[:, 0:n], TMP3[0][:, 0:n], ALU.add)
                    else:
                        B.tt("dve", TMP3[1][:, 0:n], gs[:, 0:n], pb[:, 0:n], ALU.mult)
                        B.tt("pool", MG[:, j, 0:n], ACC3[:, 0:n], TMP3[1][:, 0:n], ALU.add)
            for j in range(8):
                js = slice(j * 128, (j + 1) * 128)
                py = P()
                B.mm(py[:, 0:n], [(WOr[:, k, js], MG[:, k, 0:n]) for k in range(8)])
                B.stt("dve", xb[:, j, 0:n], py[:, 0:n], modc(l, 2, j, w_), xb[:, j, 0:n], ALU.mult, ALU.add)
            if not isctx:
                B.dma("sp", xsv[:, :, c0 - CTX:c0 - CTX + n], xb[:], okey=XSK)
            for k in range(8):
                B.act(MG[:, k, 0:n], xb[:, k, 0:n], AF.Square)
            pss = P()
            B.mm(pss[:, 0:n], [(ONESB[:], MG[:, k, 0:n]) for k in range(8)])
            B.act(ACC3[:, 0:n], pss[:, 0:n], AF.Ln, scale=1.0 / D, bias=EPS)
            B.act(ACC3[:, 0:n], ACC3[:, 0:n], AF.Exp, scale=-0.5)
            for k in range(8):
                tm = TMP3[k % 2]
                B.tt("dve", tm[:, 0:n], xb[:, k, 0:n], ACC3[:, 0:n], ALU.mult)
                B.act(HXB[:, k, 0:n], tm[:, 0:n], AF.Identity, scale=a2c(l, k, w_), bias=modc(l, 3, k, w_))
            B.dma("sp", hxv[:, :, c0:c0 + n], HXB[:, :, 0:n], okey=[("HXD", k) for k in range(8)], ikey=HXB[:])
        B.pop()
        B.mark("L%d_p3b" % l)
        B.push()
        WFIr = B.sb("WFIr", [128, 8, 2 * FFN], BF16)
        WFOr = B.sb("WFOr", [128, 22, 1024], BF16)
        NB2 = 384
        XB2 = [B.sb("XB2%d" % i, [128, 8, NB2]) for i in range(2)]
        H2s = [B.sb("H2%d" % i, [128, 8, NB2], BF16) for i in range(2)]
        ACTB = B.sb("ACTB", [128, 22, NB2], BF16)
        SQB = ACTB
        GS2 = [B.sb("GS2%d" % i, [128, NB2]) for i in range(3)]
        RS = B.sb("RS", [128, NB2])
        wfiv = WFIbf.rearrange("(k p) f -> p k f", p=128)
        wfov = WFObf.rearrange("(m p) f -> p m f", p=128)
        for pc_ in range(11):
            B.dma("sp", WFIr[:, :, pc_ * 512:(pc_ + 1) * 512], wfiv[:, :, pc_ * 512:(pc_ + 1) * 512], ikey=K4("WFIbf"))
        B.dma("sp", WFOr[:, 0:11, :], wfov[:, 0:11, :], ikey=K4("WFObf"))
        B.dma("sp", WFOr[:, 11:22, :], wfov[:, 11:22, :], ikey=K4("WFObf"))
        blocks2 = [(0, CTX)] + [(CTX + i * NB2, NB2) for i in range(SEQ // NB2)]
        if SEQ % NB2:
            blocks2.append((CTX + (SEQ // NB2) * NB2, SEQ % NB2))
        for bi, (c0, n) in enumerate(blocks2):
            isctx = bi == 0
            if isctx and last:
                continue
            w_ = 1 if isctx else 0
            if isctx:
                xb = CX
            else:
                xb = XB2[bi % 2]
                B.dma("sp", xb[:, :, 0:n], xsv[:, :, c0 - CTX:c0 - CTX + n], ikey=XSK)
            H2 = H2s[bi % 2]
            B.dma("sp", H2[:, :, 0:n], hxv[:, :, c0:c0 + n], ikey=[("HXD", k) for k in range(8)])
            for m in range(22):
                pg = P()
                B.mm(pg[:, 0:n], [(WFIr[:, k, m * 128:(m + 1) * 128], H2[:, k, 0:n]) for k in range(8)])
                pu = P()
                B.mm(pu[:, 0:n], [(WFIr[:, k, FFN + m * 128:FFN + (m + 1) * 128], H2[:, k, 0:n]) for k in range(8)])
                gs = GS2[m % 3]
                B.act(gs[:, 0:n], pg[:, 0:n], AF.Silu)
                B.tt("dve", ACTB[:, m, 0:n], gs[:, 0:n], pu[:, 0:n], ALU.mult)
            for j in range(8):
                py = P()
                B.mm(py[:, 0:n], [(WFOr[:, m, j * 128:(j + 1) * 128], ACTB[:, m, 0:n]) for m in range(22)])
                B.stt("dve", xb[:, j, 0:n], py[:, 0:n], modc(l, 5, j, w_), xb[:, j, 0:n], ALU.mult, ALU.add)
            if isctx:
                continue
            if l == depth - 1:
                for k in range(8):
                    B.act(SQB[:, k, 0:n], xb[:, k, 0:n], AF.Square)
                pss = P()
                B.mm(pss[:, 0:n], [(ONESB[:], SQB[:, k, 0:n]) for k in range(8)])
                B.act(RS[:, 0:n], pss[:, 0:n], AF.Ln, scale=1.0 / D, bias=EPS)
                B.act(RS[:, 0:n], RS[:, 0:n], AF.Exp, scale=-0.5)
                for k in range(8):
                    B.stt("dve", xb[:, k, 0:n], xb[:, k, 0:n], col("fnorm", k), RS[:, 0:n], ALU.mult, ALU.mult)
                B.dma("sp", yv[:, :, c0 - CTX:c0 - CTX + n], xb[:, :, 0:n], okey=("Y", bi))
            else:
                B.dma("sp", xsv[:, :, c0 - CTX:c0 - CTX + n], xb[:, :, 0:n], okey=XSK)
        B.pop()

    B.mark("end")
    B.barrier()
    return B


def _cols_for(inp, b):
    c = np.zeros((128, NCOLS), np.float32)

    def put(name, arr):
        o, w = COLS[name]
        arr = np.asarray(arr, np.float32)
        assert arr.shape == (128, w), (name, arr.shape, w)
        c[:, o:o + w] = arr

    def colform(v):
        v = np.asarray(v, np.float32)
        lead = v.shape[:-1]
        n = v.shape[-1] // 128
        return np.moveaxis(v.reshape(*lead, n, 128), -1, 0)

    put("c", colform(inp["c"][b]))
    put("cctx", colform(inp["c_ctx"]))
    put("fnorm", colform(inp["final_norm"]))
    put("ln1", colform(inp["ln1"]).reshape(128, 32))
    put("ln2", colform(inp["ln2"]).reshape(128, 32))
    b_in = np.asarray(inp["b_in"], np.float32)
    fm_cols = np.concatenate([np.arange(g, g + 512) for g in FM_GROUP_COL])
    put("bfm", colform(b_in[:, fm_cols]).reshape(128, 4 * 32))
    put("bgate", colform(b_in[:, GATE_COL:GATE_COL + 3072]).reshape(128, 4 * 24))
    put("lbraw", colform(inp["hg_lb_raw"]).reshape(128, 32))
    put("hgn", colform(inp["hg_norm"]).reshape(128, 16))
    put("mln", colform(inp["ml_norm"]).reshape(128, 16))
    put("convw", colform(inp["conv_w"]).reshape(128, 64))
    put("convb", colform(inp["conv_b"]).reshape(128, 16))
    put("lgb", colform(inp["lru_gate_b"]).reshape(128, 64))
    put("lam", colform(inp["lru_lambda"]).reshape(128, 32))
    return c


def make_in_maps(inp, cores):
    inp = {k: np.asarray(v) for k, v in inp.items()}
    b_in = inp["b_in"].astype(np.float32)
    tm_cols = np.concatenate([np.arange(512, 1024), np.arange(3072, 3584), np.arange(2816, 3072), np.arange(4096, 4112)])
    shared = {
        "btm": np.ascontiguousarray(b_in[:, tm_cols]),
        "b_ada": np.ascontiguousarray(inp["b_ada"].astype(np.float32).reshape(1, -1)),
        "w_ada": inp["w_ada"], "w_in": inp["w_in"], "w_branch": inp["w_branch"], "w_out": inp["w_out"],
        "w_ffn_in": inp["w_ffn_in"], "w_ffn_out": inp["w_ffn_out"], "lru_gate_w": inp["lru_gate_w"],
    }
    maps = []
    for b in cores:
        m = dict(shared)
        m["xT"] = np.ascontiguousarray(inp["x"][b].T).reshape(8, 128, SEQ)
        m["cxT"] = np.ascontiguousarray(inp["ctx"][b].T).reshape(8, 128, CTX)
        m["cols"] = _cols_for(inp, b)
        maps.append(m)
    return maps


def unpack_out(yT, depth=DEPTH):
    y = np.asarray(yT).reshape(D, SEQ)
    if (depth - 1) % 2 == 1:
        y = y.reshape(D, 64, 64).transpose(0, 2, 1).reshape(D, SEQ)
    return np.ascontiguousarray(y.T)


def kernel(**inputs):
    B = build(DEPTH, False)
    maps = make_in_maps(inputs, list(range(8)))
    res = run_bass_kernel_spmd(B.nc, maps, core_ids=list(range(8)))
    out = np.stack([unpack_out(r["yT"]) for r in res.results], axis=0)
    return out.astype(np.float32)
```

```python
import numpy as np
from contextlib import ExitStack
import concourse.bass as bass
import concourse.mybir as mybir
from concourse.bass_utils import run_bass_kernel_spmd

F32 = mybir.dt.float32
BF16 = mybir.dt.bfloat16
AF = mybir.ActivationFunctionType
ALU = mybir.AluOpType
AX = mybir.AxisListType

D = 1024
SEQ = 4096
CTX = 256
T = SEQ + CTX
NT = T // 128
DEPTH = 4
N_IN = 8208
FFN = 2816
EPS = 1e-6
NFM = 32
NTM = 1296
FM_GROUP_COL = [0, 1024, 1536, 2048, 2560, 3584, 4112, 4624]
GATE_COL = 5136

COLS = {}
_o = 0
for _n, _w in [("c", 8), ("cctx", 8), ("fnorm", 8), ("ln1", 32), ("ln2", 32), ("bfm", 4 * 32), ("bgate", 4 * 24),
               ("lbraw", 32), ("hgn", 16), ("mln", 16), ("convw", 64), ("convb", 16), ("lgb", 64), ("lam", 32)]:
    COLS[_n] = (_o, _w)
    _o += _w
NCOLS = _o


class Builder:
    NDSEM = 24

    def __init__(self):
        self.nc = bass.Bass("TRN2", target_bir_lowering=False)
        self.es = ExitStack()
        nc = self.nc
        self.eng = {"pe": nc.tensor, "act": nc.scalar, "dve": nc.vector, "pool": nc.gpsimd, "sp": nc.sync}
        self.esem = {e: self.es.enter_context(nc.semaphore("se_" + e)) for e in self.eng}
        self.ecnt = {e: 0 for e in self.eng}
        self.seen = {e: {} for e in self.eng}
        self.dsem = [self.es.enter_context(nc.semaphore("sd%d" % i)) for i in range(self.NDSEM)]
        self.dcum = [0] * self.NDSEM
        self.dnext = 0
        self.buf = {}
        self.scopes = []

    def push(self):
        st = ExitStack()
        self.scopes.append(st)
        return st

    def pop(self):
        self.barrier()
        self.scopes.pop().close()

    def _stack(self):
        return self.scopes[-1] if self.scopes else self.es

    def sb(self, name, shape, dt=F32):
        self.nuid = getattr(self, "nuid", 0) + 1
        name = "%s_%d" % (name, self.nuid)
        return self._stack().enter_context(self.nc.sbuf_tensor(name, list(shape), dt))

    def ps(self, name, shape, dt=F32):
        return self.es.enter_context(self.nc.psum_tensor(name, list(shape), dt))

    def dram(self, name, shape, dt=F32, kind="Internal"):
        return self.nc.dram_tensor(name, list(shape), dt, kind=kind).ap()

    @staticmethod
    def keys(x):
        if x is None or isinstance(x, (int, float)):
            return []
        if isinstance(x, list):
            r = []
            for y in x:
                r += Builder.keys(y)
            return r
        if isinstance(x, (tuple, str)):
            return [x]
        return [x.name]

    def _deps(self, ins, outs):
        deps = []
        for k in self.keys(ins):
            st = self.buf.get(k)
            if st and st["w"]:
                deps.append(st["w"])
        for k in self.keys(outs):
            st = self.buf.get(k)
            if st:
                if st["w"]:
                    deps.append(st["w"])
                deps.extend(st["r"].values())
        return deps

    def _wait(self, e, deps):
        seen = self.seen[e]
        need = {}
        for (sid, sem, val) in deps:
            if seen.get(sid, 0) >= val:
                continue
            if sid not in need or need[sid][1] < val:
                need[sid] = (sem, val)
        for sid, (sem, val) in need.items():
            self.eng[e].wait_ge(sem, val)
            seen[sid] = val

    def _record(self, tok, ins, outs):
        for k in self.keys(outs):
            self.buf[k] = {"w": tok, "r": {}}
        for k in self.keys(ins):
            st = self.buf.setdefault(k, {"w": None, "r": {}})
            st["r"][tok[0]] = tok

    def op(self, e, fn, outs, ins):
        self._wait(e, self._deps(ins, outs))
        inst = fn()
        inst.then_inc(self.esem[e], 1)
        self.ecnt[e] += 1
        tok = ("e_" + e + getattr(self, "sid_suffix", ""), self.esem[e], self.ecnt[e])
        self._record(tok, ins, outs)
        return tok

    def dma(self, q, out, in_, okey=None, ikey=None, **kw):
        ok = okey if okey is not None else out
        ik = ikey if ikey is not None else in_
        i = self.dnext
        self.dnext = (self.dnext + 1) % self.NDSEM
        deps = self._deps([ik], [ok])
        if self.dcum[i] > 0:
            deps.append(("d%d" % i, self.dsem[i], self.dcum[i]))
        self._wait(q, deps)
        self.eng[q].dma_start(out=out, in_=in_, **kw).then_inc(self.dsem[i], 16)
        self.dcum[i] += 16
        tok = ("d%d" % i, self.dsem[i], self.dcum[i])
        self._record(tok, [ik], [ok])
        return tok

    def rotate_sems(self):
        self.barrier()
        self.gen = getattr(self, "gen", 0) + 1
        for e in self.eng:
            self.esem[e] = self.es.enter_context(self.nc.semaphore("se%d_%s" % (self.gen, e)))
            self.ecnt[e] = 0
        self.sid_suffix = "_g%d" % self.gen

    def barrier(self):
        for e in self.eng:
            deps = [("e_" + e2 + getattr(self, "sid_suffix", ""), self.esem[e2], self.ecnt[e2]) for e2 in self.eng if e2 != e and self.ecnt[e2] > 0]
            deps += [("d%d" % i, self.dsem[i], self.dcum[i]) for i in range(self.NDSEM) if self.dcum[i] > 0]
            self._wait(e, deps)

    def mark(self, name):
        if not hasattr(self, "marks"):
            self.marks = []
        self.marks.append((name, getattr(self, "npe", 0)))

    def mm(self, out, pairs):
        self.npe = getattr(self, "npe", 0) + len(pairs)
        ins = []
        for l, r in pairs:
            ins += [l, r]
        n = len(pairs)

        def fn():
            inst = None
            for i, (l, r) in enumerate(pairs):
                inst = self.nc.tensor.matmul(out, lhsT=l, rhs=r, start=(i == 0), stop=(i == n - 1))
            return inst
        return self.op("pe", fn, [out], ins)

    def mm_multi(self, groups):
        self.npe = getattr(self, "npe", 0) + sum(len(p) for _, p in groups)
        ins, outs = [], []
        for out, pairs in groups:
            outs.append(out)
            for l, r in pairs:
                ins += [l, r]

        def fn():
            inst = None
            for out, pairs in groups:
                n = len(pairs)
                for i, (l, r) in enumerate(pairs):
                    inst = self.nc.tensor.matmul(out, lhsT=l, rhs=r, start=(i == 0), stop=(i == n - 1))
            return inst
        return self.op("pe", fn, outs, ins)

    def tr(self, out, in_, ident):
        self.npe = getattr(self, "npe", 0) + 1
        return self.op("pe", lambda: self.nc.tensor.transpose(out, in_, ident), [out], [in_, ident])

    def act(self, out, in_, func, bias=None, scale=None):
        kw = {}
        if bias is not None:
            kw["bias"] = bias
        if scale is not None:
            kw["scale"] = scale
        return self.op("act", lambda: self.nc.scalar.activation(out=out, in_=in_, func=func, **kw), [out], [in_, bias, scale])

    def tt(self, e, out, in0, in1, op):
        return self.op(e, lambda: self.eng[e].tensor_tensor(out=out, in0=in0, in1=in1, op=op), [out], [in0, in1])

    def ts(self, e, out, in0, s1, op0, s2=None, op1=None):
        if op1 is None:
            return self.op(e, lambda: self.eng[e].tensor_scalar(out=out, in0=in0, scalar1=s1, scalar2=None, op0=op0), [out], [in0, s1])
        return self.op(e, lambda: self.eng[e].tensor_scalar(out=out, in0=in0, scalar1=s1, scalar2=s2, op0=op0, op1=op1), [out], [in0, s1, s2])

    def stt(self, e, out, in0, scalar, in1, op0, op1):
        return self.op(e, lambda: self.eng[e].scalar_tensor_tensor(out=out, in0=in0, scalar=scalar, in1=in1, op0=op0, op1=op1), [out], [in0, scalar, in1])

    def copy(self, e, out, in_):
        if e == "act":
            return self.op(e, lambda: self.nc.scalar.copy(out=out, in_=in_), [out], [in_])
        return self.op(e, lambda: self.eng[e].tensor_copy(out=out, in_=in_), [out], [in_])

    def memset(self, e, out, val):
        return self.op(e, lambda: self.eng[e].memset(out, val), [out], [])

    def scan(self, out, d0, d1, init, e="dve"):
        return self.op(e, lambda: self.eng[e].tensor_tensor_scan(out=out, data0=d0, data1=d1, initial=init, op0=ALU.mult, op1=ALU.add), [out], [d0, d1, init])

    def recip(self, out, in_, e="dve"):
        return self.op(e, lambda: self.eng[e].reciprocal(out=out, in_=in_), [out], [in_])

    def reduce(self, out, in_, e="dve"):
        return self.op(e, lambda: self.eng[e].tensor_reduce(out=out, in_=in_, axis=AX.X, op=ALU.add), [out], [in_])

    def aselect(self, out, in_, pattern, cmp, fill, base, cm):
        return self.op("pool", lambda: self.nc.gpsimd.affine_select(out=out, in_=in_, pattern=pattern, compare_op=cmp, fill=fill, base=base, channel_multiplier=cm), [out], [in_])


def softplus_negabs(B, out, x, tmp1, tmp2):
    B.stt("dve", tmp1, x, -1.0, x, ALU.mult, ALU.max)
    B.act(tmp1, tmp1, AF.Exp, scale=-1.0)
    B.ts("dve", tmp2, tmp1, 2.0, ALU.add)
    B.recip(tmp2, tmp2)
    B.tt("dve", tmp1, tmp1, tmp2, ALU.mult)
    B.tt("dve", tmp2, tmp1, tmp1, ALU.mult)
    B.ts("dve", out, tmp2, 1.0 / 11.0, ALU.mult, 1.0 / 9.0, ALU.add)
    for cst in (1.0 / 7.0, 1.0 / 5.0, 1.0 / 3.0, 1.0):
        B.tt("dve", out, out, tmp2, ALU.mult)
        B.ts("dve", out, out, cst, ALU.add)
    B.tt("dve", out, out, tmp1, ALU.mult)
    B.ts("dve", out, out, 2.0, ALU.mult)


def build(depth=DEPTH, debug=False):
    B = Builder()
    nc = B.nc
    EI = "ExternalInput"
    xT_d = B.dram("xT", [8, 128, SEQ], F32, EI)
    cxT_d = B.dram("cxT", [8, 128, CTX], F32, EI)
    cols_d = B.dram("cols", [128, NCOLS], F32, EI)
    btm_d = B.dram("btm", [DEPTH, NTM], F32, EI)
    bada_d = B.dram("b_ada", [1, DEPTH * 6 * D], F32, EI)
    wada_d = B.dram("w_ada", [DEPTH, D, 6 * D], F32, EI)
    win_d = B.dram("w_in", [DEPTH, D, N_IN], F32, EI)
    wbr_d = B.dram("w_branch", [DEPTH, 3, 512, D], F32, EI)
    wout_d = B.dram("w_out", [DEPTH, D, D], F32, EI)
    wfi_d = B.dram("w_ffn_in", [DEPTH, D, 2 * FFN], F32, EI)
    wfo_d = B.dram("w_ffn_out", [DEPTH, FFN, D], F32, EI)
    lgw_d = B.dram("lru_gate_w", [DEPTH, 2, 2, 8, 64, 64], F32, EI)
    y_d = B.dram("yT", [8, 128, SEQ], F32, "ExternalOutput")
    dk = "ExternalOutput" if debug else "Internal"
    XS = [B.dram("XS0", [8, 128, SEQ], F32, dk), B.dram("XS1", [8, 128, SEQ], F32, dk)]
    HXD = B.dram("HXD", [8, 128, T], BF16, "Internal")
    PXF = B.dram("PXF", [NFM, 128, T], F32, dk)
    PXT = B.dram("PXT", [NT, 128, NTM], F32, dk)
    BR = B.dram("BR", [12, 128, T], BF16, "Internal")
    BRdbg = B.dram("BRdbg", [12, 128, T], F32, "ExternalOutput") if debug else None
    WGbf = B.dram("WGbf", [D, 3072], BF16, "Internal")
    WBbf = B.dram("WBbf", [1536, D], BF16, "Internal")
    WObf = B.dram("WObf", [D, D], BF16, "Internal")
    WFIbf = B.dram("WFIbf", [D, 2 * FFN], BF16, "Internal")
    WFObf = B.dram("WFObf", [FFN, D], BF16, "Internal")

    PS = [B.ps("P%d" % i, [128, 512], F32) for i in range(7)]
    PB = B.ps("PB", [128, 1024], BF16)
    psrot = [0]

    def P():
        psrot[0] = (psrot[0] + 1) % 7
        return PS[psrot[0]]

    CC = B.sb("CC", [128, NCOLS])
    IDF = B.sb("IDF", [128, 128])
    IDB = B.sb("IDB", [128, 128], BF16)
    ONESF = B.sb("ONESF", [128, 128])
    ONESB = B.sb("ONESB", [128, 128], BF16)
    TRIF = B.sb("TRIF", [128, 128])
    TRIB = B.sb("TRIB", [128, 128])
    MHF = B.sb("MHF", [128, 128])
    MHB = B.sb("MHB", [128, 128])
    MODC = B.sb("MODC", [128, DEPTH * 96])
    A1 = B.sb("A1", [128, DEPTH * 16])
    A2 = B.sb("A2", [128, DEPTH * 16])
    LB = B.sb("LB", [128, 32])
    OML = B.sb("OML", [128, 32])
    CST = B.sb("CST", [128, 32])
    CST2 = B.sb("CST2", [128, 32])
    BQ8 = B.sb("BQ8", [128, 8])
    CX = B.sb("CX", [128, 8, CTX])
    SMT = [B.sb("SMT%d" % i, [128, 32]) for i in range(4)]

    def col(name, i):
        o = COLS[name][0] + i
        return CC[:, o:o + 1]

    def modc(l, j, k, w):
        o = ((l * 48 + j * 8 + k) * 2 + w)
        return MODC[:, o:o + 1]

    def a1c(l, k, w):
        o = (l * 8 + k) * 2 + w
        return A1[:, o:o + 1]

    def a2c(l, k, w):
        o = (l * 8 + k) * 2 + w
        return A2[:, o:o + 1]

    B.dma("sp", CC[:], cols_d[:, :])
    B.dma("sp", CX[:], cxT_d.rearrange("k p t -> p k t"))
    B.memset("pool", ONESF[:], 1.0)
    B.memset("pool", ONESB[:], 1.0)
    B.aselect(IDF[:], ONESF[:], [[-1, 128]], ALU.is_equal, 0.0, 0, 1)
    B.copy("pool", IDB[:], IDF[:])
    B.aselect(TRIF[:], ONESF[:], [[1, 128]], ALU.is_ge, 0.0, 0, -1)
    B.aselect(TRIB[:], ONESF[:], [[-1, 128]], ALU.is_ge, 0.0, 0, 1)
    RM = B.sb("RM", [128, 4])
    BD = B.sb("BD", [128, 128])
    for q in range(4):
        B.aselect(RM[:, q:q + 1], ONESF[:, 0:1], [[0, 1]], ALU.is_ge, 0.0, -32 * q, 1)
        B.aselect(RM[:, q:q + 1], RM[:, q:q + 1], [[0, 1]], ALU.is_ge, 0.0, 32 * q + 31, -1)
        B.copy("pool", BD[:, q * 32:(q + 1) * 32], RM[:, q:q + 1].to_broadcast([128, 32]))
    MASK32 = B.sb("MASK32", [128, 32])
    B.memset("pool", MASK32[:], 1.0)
    B.memset("pool", MASK32[:, 0:1], 0.0)
    B.tt("pool", MHF[:], TRIF[:], BD[:], ALU.mult)
    B.tt("pool", MHB[:], TRIB[:], BD[:], ALU.mult)

    lo, _ = COLS["lbraw"]
    E_ = SMT[0]
    B.act(E_[:], CC[:, lo:lo + 32], AF.Exp)
    S_ = SMT[1]
    B.tt("dve", S_[:, 0:8], E_[:, 0:8], E_[:, 8:16], ALU.add)
    B.tt("dve", S_[:, 0:8], S_[:, 0:8], E_[:, 16:24], ALU.add)
    B.tt("dve", S_[:, 0:8], S_[:, 0:8], E_[:, 24:32], ALU.add)
    B.recip(S_[:, 0:8], S_[:, 0:8])
    for l in range(1, 4):
        B.tt("dve", E_[:, l * 8:(l + 1) * 8], E_[:, l * 8:(l + 1) * 8], S_[:, 0:8], ALU.mult)
    B.memset("dve", LB[:, 0:8], 0.0)
    B.copy("dve", LB[:, 8:16], E_[:, 8:16])
    B.tt("dve", LB[:, 16:24], LB[:, 8:16], E_[:, 16:24], ALU.add)
    B.tt("dve", LB[:, 24:32], LB[:, 16:24], E_[:, 24:32], ALU.add)
    B.ts("dve", OML[:], LB[:], -1.0, ALU.mult, 1.0, ALU.add)
    lo, _ = COLS["lam"]
    softplus_negabs(B, SMT[0][:], CC[:, lo:lo + 32], SMT[1][:], SMT[2][:])
    B.ts("dve", SMT[1][:], CC[:, lo:lo + 32], -1.0, ALU.mult, 0.0, ALU.max)
    B.tt("dve", SMT[0][:], SMT[0][:], SMT[1][:], ALU.add)
    B.ts("dve", CST[:], SMT[0][:], -8.0, ALU.mult)
    B.ts("dve", CST2[:], SMT[0][:], -16.0, ALU.mult)

    B.push()
    S2 = B.sb("S2", [128, 8, 2], BF16)
    BADA = B.sb("BADA", [2, DEPTH * 6 * D])
    MODR = B.sb("MODR", [2, 6 * D])
    WA = [B.sb("WA%d" % i, [128, 8, 512], BF16) for i in range(2)]
    lo, _ = COLS["c"]
    B.act(S2[:, :, 0], CC[:, lo:lo + 8], AF.Silu)
    lo, _ = COLS["cctx"]
    B.act(S2[:, :, 1], CC[:, lo:lo + 8], AF.Silu)
    B.dma("sp", BADA[:], bada_d[0].partition_broadcast(2))
    for l in range(depth):
        wv = wada_d[l].rearrange("(k p) f -> p k f", p=128)
        for fc in range(12):
            w = WA[fc % 2]
            B.dma("pool", w[:], wv[:, :, fc * 512:(fc + 1) * 512])
            p = P()
            B.mm(p[0:2, :], [(S2[:, k, :], w[:, k, :]) for k in range(8)])
            B.tt("dve", MODR[:, fc * 512:(fc + 1) * 512], p[0:2, :], BADA[:, l * 6144 + fc * 512: l * 6144 + (fc + 1) * 512], ALU.add)
        p = P()
        for f in range(48):
            B.tr(p[:, f * 2:(f + 1) * 2], MODR[0:2, f * 128:(f + 1) * 128], IDF[0:2, 0:2])
        B.copy("dve", MODC[:, l * 96:(l + 1) * 96], p[:, 0:96])
        lo1, _ = COLS["ln1"]
        lo2, _ = COLS["ln2"]
        for (AT_, jv, lo_) in ((A1, 1, lo1), (A2, 4, lo2)):
            src = MODC[:, l * 96 + jv * 16: l * 96 + (jv + 1) * 16].rearrange("p (k w) -> p k w", w=2)
            dst = AT_[:, l * 16:(l + 1) * 16].rearrange("p (k w) -> p k w", w=2)
            lnb = CC[:, lo_ + l * 8: lo_ + (l + 1) * 8].unsqueeze(2).to_broadcast([128, 8, 2])
            B.stt("dve", dst, src, 1.0, lnb, ALU.add, ALU.mult)
    B.pop()

    blocks = [(0, CTX)] + [(CTX + i * 512, 512) for i in range(8)]
    cur = None
    for l in range(depth):
        B.rotate_sems()
        last = (l == DEPTH - 1)
        perm = l >= 1
        if l == 0:
            xs_old, xs_new = xT_d, XS[0]
        else:
            xs_old, xs_new = XS[(l - 1) % 2], XS[l % 2]

        B.mark("L%d_p1" % l)
        B.push()
        HX = [B.sb("HX%d" % k, [128, T], BF16) for k in range(8)]
        B.push()
        XT = [B.sb("XT%d" % i, [128, SEQ]) for i in range(2)]
        ACC = B.sb("ACC", [128, SEQ])
        RSTD = B.sb("RSTD", [128, SEQ])
        XP = B.sb("XP", [128, SEQ])
        CACC = B.sb("CACC", [128, CTX])
        CRS = B.sb("CRS", [128, CTX])
        CT = B.sb("CT", [128, CTX])
        for k in range(8):
            xt = XT[k % 2]
            B.dma("sp", xt[:], xs_old[k], ikey=("XS", id(xs_old), k))
            if k == 0:
                B.act(ACC[:], xt[:], AF.Square)
                B.act(CACC[:], CX[:, k, :], AF.Square)
            else:
                B.act(XP[:], xt[:], AF.Square)
                B.tt("pool", ACC[:], ACC[:], XP[:], ALU.add)
                B.act(CT[:], CX[:, k, :], AF.Square)
                B.tt("dve", CACC[:], CACC[:], CT[:], ALU.add)
        for b8 in range(8):
            p = P()
            B.mm(p[:, :], [(ONESF[:], ACC[:, b8 * 512:(b8 + 1) * 512])])
            B.act(RSTD[:, b8 * 512:(b8 + 1) * 512], p[:, :], AF.Ln, scale=1.0 / D, bias=EPS)
        B.act(RSTD[:], RSTD[:], AF.Exp, scale=-0.5)
        p = P()
        B.mm(p[:, 0:CTX], [(ONESF[:], CACC[:])])
        B.act(CRS[:], p[:, 0:CTX], AF.Ln, scale=1.0 / D, bias=EPS)
        B.act(CRS[:], CRS[:], AF.Exp, scale=-0.5)
        for k in range(8):
            xt = XT[k % 2]
            B.dma("sp", xt[:], xs_old[k], ikey=("XS", id(xs_old), k))
            B.tt("dve", ACC[:], xt[:], RSTD[:], ALU.mult)
            hxo = HX[k][:, CTX:T]
            if perm:
                hxo = hxo.rearrange("p (a b) -> p b a", a=64, b=64)
                src = ACC[:].rearrange("p (a b) -> p a b", a=64)
            else:
                src = ACC[:]
            B.act(hxo, src, AF.Identity, scale=a1c(l, k, 0), bias=modc(l, 0, k, 0))
            if perm:
                B.copy("pool", XP[:].rearrange("p (a b) -> p b a", a=64, b=64), xt[:].rearrange("p (a b) -> p a b", a=64))
                B.dma("sp", xs_new[k], XP[:], okey=("XS", id(xs_new), k))
            else:
                B.dma("sp", xs_new[k], xt[:], okey=("XS", id(xs_new), k))
            B.tt("dve", CT[:], CX[:, k, :], CRS[:], ALU.mult)
            B.act(HX[k][:, 0:CTX], CT[:], AF.Identity, scale=a1c(l, k, 1), bias=modc(l, 0, k, 1))
            B.dma("sp", HXD[k], HX[k][:], okey=("HXD", k))
        B.pop()

        B.mark("L%d_p1c" % l)
        B.push()
        BT = B.sb("BT", [128, NTM])
        WT = B.sb("WT", [128, 8, NTM], BF16)
        WF = [B.sb("WF%d" % i, [128, 8, 512], BF16) for i in range(2)]
        STG = [B.sb("STG%d" % i, [128, T]) for i in range(2)]
        STT = [B.sb("STT%d" % i, [128, NTM]) for i in range(2)]
        wv = win_d[l].rearrange("(k p) f -> p k f", p=128)
        B.dma("sp", BT[:], btm_d[l].partition_broadcast(128))
        B.dma("pool", WT[:, :, 0:512], wv[:, :, 512:1024])
        B.dma("pool", WT[:, :, 512:1024], wv[:, :, 3072:3584])
        B.dma("pool", WT[:, :, 1024:1280], wv[:, :, 2816:3072])
        B.dma("pool", WT[:, :, 1280:1296], wv[:, :, 4096:4112])
        lo, _ = COLS["bfm"]
        B.ts("dve", BQ8[:, 0:2], CC[:, lo + l * 32 + 16: lo + l * 32 + 18], 0.125, ALU.mult)
        for g in range(8):
            w = WF[g % 2]
            B.dma("pool", w[:], wv[:, :, FM_GROUP_COL[g]:FM_GROUP_COL[g] + 512])
            for q in range(4):
                ft = g * 4 + q
                stg = STG[ft % 2]
                bias = col("bfm", l * 32 + ft)
                scale = None
                if ft < 8:
                    func = AF.Silu
                elif 20 <= ft < 24:
                    func = AF.Sigmoid
                else:
                    func = AF.Identity
                if ft in (16, 17):
                    scale = 0.125
                    bias = BQ8[:, ft - 16:ft - 15]
                for (c0, n) in blocks:
                    p = P()
                    B.mm(p[:, 0:n], [(w[:, k, q * 128:(q + 1) * 128], HX[k][:, c0:c0 + n]) for k in range(8)])
                    B.act(stg[:, c0:c0 + n], p[:, 0:n], func, bias=bias, scale=scale)
                B.dma("sp", PXF[ft], stg[:], okey=("PXF", ft))
        for i in range(NT):
            sl = slice(i * 128, (i + 1) * 128)
            pa, pb, pc = P(), P(), P()
            B.mm_multi([
                (pa[:, :], [(HX[k][:, sl], WT[:, k, 0:512]) for k in range(8)]),
                (pb[:, :], [(HX[k][:, sl], WT[:, k, 512:1024]) for k in range(8)]),
                (pc[:, 0:272], [(HX[k][:, sl], WT[:, k, 1024:1296]) for k in range(8)]),
            ])
            st = STT[i % 2]
            B.tt("dve", st[:, 0:512], pa[:, :], BT[:, 0:512], ALU.add)
            B.tt("dve", st[:, 512:1024], pb[:, :], BT[:, 512:1024], ALU.add)
            B.tt("dve", st[:, 1024:1296], pc[:, 0:272], BT[:, 1024:1296], ALU.add)
            B.dma("sp", PXT[i], st[:], okey=("PXT", i))
        B.pop()
        B.pop()

        for r4 in range(4):
            rs = slice(r4 * 256, (r4 + 1) * 256)
            B.dma("pool", WGbf[rs, :], win_d[l][rs, GATE_COL:GATE_COL + 3072], okey=("WGbf", r4))
            B.dma("pool", WObf[rs, :], wout_d[l][rs, :], okey=("WObf", r4))
            B.dma("pool", WFIbf[rs, :], wfi_d[l][rs, :], okey=("WFIbf", r4))
        wb2 = wbr_d[l].rearrange("n r f -> (n r) f")
        for r4 in range(3):
            B.dma("pool", WBbf[r4 * 512:(r4 + 1) * 512, :], wb2[r4 * 512:(r4 + 1) * 512, :], okey=("WBbf", r4))
        for r4 in range(4):
            rs = slice(r4 * 704, (r4 + 1) * 704)
            B.dma("pool", WFObf[rs, :], wfo_d[l][rs, :], okey=("WFObf", r4))
        PXT_ALL = [("PXT", i) for i in range(NT)]
        pxt_v = PXT.rearrange("i p c -> p i c")

        B.mark("L%d_lru" % l)
        HT = T // 2
        HB = [(0, HT), (HT, T)]
        SEGS = ((0, CTX), (CTX, T))
        B.push()
        LXs = [B.sb("LX%d" % i, [128, T]) for i in range(2)]
        GWs = [B.sb("GW%d" % i, [128, 4, 128]) for i in range(2)]
        LYs = [[B.sb("LY%d_%d" % (r, i), [128, HT]) for i in range(2)] for r in range(2)]
        XC = [B.sb("XC%d" % i, [128, HT]) for i in range(2)]
        HL = [B.sb("HL%d" % i, [128, HT]) for i in range(2)]
        RR = [B.sb("RR%d" % i, [128, HT]) for i in range(2)]
        II = [B.sb("II%d" % i, [128, HT]) for i in range(2)]
        AA = [B.sb("AA%d" % i, [128, HT]) for i in range(2)]
        OB = [B.sb("OB%d" % i, [128, HT], BF16) for i in range(2)]
        for r in range(2):
            B.memset("pool", GWs[r][:], 0.0)

        def load_lru(j):
            B.dma("sp", LXs[j % 2][:], PXF[24 + j], ikey=("PXF", 24 + j))
            for hs, (r0, r1) in enumerate(HB):
                B.dma("sp", LYs[j % 2][hs][:], PXF[28 + j][:, r0:r1], ikey=("PXF", 28 + j))
            for z in range(2):
                for g in range(2):
                    for h2 in range(2):
                        B.dma("sp", GWs[j % 2][h2 * 64:(h2 + 1) * 64, z * 2 + g, h2 * 64:(h2 + 1) * 64], lgw_d[l, z, g, 2 * j + h2])

        load_lru(0)
        for j in range(4):
            if j + 1 < 4:
                load_lru(j + 1)
            LX, GW, LY = LXs[j % 2], GWs[j % 2], LYs[j % 2]
            cw = lambda tap: col("convw", l * 16 + tap * 4 + j)
            for hs, (r0, r1) in enumerate(HB):
                B.ts("dve", XC[hs][:], LX[:, r0:r1], cw(2), ALU.mult, col("convb", l * 4 + j), ALU.add)
            for tap, o in ((0, -2), (1, -1), (3, 1)):
                for hs, (r0, r1) in enumerate(HB):
                    for (s0, s1) in SEGS:
                        d0 = max(r0, s0 + max(0, -o))
                        d1 = min(r1, s1 - max(0, o))
                        if d1 > d0:
                            B.stt("dve", XC[hs][:, d0 - r0:d1 - r0], LX[:, d0 + o:d1 + o], cw(tap), XC[hs][:, d0 - r0:d1 - r0], ALU.mult, ALU.add)
            for z in range(2):
                for hs, (r0, r1) in enumerate(HB):
                    c = 0
                    while c < HT:
                        n = min(512, HT - c)
                        p = P()
                        B.mm(p[:, 0:n], [(GW[:, z * 2 + 0, :], XC[hs][:, c:c + n])])
                        B.act(RR[hs][:, c:c + n], p[:, 0:n], AF.Sigmoid, bias=col("lgb", l * 16 + (z * 2 + 0) * 4 + j))
                        p = P()
                        B.mm(p[:, 0:n], [(GW[:, z * 2 + 1, :], XC[hs][:, c:c + n])])
                        B.act(II[hs][:, c:c + n], p[:, 0:n], AF.Sigmoid, bias=col("lgb", l * 16 + (z * 2 + 1) * 4 + j))
                        c += n
                ci = l * 8 + z * 4 + j
                for hs in range(2):
                    B.act(AA[hs][:], RR[hs][:], AF.Exp, scale=CST[:, ci:ci + 1])
                for hs in range(2):
                    B.act(RR[hs][:], RR[hs][:], AF.Exp, scale=CST2[:, ci:ci + 1])
                for hs in range(2):
                    B.act(RR[hs][:], RR[hs][:], AF.Sqrt, scale=-1.0, bias=1.0)
                for hs in range(2):
                    B.tt("dve", II[hs][:], II[hs][:], RR[hs][:], ALU.mult)
                for hs in range(2):
                    B.tt("dve" if hs == 0 else "pool", II[hs][:], II[hs][:], XC[hs][:], ALU.mult)
                if z == 0:
                    B.scan(RR[0][:, 0:CTX], AA[0][:, 0:CTX], II[0][:, 0:CTX], 0.0)
                    B.scan(RR[0][:, CTX:HT], AA[0][:, CTX:HT], II[0][:, CTX:HT], RR[0][:, CTX - 1:CTX])
                    B.scan(RR[1][:], AA[1][:], II[1][:], RR[0][:, HT - 1:HT])
                    for hs in range(2):
                        B.copy("act", HL[hs][:], RR[hs][:])
                else:
                    B.scan(RR[0][:, 0:CTX][:, ::-1], AA[0][:, 0:CTX][:, ::-1], II[0][:, 0:CTX][:, ::-1], 0.0)
                    B.scan(RR[1][:, ::-1], AA[1][:, ::-1], II[1][:, ::-1], RR[0][:, 0:1])
                    B.scan(RR[0][:, CTX:HT][:, ::-1], AA[0][:, CTX:HT][:, ::-1], II[0][:, CTX:HT][:, ::-1], RR[1][:, 0:1])
                    for hs in range(2):
                        B.tt("dve", HL[hs][:], HL[hs][:], RR[hs][:], ALU.add)
            for hs in range(2):
                B.act(AA[hs][:], LY[hs][:], AF.Square)
            for hs in range(2):
                B.ts("dve", AA[hs][:], AA[hs][:], 0.044715 * 1.5957691216, ALU.mult, 1.5957691216, ALU.add)
                B.tt("dve", AA[hs][:], AA[hs][:], LY[hs][:], ALU.mult)
            for hs in range(2):
                B.act(AA[hs][:], AA[hs][:], AF.Sigmoid)
            for hs, (r0, r1) in enumerate(HB):
                B.tt("dve" if hs == 0 else "pool", AA[hs][:], AA[hs][:], LY[hs][:], ALU.mult)
                B.tt("dve", OB[hs][:], AA[hs][:], HL[hs][:], ALU.mult)
                B.dma("sp", BR[8 + j][:, r0:r1], OB[hs][:], okey=("BR", 8 + j))
                if debug:
                    B.tt("dve", II[hs][:], AA[hs][:], HL[hs][:], ALU.mult)
                    B.dma("sp", BRdbg[8 + j][:, r0:r1], II[hs][:])
        B.pop()

        B.mark("L%d_mlstm" % l)
        B.push()
        GT = B.sb("GT", [128, NT, 16])
        LF = B.sb("LF", [128, NT, 8])
        BB = B.sb("BBm", [128, NT, 8])
        BTOT = B.sb("BTOT", [128, NT, 8])
        WP = B.sb("WPm", [128, NT, 8])
        WS = B.sb("WSm", [128, NT, 8])
        EN = B.sb("ENm", [128, NT, 8])
        EBT = B.sb("EBT", [128, NT, 8])
        TM1 = B.sb("TM1", [128, NT, 8])
        TM2 = B.sb("TM2", [128, NT, 8])
        B.dma("sp", GT[:], pxt_v[:, :, 1280:1296], ikey=PXT_ALL)
        softplus_negabs(B, LF[:], GT[:, :, 8:16], TM1[:], TM2[:])
        B.ts("dve", TM1[:], GT[:, :, 8:16], 0.0, ALU.min)
        B.tt("dve", LF[:], TM1[:], LF[:], ALU.subtract)
        p = P()
        pv = p[:, 0:NT * 8].rearrange("p (i c) -> p i c", c=8)
        for i in range(NT):
            B.mm(pv[:, i, 0:4], [(TRIF[:], LF[:, i, 0:4])])
            B.mm(pv[:, i, 4:8], [(TRIB[:], LF[:, i, 4:8])])
        B.copy("dve", BB[:], pv)
        p = P()
        B.mm(p[:, 0:NT * 8], [(ONESF[:], LF[:].rearrange("p i c -> p (i c)"))])
        B.copy("dve", BTOT[:], p[:, 0:NT * 8].rearrange("p (i c) -> p i c", c=8))
        B.tt("dve", TM1[:], GT[:, :, 0:8], BB[:], ALU.subtract)
        B.act(WP[:], TM1[:], AF.Exp)
        B.tt("dve", TM1[:], TM1[:], BTOT[:], ALU.add)
        B.act(WS[:], TM1[:], AF.Exp)
        B.act(EN[:], BB[:], AF.Exp, scale=-1.0)
        B.act(EBT[:], BTOT[:], AF.Exp)
        ord_f = list(range(NT))
        ord_b = [1, 0] + list(range(NT - 1, 1, -1))
        HH = B.sb("HH", [128, NT, 132])
        TMPH = B.sb("TMPH", [128, NT, 132])
        KWA = [B.sb("KWA%d" % z, [128, NT, 64], BF16) for z in range(2)]
        DEN = [B.sb("DENm%d" % z, [128, NT]) for z in range(2)]
        OBm = B.sb("OBm", [128, T], BF16)
        CS = [[B.sb("CSm%d_%d" % (z, r), [64, 132]) for r in range(4)] for z in range(2)]
        CB = [[B.sb("CBm%d_%d" % (z, r), [64, 132], BF16) for r in range(4)] for z in range(2)]
        PT = [[B.sb("PTm%d_%d" % (z, r), [128, 128], BF16) for r in range(3)] for z in range(2)]
        KW = [[B.sb("KWm%d_%d" % (z, r), [128, 64], BF16) for r in range(3)] for z in range(2)]
        RC = [[B.sb("RCm%d_%d" % (z, r), [128, 2]) for r in range(3)] for z in range(2)]
        SS = B.sb("SSm", [128, NT])
        MIN_ = [dict(QT=B.sb("QTm%d" % r, [64, T], BF16), KT=B.sb("KTm%d" % r, [64, T], BF16),
                     KTK=B.sb("KTK%d" % r, [128, NT, 64]), VA=B.sb("VA%d" % r, [128, NT, 132], BF16),
                     SOG=B.sb("SOG%d" % r, [128, T])) for r in range(2)]
        for r in range(2):
            B.memset("pool", MIN_[r]["VA"][:, :, 128:132], 1.0)

        def load_head(h):
            m = MIN_[h % 2]
            r0 = (h % 2) * 64
            B.dma("pool", m["QT"][:], PXF[16 + h // 2][r0:r0 + 64, :], ikey=("PXF", 16 + h // 2))
            B.dma("pool", m["KT"][:], PXF[18 + h // 2][r0:r0 + 64, :], ikey=("PXF", 18 + h // 2))
            B.dma("sp", m["KTK"][:], pxt_v[:, :, 1024 + h * 64:1024 + (h + 1) * 64], ikey=PXT_ALL)
            B.dma("pool", m["VA"][:, :, 0:128], pxt_v[:, :, 512 + h * 128:512 + (h + 1) * 128], ikey=PXT_ALL)
            B.dma("sp", m["SOG"][:], PXF[20 + h], ikey=("PXF", 20 + h))

        load_head(0)
        for h in range(4):
            if h + 1 < 4:
                load_head(h + 1)
            m = MIN_[h % 2]
            QT, KT, KTK, VA, SOG = m["QT"], m["KT"], m["KTK"], m["VA"], m["SOG"]
            for z in range(2):
                B.memset("pool", CS[z][0][:], 0.0)
                B.memset("pool", CB[z][0][:], 0.0)
            def emit_pt(step, z):
                i = (ord_f if z == 0 else ord_b)[step]
                sl = slice(i * 128, (i + 1) * 128)
                zh = z * 4 + h
                p = P()
                B.mm(p[:, 0:128], [(KT[:, sl], QT[:, sl])])
                B.stt("dve", PT[z][step % 3][:], p[:, 0:128], WP[:, i, zh:zh + 1], (TRIF if z == 0 else TRIB)[:], ALU.mult, ALU.mult)

            for z in range(2):
                zh = z * 4 + h
                B.tt("dve", KWA[z][:], KTK[:], WS[:, :, zh].unsqueeze(2).to_broadcast([128, NT, 64]), ALU.mult)
                emit_pt(0, z)
            for step in range(NT):
                for z in range(2):
                    i = (ord_f if z == 0 else ord_b)[step]
                    sl = slice(i * 128, (i + 1) * 128)
                    zh = z * 4 + h
                    pd = P()
                    B.mm(pd[0:64, 0:132], [(KWA[z][:, i, :], VA[:, i, :])])
                    B.stt("dve", CS[z][(step + 1) % 4][:], CS[z][step % 4][:], EBT[0:64, i, zh:zh + 1], pd[0:64, 0:132], ALU.mult, ALU.add)
                    B.copy("act", CB[z][(step + 1) % 4][:], CS[z][(step + 1) % 4][:])
                    po = P()
                    B.mm(po[:, 0:132], [(PT[z][step % 3][:], VA[:, i, :]), (QT[:, sl], CB[z][step % 4][:])])
                    B.copy("act", (HH if z == 0 else TMPH)[:, i, :], po[:, 0:132])
                    if step + 1 < NT:
                        emit_pt(step + 1, z)
            for z in range(2):
                zh = z * 4 + h
                Hz = HH if z == 0 else TMPH
                B.stt("dve", DEN[z][:], Hz[:, :, 128], -1.0, Hz[:, :, 128], ALU.mult, ALU.max)
                B.tt("dve", DEN[z][:], DEN[z][:], EN[:, :, zh], ALU.max)
                B.act(DEN[z][:], DEN[z][:], AF.Ln)
                B.act(DEN[z][:], DEN[z][:], AF.Exp, scale=-1.0)
                B.tt("dve", Hz[:, :, 0:128], Hz[:, :, 0:128], DEN[z][:].unsqueeze(2).to_broadcast([128, NT, 128]), ALU.mult)
            B.tt("dve", HH[:, :, 0:128], HH[:, :, 0:128], TMPH[:, :, 0:128], ALU.add)
            B.tt("pool", TMPH[:, :, 0:128], HH[:, :, 0:128], HH[:, :, 0:128], ALU.mult)
            B.reduce(SS[:], TMPH[:, :, 0:128])
            B.act(SS[:], SS[:], AF.Ln, scale=1.0 / 128, bias=EPS)
            B.act(SS[:], SS[:], AF.Exp, scale=-0.5)
            B.tt("dve", HH[:, :, 0:128], HH[:, :, 0:128], SS[:].unsqueeze(2).to_broadcast([128, NT, 128]), ALU.mult)
            for i4 in range(0, NT, 4):
                p = P()
                nn = min(4, NT - i4)
                for u in range(nn):
                    B.tr(p[:, u * 128:(u + 1) * 128], HH[:, i4 + u, 0:128], IDF[:])
                B.stt("dve", OBm[:, i4 * 128:(i4 + nn) * 128], p[:, 0:nn * 128], col("mln", l * 4 + h), SOG[:, i4 * 128:(i4 + nn) * 128], ALU.mult, ALU.mult)
            B.dma("sp", BR[4 + h], OBm[:], okey=("BR", 4 + h))
            if debug:
                B.copy("dve", TMPH[:].rearrange("p i c -> p (i c)")[:, 0:T], OBm[:])
                B.dma("sp", BRdbg[4 + h], TMPH[:].rearrange("p i c -> p (i c)")[:, 0:T])
        B.pop()

        B.mark("L%d_hgrn" % l)
        NCH = T // 32
        B.push()
        RMASK = B.sb("RMASK", [128, T])
        B.memset("pool", RMASK[:], 1.0)
        B.memset("pool", RMASK[:].rearrange("p (c t) -> p c t", t=32)[:, :, 0], 0.0)
        for h in range(4):
            B.push()
            QS = B.sb("QS", [128, T])
            VT = B.sb("VTh", [128, NT, 128], BF16)
            OA = B.sb("OA", [128, T])
            FFh = [B.sb("FF%d" % i, [128, T // 2]) for i in range(2)]
            KKh = [B.sb("KK%d" % i, [128, T // 2]) for i in range(2)]
            EEh = [B.sb("EE%d" % i, [128, T // 2]) for i in range(2)]
            QT2 = B.sb("QT2", [128, T], BF16)
            KT2 = B.sb("KT2", [128, T], BF16)
            KTO = B.sb("KTO", [128, NT, 128], BF16)
            BMID = B.sb("BMID", [128, NCH])
            BLS = B.sb("BLS", [128, NCH])
            EM = B.sb("EMh", [128, NCH])
            EL2 = B.sb("EL2", [128, NCH])
            ELM = B.sb("ELM", [128, NCH])
            RING = 6
            SR = [B.sb("SR%d" % r, [128, 128]) for r in range(RING)]
            PDS = [None, None, None]
            SPR = [B.sb("SPR%d" % r, [128, 128], BF16) for r in range(RING)]
            ATR = [B.sb("ATR%d" % r, [128, 128], BF16) for r in range(3)]
            cidx = 0
            B.dma("sp", QS[:], PXF[h], ikey=("PXF", h))
            B.dma("pool", VT[:], pxt_v[:, :, h * 128:(h + 1) * 128], ikey=PXT_ALL)
            VTM = [B.sb("VTM%d" % q, [128, NT, 128], BF16) for q in range(4)]
            for q in range(4):
                B.ts("dve", VTM[q][:], VT[:], RM[:, q:q + 1], ALU.mult)
            for z in range(2):
                zh = l * 8 + z * 4 + h
                mid = 15 if z == 0 else 16
                lastp = 31 if z == 0 else 0
                HT = T // 2
                NCH2 = NCH // 2

                def steps(hs):
                    r = slice(hs * HT, (hs + 1) * HT)
                    cr = slice(hs * NCH2, (hs + 1) * NCH2)
                    F_, K_, E_ = FFh[hs], KKh[hs], EEh[hs]
                    ev = E_[:].rearrange("p (c t) -> p c t", t=32)
                    fv = F_[:].rearrange("p (c t) -> p c t", t=32)
                    KT3 = E_[:].bitcast(BF16)[:, 0:HT]
                    yield lambda: B.dma("sp", F_[:], PXF[8 + z * 4 + h][:, r], ikey=("PXF", 8 + z * 4 + h))
                    yield lambda: B.act(F_[:], F_[:], AF.Sigmoid)
                    yield lambda: B.ts("dve", F_[:], F_[:], OML[:, zh:zh + 1], ALU.mult, LB[:, zh:zh + 1], ALU.add)
                    yield lambda: B.act(K_[:], F_[:], AF.Identity, scale=-1.0, bias=1.0)
                    yield lambda: B.act(F_[:], F_[:], AF.Ln)
                    yield lambda: B.scan(E_[:], RMASK[:, r], F_[:], 0.0)
                    if z == 1:
                        yield lambda: B.copy("dve", BLS[:, cr], ev[:, :, 31])
                        yield lambda: B.tt("dve", ev, BLS[:, cr].unsqueeze(2).to_broadcast([128, NCH2, 32]), ev, ALU.subtract)
                        yield lambda: B.tt("dve", E_[:], E_[:], F_[:], ALU.add)
                    yield lambda: B.copy("pool", BMID[:, cr], ev[:, :, mid])
                    yield lambda: B.copy("pool", BLS[:, cr], ev[:, :, lastp])
                    yield lambda: B.act(EM[:, cr], BMID[:, cr], AF.Exp)
                    yield lambda: B.act(EL2[:, cr], BLS[:, cr], AF.Exp)
                    yield lambda: B.tt("pool", BLS[:, cr], BLS[:, cr], BMID[:, cr], ALU.subtract)
                    yield lambda: B.act(ELM[:, cr], BLS[:, cr], AF.Exp)
                    yield lambda: B.tt("dve", ev, ev, BMID[:, cr].unsqueeze(2).to_broadcast([128, NCH2, 32]), ALU.subtract)
                    yield lambda: B.act(F_[:], E_[:], AF.Exp)
                    yield lambda: B.tt("dve", QT2[:, r], QS[:, r], F_[:], ALU.mult)
                    yield lambda: B.act(F_[:], E_[:], AF.Exp, scale=-1.0)
                    yield lambda: B.tt("dve", KT2[:, r], K_[:], F_[:], ALU.mult)
                    yield lambda: B.tt("dve", fv, fv, ELM[:, cr].unsqueeze(2).to_broadcast([128, NCH2, 32]), ALU.mult)
                    yield lambda: B.tt("dve", KT3, K_[:], F_[:], ALU.mult)
                    nth = NT // 2
                    for i4 in range(0, nth, 4):
                        nn = min(4, nth - i4)

                        def trs(i4=i4, nn=nn):
                            for u in range(nn):
                                B.tr(PB[:, u * 128:(u + 1) * 128], KT3[:, (i4 + u) * 128:(i4 + u + 1) * 128], IDB[:])
                            B.copy("act", KTO[:, hs * nth + i4:hs * nth + i4 + nn, :], PB[:, 0:nn * 128].rearrange("p (u k) -> p u k", k=128))
                        yield trs

                gens = [steps(0), steps(1)]
                live = [True, True]
                while any(live):
                    for hs in range(2):
                        if live[hs]:
                            try:
                                next(gens[hs])()
                            except StopIteration:
                                live[hs] = False
                B.memset("pool", SR[0][:], 0.0)
                order = ord_f if z == 0 else ord_b
                cidx = 0

                def emit_pre(ti):
                    i = order[ti]
                    sl = slice(i * 128, (i + 1) * 128)
                    pa = P()
                    B.mm(pa[:, 0:128], [(KT2[:, sl], QT2[:, sl])])
                    B.tt("dve", ATR[ti % 3][:], pa[:, 0:128], (MHF if z == 0 else MHB)[:], ALU.mult)
                    pd = P()
                    B.mm_multi([(pd[:, hf * 128:(hf + 1) * 128], [(KTO[:, i, :], VTM[hf][:, i, :])]) for hf in range(4)])
                    PDS[ti % 3] = pd

                emit_pre(0)
                for ti, i in enumerate(order):
                    sl = slice(i * 128, (i + 1) * 128)
                    if ti + 1 < NT:
                        emit_pre(ti + 1)
                    ATb = ATR[ti % 3]
                    po = P()
                    halves = (0, 1, 2, 3) if z == 0 else (3, 2, 1, 0)
                    for hf in halves:
                        c = 4 * i + hf
                        lo = hf * 32
                        scur, snext = SR[cidx % RING], SR[(cidx + 1) % RING]
                        spb = SPR[cidx % RING]
                        B.stt("dve", snext[:], scur[:], EL2[:, c:c + 1], PDS[ti % 3][:, hf * 128:(hf + 1) * 128], ALU.mult, ALU.add)
                        B.act(spb[:], scur[:], AF.Identity, scale=EM[:, c:c + 1])
                        B.mm(po[:, lo:lo + 32], [(VT[:, i, :], ATb[:, lo:lo + 32]), (spb[:], QT2[:, c * 32:(c + 1) * 32])])
                        cidx += 1
                    if z == 0:
                        B.copy("act", OA[:, sl], po[:, 0:128])
                    else:
                        B.tt("dve", OA[:, sl], OA[:, sl], po[:, 0:128], ALU.add)
            HT = T // 2
            for hs in range(2):
                r0 = hs * HT
                B.act(EEh[hs][:], OA[:, r0:r0 + HT], AF.Square)
                B.dma("sp", KKh[hs][:], PXF[4 + h][:, r0:r0 + HT], ikey=("PXF", 4 + h))
            for hs in range(2):
                r0 = hs * HT
                c = 0
                while c < HT:
                    n = min(512, HT - c)
                    p = P()
                    B.mm(p[:, 0:n], [(ONESF[:], EEh[hs][:, c:c + n])])
                    B.act(FFh[hs][:, c:c + n], p[:, 0:n], AF.Ln, scale=1.0 / 128, bias=EPS)
                    c += n
                B.act(FFh[hs][:], FFh[hs][:], AF.Exp, scale=-0.5)
            OBh = QT2
            for hs in range(2):
                r0 = hs * HT
                B.tt("dve", OA[:, r0:r0 + HT], OA[:, r0:r0 + HT], FFh[hs][:], ALU.mult)
                B.stt("dve", OBh[:, r0:r0 + HT], OA[:, r0:r0 + HT], col("hgn", l * 4 + h), KKh[hs][:], ALU.mult, ALU.mult)
            B.dma("sp", BR[h], OBh[:], okey=("BR", h))
            if debug:
                B.copy("dve", OA[:], OBh[:])
                B.dma("sp", BRdbg[h], OA[:])
            B.pop()
        B.pop()

        B.mark("L%d_p3" % l)
        xsv = xs_new.rearrange("k p t -> p k t")
        hxv = HXD.rearrange("k p t -> p k t")
        brv = BR.rearrange("k p t -> p k t")
        yv = y_d.rearrange("k p t -> p k t")
        XSK = [("XS", id(xs_new), k) for k in range(8)]
        wv = win_d[l].rearrange("(k p) f -> p k f", p=128)
        B.push()
        WGr = B.sb("WGr", [128, 8, 3072], BF16)
        WBr = B.sb("WBr", [128, 12, 1024], BF16)
        WOr = B.sb("WOr", [128, 8, 1024], BF16)
        XBs = [B.sb("XB%d" % i, [128, 8, 512]) for i in range(2)]
        HXBs = [B.sb("HXB%d" % i, [128, 8, 512], BF16) for i in range(2)]
        BRBs = [B.sb("BRB%d" % i, [128, 12, 512], BF16) for i in range(2)]
        MG = B.sb("MG", [128, 8, 512], BF16)
        GS = [B.sb("GS%d" % i, [128, 512]) for i in range(3)]
        TMP3 = [B.sb("TMP3%d" % i, [128, 512]) for i in range(2)]
        ACC3 = B.sb("ACC3", [128, 512])
        wgv = WGbf.rearrange("(k p) f -> p k f", p=128)
        wbv = WBbf.rearrange("(q p) f -> p q f", p=128)
        wov = WObf.rearrange("(k p) f -> p k f", p=128)
        K4 = lambda nm: [(nm, r4) for r4 in range(4)]
        for nb in range(3):
            for hh in range(2):
                B.dma("sp", WGr[:, hh * 4:(hh + 1) * 4, nb * 1024:(nb + 1) * 1024],
                      wgv[:, hh * 4:(hh + 1) * 4, nb * 1024:(nb + 1) * 1024], ikey=K4("WGbf"))
            B.dma("sp", WBr[:, nb * 4:(nb + 1) * 4, :], wbv[:, nb * 4:(nb + 1) * 4, :], ikey=[("WBbf", r4) for r4 in range(3)])
        for hh in range(2):
            B.dma("sp", WOr[:, hh * 4:(hh + 1) * 4, :], wov[:, hh * 4:(hh + 1) * 4, :], ikey=K4("WObf"))
        blist = [(bi, c0, n) for bi, (c0, n) in enumerate(blocks) if not (bi == 0 and last)]

        def load3a(bi, c0, n):
            if bi != 0:
                B.dma("sp", XBs[bi % 2][:], xsv[:, :, c0 - CTX:c0 - CTX + n], ikey=XSK)
            B.dma("sp", HXBs[bi % 2][:, :, 0:n], hxv[:, :, c0:c0 + n], ikey=[("HXD", k) for k in range(8)])
            B.dma("sp", BRBs[bi % 2][:, :, 0:n], brv[:, :, c0:c0 + n], ikey=[("BR", k) for k in range(12)])

        load3a(*blist[0])
        for bpos, (bi, c0, n) in enumerate(blist):
            isctx = bi == 0
            if bpos + 1 < len(blist):
                load3a(*blist[bpos + 1])
            w_ = 1 if isctx else 0
            HXB, BRB = HXBs[bi % 2], BRBs[bi % 2]
            xb = CX if isctx else XBs[bi % 2]
            for j in range(8):
                js = slice(j * 128, (j + 1) * 128)
                for nb in range(3):
                    pg = P()
                    B.mm(pg[:, 0:n], [(WGr[:, k, nb * 1024 + j * 128: nb * 1024 + (j + 1) * 128], HXB[:, k, 0:n]) for k in range(8)])
                    gs = GS[nb]
                    B.act(gs[:, 0:n], pg[:, 0:n], AF.Sigmoid, bias=col("bgate", l * 24 + nb * 8 + j))
                    pb = P()
                    B.mm(pb[:, 0:n], [(WBr[:, nb * 4 + k, js], BRB[:, nb * 4 + k, 0:n]) for k in range(4)])
                    if nb == 0:
                        B.tt("dve", ACC3[:, 0:n], gs[:, 0:n], pb[:, 0:n], ALU.mult)
                    elif nb == 1:
                        B.tt("dve", TMP3[0][:, 0:n], gs[:, 0:n], pb[:, 0:n], ALU.mult)
                        B.tt("pool", ACC3[:, 0:n], ACC3[:, 0:n], TMP3[0][:, 0:n], ALU.add)
                    else:
                        B.tt("dve", TMP3[1][:, 0:n], gs[:, 0:n], pb[:, 0:n], ALU.mult)
                        B.tt("pool", MG[:, j, 0:n], ACC3[:, 0:n], TMP3[1][:, 0:n], ALU.add)
            for j in range(8):
                js = slice(j * 128, (j + 1) * 128)
                py = P()
                B.mm(py[:, 0:n], [(WOr[:, k, js], MG[:, k, 0:n]) for k in range(8)])
                B.stt("dve", xb[:, j, 0:n], py[:, 0:n], modc(l, 2, j, w_), xb[:, j, 0:n], ALU.mult, ALU.add)
            if not isctx:
                B.dma("sp", xsv[:, :, c0 - CTX:c0 - CTX + n], xb[:], okey=XSK)
            for k in range(8):
                B.act(MG[:, k, 0:n], xb[:, k, 0:n], AF.Square)
            pss = P()
            B.mm(pss[:, 0:n], [(ONESB[:], MG[:, k, 0:n]) for k in range(8)])
            B.act(ACC3[:, 0:n], pss[:, 0:n], AF.Ln, scale=1.0 / D, bias=EPS)
            B.act(ACC3[:, 0:n], ACC3[:, 0:n], AF.Exp, scale=-0.5)
            for k in range(8):
                tm = TMP3[k % 2]
                B.tt("dve", tm[:, 0:n], xb[:, k, 0:n], ACC3[:, 0:n], ALU.mult)
                B.act(HXB[:, k, 0:n], tm[:, 0:n], AF.Identity, scale=a2c(l, k, w_), bias=modc(l, 3, k, w_))
            B.dma("sp", hxv[:, :, c0:c0 + n], HXB[:, :, 0:n], okey=[("HXD", k) for k in range(8)], ikey=HXB[:])
        B.pop()
        B.mark("L%d_p3b" % l)
        B.push()
        WFIr = B.sb("WFIr", [128, 8, 2 * FFN], BF16)
        WFOr = B.sb("WFOr", [128, 22, 1024], BF16)
        NB2 = 384
        XB2 = [B.sb("XB2%d" % i, [128, 8, NB2]) for i in range(2)]
        H2s = [B.sb("H2%d" % i, [128, 8, NB2], BF16) for i in range(2)]
        ACTB = B.sb("ACTB", [128, 22, NB2], BF16)
        SQB = ACTB
        GS2 = [B.sb("GS2%d" % i, [128, NB2]) for i in range(3)]
        RS = B.sb("RS", [128, NB2])
        wfiv = WFIbf.rearrange("(k p) f -> p k f", p=128)
        wfov = WFObf.rearrange("(m p) f -> p m f", p=128)
        for pc_ in range(11):
            B.dma("sp", WFIr[:, :, pc_ * 512:(pc_ + 1) * 512], wfiv[:, :, pc_ * 512:(pc_ + 1) * 512], ikey=K4("WFIbf"))
        B.dma("sp", WFOr[:, 0:11, :], wfov[:, 0:11, :], ikey=K4("WFObf"))
        B.dma("sp", WFOr[:, 11:22, :], wfov[:, 11:22, :], ikey=K4("WFObf"))
        blocks2 = [(0, CTX)] + [(CTX + i * NB2, NB2) for i in range(SEQ // NB2)]
        if SEQ % NB2:
            blocks2.append((CTX + (SEQ // NB2) * NB2, SEQ % NB2))
        blist2 = [(bi, c0, n) for bi, (c0, n) in enumerate(blocks2) if not (bi == 0 and last)]

        def load3b(bi, c0, n):
            if bi != 0:
                B.dma("sp", XB2[bi % 2][:, :, 0:n], xsv[:, :, c0 - CTX:c0 - CTX + n], ikey=XSK)
            B.dma("sp", H2s[bi % 2][:, :, 0:n], hxv[:, :, c0:c0 + n], ikey=[("HXD", k) for k in range(8)])

        load3b(*blist2[0])
        for bpos, (bi, c0, n) in enumerate(blist2):
            isctx = bi == 0
            if bpos + 1 < len(blist2):
                load3b(*blist2[bpos + 1])
            w_ = 1 if isctx else 0
            xb = CX if isctx else XB2[bi % 2]
            H2 = H2s[bi % 2]
            for m in range(22):
                pg = P()
                B.mm(pg[:, 0:n], [(WFIr[:, k, m * 128:(m + 1) * 128], H2[:, k, 0:n]) for k in range(8)])
                pu = P()
                B.mm(pu[:, 0:n], [(WFIr[:, k, FFN + m * 128:FFN + (m + 1) * 128], H2[:, k, 0:n]) for k in range(8)])
                gs = GS2[m % 3]
                B.act(gs[:, 0:n], pg[:, 0:n], AF.Silu)
                B.tt("dve", ACTB[:, m, 0:n], gs[:, 0:n], pu[:, 0:n], ALU.mult)
            for j in range(8):
                py = P()
                B.mm(py[:, 0:n], [(WFOr[:, m, j * 128:(j + 1) * 128], ACTB[:, m, 0:n]) for m in range(22)])
                B.stt("dve", xb[:, j, 0:n], py[:, 0:n], modc(l, 5, j, w_), xb[:, j, 0:n], ALU.mult, ALU.add)
            if isctx:
                continue
            if l == depth - 1:
                for k in range(8):
                    B.act(SQB[:, k, 0:n], xb[:, k, 0:n], AF.Square)
                pss = P()
                B.mm(pss[:, 0:n], [(ONESB[:], SQB[:, k, 0:n]) for k in range(8)])
                B.act(RS[:, 0:n], pss[:, 0:n], AF.Ln, scale=1.0 / D, bias=EPS)
                B.act(RS[:, 0:n], RS[:, 0:n], AF.Exp, scale=-0.5)
                for k in range(8):
                    B.stt("dve", xb[:, k, 0:n], xb[:, k, 0:n], col("fnorm", k), RS[:, 0:n], ALU.mult, ALU.mult)
                B.dma("sp", yv[:, :, c0 - CTX:c0 - CTX + n], xb[:, :, 0:n], okey=("Y", bi))
            else:
                B.dma("sp", xsv[:, :, c0 - CTX:c0 - CTX + n], xb[:, :, 0:n], okey=XSK)
        B.pop()

    B.mark("end")
    B.barrier()
    return B


def _cols_for(inp, b):
    c = np.zeros((128, NCOLS), np.float32)

    def put(name, arr):
        o, w = COLS[name]
        arr = np.asarray(arr, np.float32)
        assert arr.shape == (128, w), (name, arr.shape, w)
        c[:, o:o + w] = arr

    def colform(v):
        v = np.asarray(v, np.float32)
        lead = v.shape[:-1]
        n = v.shape[-1] // 128
        return np.moveaxis(v.reshape(*lead, n, 128), -1, 0)

    put("c", colform(inp["c"][b]))
    put("cctx", colform(inp["c_ctx"]))
    put("fnorm", colform(inp["final_norm"]))
    put("ln1", colform(inp["ln1"]).reshape(128, 32))
    put("ln2", colform(inp["ln2"]).reshape(128, 32))
    b_in = np.asarray(inp["b_in"], np.float32)
    fm_cols = np.concatenate([np.arange(g, g + 512) for g in FM_GROUP_COL])
    put("bfm", colform(b_in[:, fm_cols]).reshape(128, 4 * 32))
    put("bgate", colform(b_in[:, GATE_COL:GATE_COL + 3072]).reshape(128, 4 * 24))
    put("lbraw", colform(inp["hg_lb_raw"]).reshape(128, 32))
    put("hgn", colform(inp["hg_norm"]).reshape(128, 16))
    put("mln", colform(inp["ml_norm"]).reshape(128, 16))
    put("convw", colform(inp["conv_w"]).reshape(128, 64))
    put("convb", colform(inp["conv_b"]).reshape(128, 16))
    put("lgb", colform(inp["lru_gate_b"]).reshape(128, 64))
    put("lam", colform(inp["lru_lambda"]).reshape(128, 32))
    return c


def make_in_maps(inp, cores):
    inp = {k: np.asarray(v) for k, v in inp.items()}
    b_in = inp["b_in"].astype(np.float32)
    tm_cols = np.concatenate([np.arange(512, 1024), np.arange(3072, 3584), np.arange(2816, 3072), np.arange(4096, 4112)])
    shared = {
        "btm": np.ascontiguousarray(b_in[:, tm_cols]),
        "b_ada": np.ascontiguousarray(inp["b_ada"].astype(np.float32).reshape(1, -1)),
        "w_ada": inp["w_ada"], "w_in": inp["w_in"], "w_branch": inp["w_branch"], "w_out": inp["w_out"],
        "w_ffn_in": inp["w_ffn_in"], "w_ffn_out": inp["w_ffn_out"], "lru_gate_w": inp["lru_gate_w"],
    }
    maps = []
    for b in cores:
        m = dict(shared)
        m["xT"] = np.ascontiguousarray(inp["x"][b].T).reshape(8, 128, SEQ)
        m["cxT"] = np.ascontiguousarray(inp["ctx"][b].T).reshape(8, 128, CTX)
        m["cols"] = _cols_for(inp, b)
        maps.append(m)
    return maps


def unpack_out(yT, depth=DEPTH):
    y = np.asarray(yT).reshape(D, SEQ)
    if (depth - 1) % 2 == 1:
        y = y.reshape(D, 64, 64).transpose(0, 2, 1).reshape(D, SEQ)
    return np.ascontiguousarray(y.T)


def kernel(**inputs):
    B = build(DEPTH, False)
    maps = make_in_maps(inputs, list(range(8)))
    res = run_bass_kernel_spmd(B.nc, maps, core_ids=list(range(8)))
    out = np.stack([unpack_out(r["yT"]) for r in res.results], axis=0)
    return out.astype(np.float32)
```

```python
import numpy as np
from contextlib import ExitStack
import concourse.bass as bass
import concourse.mybir as mybir
from concourse.bass_utils import run_bass_kernel_spmd

F32 = mybir.dt.float32
BF16 = mybir.dt.bfloat16
AF = mybir.ActivationFunctionType
ALU = mybir.AluOpType
AX = mybir.AxisListType

D = 1024
SEQ = 4096
CTX = 256
T = SEQ + CTX
NT = T // 128
DEPTH = 4
N_IN = 8208
FFN = 2816
EPS = 1e-6
NFM = 32
NTM = 1296
FM_GROUP_COL = [0, 1024, 1536, 2048, 2560, 3584, 4112, 4624]
GATE_COL = 5136

COLS = {}
_o = 0
for _n, _w in [("c", 8), ("cctx", 8), ("fnorm", 8), ("ln1", 32), ("ln2", 32), ("bfm", 4 * 32), ("bgate", 4 * 24),
               ("lbraw", 32), ("hgn", 16), ("mln", 16), ("convw", 64), ("convb", 16), ("lgb", 64), ("lam", 32)]:
    COLS[_n] = (_o, _w)
    _o += _w
NCOLS = _o


class Builder:
    NDSEM = 24

    def __init__(self):
        self.nc = bass.Bass("TRN2", target_bir_lowering=False)
        self.es = ExitStack()
        nc = self.nc
        self.eng = {"pe": nc.tensor, "act": nc.scalar, "dve": nc.vector, "pool": nc.gpsimd, "sp": nc.sync}
        self.esem = {e: self.es.enter_context(nc.semaphore("se_" + e)) for e in self.eng}
        self.ecnt = {e: 0 for e in self.eng}
        self.seen = {e: {} for e in self.eng}
        self.dsem = [self.es.enter_context(nc.semaphore("sd%d" % i)) for i in range(self.NDSEM)]
        self.dcum = [0] * self.NDSEM
        self.dnext = 0
        self.buf = {}
        self.scopes = []

    def push(self):
        st = ExitStack()
        self.scopes.append(st)
        return st

    def pop(self):
        self.barrier()
        self.scopes.pop().close()

    def _stack(self):
        return self.scopes[-1] if self.scopes else self.es

    def sb(self, name, shape, dt=F32):
        self.nuid = getattr(self, "nuid", 0) + 1
        name = "%s_%d" % (name, self.nuid)
        return self._stack().enter_context(self.nc.sbuf_tensor(name, list(shape), dt))

    def ps(self, name, shape, dt=F32):
        return self.es.enter_context(self.nc.psum_tensor(name, list(shape), dt))

    def dram(self, name, shape, dt=F32, kind="Internal"):
        return self.nc.dram_tensor(name, list(shape), dt, kind=kind).ap()

    @staticmethod
    def keys(x):
        if x is None or isinstance(x, (int, float)):
            return []
        if isinstance(x, list):
            r = []
            for y in x:
                r += Builder.keys(y)
            return r
        if isinstance(x, (tuple, str)):
            return [x]
        return [x.name]

    def _deps(self, ins, outs):
        deps = []
        for k in self.keys(ins):
            st = self.buf.get(k)
            if st and st["w"]:
                deps.append(st["w"])
        for k in self.keys(outs):
            st = self.buf.get(k)
            if st:
                if st["w"]:
                    deps.append(st["w"])
                deps.extend(st["r"].values())
        return deps

    def _wait(self, e, deps):
        seen = self.seen[e]
        need = {}
        for (sid, sem, val) in deps:
            if seen.get(sid, 0) >= val:
                continue
            if sid not in need or need[sid][1] < val:
                need[sid] = (sem, val)
        for sid, (sem, val) in need.items():
            self.eng[e].wait_ge(sem, val)
            seen[sid] = val

    def _record(self, tok, ins, outs):
        for k in self.keys(outs):
            self.buf[k] = {"w": tok, "r": {}}
        for k in self.keys(ins):
            st = self.buf.setdefault(k, {"w": None, "r": {}})
            st["r"][tok[0]] = tok

    def op(self, e, fn, outs, ins):
        self._wait(e, self._deps(ins, outs))
        inst = fn()
        inst.then_inc(self.esem[e], 1)
        self.ecnt[e] += 1
        tok = ("e_" + e + getattr(self, "sid_suffix", ""), self.esem[e], self.ecnt[e])
        self._record(tok, ins, outs)
        return tok

    def dma(self, q, out, in_, okey=None, ikey=None, **kw):
        ok = okey if okey is not None else out
        ik = ikey if ikey is not None else in_
        i = self.dnext
        self.dnext = (self.dnext + 1) % self.NDSEM
        deps = self._deps([ik], [ok])
        if self.dcum[i] > 0:
            deps.append(("d%d" % i, self.dsem[i], self.dcum[i]))
        self._wait(q, deps)
        self.eng[q].dma_start(out=out, in_=in_, **kw).then_inc(self.dsem[i], 16)
        self.dcum[i] += 16
        tok = ("d%d" % i, self.dsem[i], self.dcum[i])
        self._record(tok, [ik], [ok])
        return tok

    def rotate_sems(self):
        self.barrier()
        self.gen = getattr(self, "gen", 0) + 1
        for e in self.eng:
            self.esem[e] = self.es.enter_context(self.nc.semaphore("se%d_%s" % (self.gen, e)))
            self.ecnt[e] = 0
        self.sid_suffix = "_g%d" % self.gen

    def barrier(self):
        for e in self.eng:
            deps = [("e_" + e2 + getattr(self, "sid_suffix", ""), self.esem[e2], self.ecnt[e2]) for e2 in self.eng if e2 != e and self.ecnt[e2] > 0]
            deps += [("d%d" % i, self.dsem[i], self.dcum[i]) for i in range(self.NDSEM) if self.dcum[i] > 0]
            self._wait(e, deps)

    def mark(self, name):
        if not hasattr(self, "marks"):
            self.marks = []
        self.marks.append((name, getattr(self, "npe", 0)))

    def mm(self, out, pairs):
        self.npe = getattr(self, "npe", 0) + len(pairs)
        ins = []
        for l, r in pairs:
            ins += [l, r]
        n = len(pairs)

        def fn():
            inst = None
            for i, (l, r) in enumerate(pairs):
                inst = self.nc.tensor.matmul(out, lhsT=l, rhs=r, start=(i == 0), stop=(i == n - 1))
            return inst
        return self.op("pe", fn, [out], ins)

    def mm_multi(self, groups):
        self.npe = getattr(self, "npe", 0) + sum(len(p) for _, p in groups)
        ins, outs = [], []
        for out, pairs in groups:
            outs.append(out)
            for l, r in pairs:
                ins += [l, r]

        def fn():
            inst = None
            for out, pairs in groups:
                n = len(pairs)
                for i, (l, r) in enumerate(pairs):
                    inst = self.nc.tensor.matmul(out, lhsT=l, rhs=r, start=(i == 0), stop=(i == n - 1))
            return inst
        return self.op("pe", fn, outs, ins)

    def tr(self, out, in_, ident):
        self.npe = getattr(self, "npe", 0) + 1
        return self.op("pe", lambda: self.nc.tensor.transpose(out, in_, ident), [out], [in_, ident])

    def act(self, out, in_, func, bias=None, scale=None):
        kw = {}
        if bias is not None:
            kw["bias"] = bias
        if scale is not None:
            kw["scale"] = scale
        return self.op("act", lambda: self.nc.scalar.activation(out=out, in_=in_, func=func, **kw), [out], [in_, bias, scale])

    def tt(self, e, out, in0, in1, op):
        return self.op(e, lambda: self.eng[e].tensor_tensor(out=out, in0=in0, in1=in1, op=op), [out], [in0, in1])

    def ts(self, e, out, in0, s1, op0, s2=None, op1=None):
        if op1 is None:
            return self.op(e, lambda: self.eng[e].tensor_scalar(out=out, in0=in0, scalar1=s1, scalar2=None, op0=op0), [out], [in0, s1])
        return self.op(e, lambda: self.eng[e].tensor_scalar(out=out, in0=in0, scalar1=s1, scalar2=s2, op0=op0, op1=op1), [out], [in0, s1, s2])

    def stt(self, e, out, in0, scalar, in1, op0, op1):
        return self.op(e, lambda: self.eng[e].scalar_tensor_tensor(out=out, in0=in0, scalar=scalar, in1=in1, op0=op0, op1=op1), [out], [in0, scalar, in1])

    def copy(self, e, out, in_):
        if e == "act":
            return self.op(e, lambda: self.nc.scalar.copy(out=out, in_=in_), [out], [in_])
        return self.op(e, lambda: self.eng[e].tensor_copy(out=out, in_=in_), [out], [in_])

    def memset(self, e, out, val):
        return self.op(e, lambda: self.eng[e].memset(out, val), [out], [])

    def scan(self, out, d0, d1, init, e="dve"):
        return self.op(e, lambda: self.eng[e].tensor_tensor_scan(out=out, data0=d0, data1=d1, initial=init, op0=ALU.mult, op1=ALU.add), [out], [d0, d1, init])

    def recip(self, out, in_, e="dve"):
        return self.op(e, lambda: self.eng[e].reciprocal(out=out, in_=in_), [out], [in_])

    def reduce(self, out, in_, e="dve"):
        return self.op(e, lambda: self.eng[e].tensor_reduce(out=out, in_=in_, axis=AX.X, op=ALU.add), [out], [in_])

    def aselect(self, out, in_, pattern, cmp, fill, base, cm):
        return self.op("pool", lambda: self.nc.gpsimd.affine_select(out=out, in_=in_, pattern=pattern, compare_op=cmp, fill=fill, base=base, channel_multiplier=cm), [out], [in_])


def softplus_negabs(B, out, x, tmp1, tmp2):
    B.stt("dve", tmp1, x, -1.0, x, ALU.mult, ALU.max)
    B.act(tmp1, tmp1, AF.Exp, scale=-1.0)
    B.ts("dve", tmp2, tmp1, 2.0, ALU.add)
    B.recip(tmp2, tmp2)
    B.tt("dve", tmp1, tmp1, tmp2, ALU.mult)
    B.tt("dve", tmp2, tmp1, tmp1, ALU.mult)
    B.ts("dve", out, tmp2, 1.0 / 11.0, ALU.mult, 1.0 / 9.0, ALU.add)
    for cst in (1.0 / 7.0, 1.0 / 5.0, 1.0 / 3.0, 1.0):
        B.tt("dve", out, out, tmp2, ALU.mult)
        B.ts("dve", out, out, cst, ALU.add)
    B.tt("dve", out, out, tmp1, ALU.mult)
    B.ts("dve", out, out, 2.0, ALU.mult)


def build(depth=DEPTH, debug=False):
    B = Builder()
    nc = B.nc
    EI = "ExternalInput"
    xT_d = B.dram("xT", [8, 128, SEQ], F32, EI)
    cxT_d = B.dram("cxT", [8, 128, CTX], F32, EI)
    cols_d = B.dram("cols", [128, NCOLS], F32, EI)
    btm_d = B.dram("btm", [DEPTH, NTM], F32, EI)
    bada_d = B.dram("b_ada", [1, DEPTH * 6 * D], F32, EI)
    wada_d = B.dram("w_ada", [DEPTH, D, 6 * D], F32, EI)
    win_d = B.dram("w_in", [DEPTH, D, N_IN], F32, EI)
    wbr_d = B.dram("w_branch", [DEPTH, 3, 512, D], F32, EI)
    wout_d = B.dram("w_out", [DEPTH, D, D], F32, EI)
    wfi_d = B.dram("w_ffn_in", [DEPTH, D, 2 * FFN], F32, EI)
    wfo_d = B.dram("w_ffn_out", [DEPTH, FFN, D], F32, EI)
    lgw_d = B.dram("lru_gate_w", [DEPTH, 2, 2, 8, 64, 64], F32, EI)
    y_d = B.dram("yT", [8, 128, SEQ], F32, "ExternalOutput")
    dk = "ExternalOutput" if debug else "Internal"
    XS = [B.dram("XS0", [8, 128, SEQ], F32, dk), B.dram("XS1", [8, 128, SEQ], F32, dk)]
    HXD = B.dram("HXD", [8, 128, T], BF16, "Internal")
    PXF = B.dram("PXF", [NFM, 128, T], F32, dk)
    PXT = B.dram("PXT", [NT, 128, NTM], F32, dk)
    BR = B.dram("BR", [12, 128, T], BF16, "Internal")
    BRdbg = B.dram("BRdbg", [12, 128, T], F32, "ExternalOutput") if debug else None
    WGbf = B.dram("WGbf", [D, 3072], BF16, "Internal")
    WBbf = B.dram("WBbf", [1536, D], BF16, "Internal")
    WObf = B.dram("WObf", [D, D], BF16, "Internal")
    WFIbf = B.dram("WFIbf", [D, 2 * FFN], BF16, "Internal")
    WFObf = B.dram("WFObf", [FFN, D], BF16, "Internal")

    PS = [B.ps("P%d" % i, [128, 512], F32) for i in range(7)]
    PB = B.ps("PB", [128, 1024], BF16)
    psrot = [0]

    def P():
        psrot[0] = (psrot[0] + 1) % 7
        return PS[psrot[0]]

    CC = B.sb("CC", [128, NCOLS])
    IDF = B.sb("IDF", [128, 128])
    IDB = B.sb("IDB", [128, 128], BF16)
    ONESF = B.sb("ONESF", [128, 128])
    ONESB = B.sb("ONESB", [128, 128], BF16)
    TRIF = B.sb("TRIF", [128, 128])
    TRIB = B.sb("TRIB", [128, 128])
    MHF = B.sb("MHF", [128, 128])
    MHB = B.sb("MHB", [128, 128])
    MODC = B.sb("MODC", [128, DEPTH * 96])
    A1 = B.sb("A1", [128, DEPTH * 16])
    A2 = B.sb("A2", [128, DEPTH * 16])
    LB = B.sb("LB", [128, 32])
    OML = B.sb("OML", [128, 32])
    CST = B.sb("CST", [128, 32])
    CST2 = B.sb("CST2", [128, 32])
    BQ8 = B.sb("BQ8", [128, 8])
    CX = B.sb("CX", [128, 8, CTX])
    SMT = [B.sb("SMT%d" % i, [128, 32]) for i in range(4)]

    def col(name, i):
        o = COLS[name][0] + i
        return CC[:, o:o + 1]

    def modc(l, j, k, w):
        o = ((l * 48 + j * 8 + k) * 2 + w)
        return MODC[:, o:o + 1]

    def a1c(l, k, w):
        o = (l * 8 + k) * 2 + w
        return A1[:, o:o + 1]

    def a2c(l, k, w):
        o = (l * 8 + k) * 2 + w
        return A2[:, o:o + 1]

    B.dma("sp", CC[:], cols_d[:, :])
    B.dma("sp", CX[:], cxT_d.rearrange("k p t -> p k t"))
    B.memset("pool", ONESF[:], 1.0)
    B.memset("pool", ONESB[:], 1.0)
    B.aselect(IDF[:], ONESF[:], [[-1, 128]], ALU.is_equal, 0.0, 0, 1)
    B.copy("pool", IDB[:], IDF[:])
    B.aselect(TRIF[:], ONESF[:], [[1, 128]], ALU.is_ge, 0.0, 0, -1)
    B.aselect(TRIB[:], ONESF[:], [[-1, 128]], ALU.is_ge, 0.0, 0, 1)
    RM = B.sb("RM", [128, 4])
    BD = B.sb("BD", [128, 128])
    for q in range(4):
        B.aselect(RM[:, q:q + 1], ONESF[:, 0:1], [[0, 1]], ALU.is_ge, 0.0, -32 * q, 1)
        B.aselect(RM[:, q:q + 1], RM[:, q:q + 1], [[0, 1]], ALU.is_ge, 0.0, 32 * q + 31, -1)
        B.copy("pool", BD[:, q * 32:(q + 1) * 32], RM[:, q:q + 1].to_broadcast([128, 32]))
    MASK32 = B.sb("MASK32", [128, 32])
    B.memset("pool", MASK32[:], 1.0)
    B.memset("pool", MASK32[:, 0:1], 0.0)
    B.tt("pool", MHF[:], TRIF[:], BD[:], ALU.mult)
    B.tt("pool", MHB[:], TRIB[:], BD[:], ALU.mult)

    lo, _ = COLS["lbraw"]
    E_ = SMT[0]
    B.act(E_[:], CC[:, lo:lo + 32], AF.Exp)
    S_ = SMT[1]
    B.tt("dve", S_[:, 0:8], E_[:, 0:8], E_[:, 8:16], ALU.add)
    B.tt("dve", S_[:, 0:8], S_[:, 0:8], E_[:, 16:24], ALU.add)
    B.tt("dve", S_[:, 0:8], S_[:, 0:8], E_[:, 24:32], ALU.add)
    B.recip(S_[:, 0:8], S_[:, 0:8])
    for l in range(1, 4):
        B.tt("dve", E_[:, l * 8:(l + 1) * 8], E_[:, l * 8:(l + 1) * 8], S_[:, 0:8], ALU.mult)
    B.memset("dve", LB[:, 0:8], 0.0)
    B.copy("dve", LB[:, 8:16], E_[:, 8:16])
    B.tt("dve", LB[:, 16:24], LB[:, 8:16], E_[:, 16:24], ALU.add)
    B.tt("dve", LB[:, 24:32], LB[:, 16:24], E_[:, 24:32], ALU.add)
    B.ts("dve", OML[:], LB[:], -1.0, ALU.mult, 1.0, ALU.add)
    lo, _ = COLS["lam"]
    softplus_negabs(B, SMT[0][:], CC[:, lo:lo + 32], SMT[1][:], SMT[2][:])
    B.ts("dve", SMT[1][:], CC[:, lo:lo + 32], -1.0, ALU.mult, 0.0, ALU.max)
    B.tt("dve", SMT[0][:], SMT[0][:], SMT[1][:], ALU.add)
    B.ts("dve", CST[:], SMT[0][:], -8.0, ALU.mult)
    B.ts("dve", CST2[:], SMT[0][:], -16.0, ALU.mult)

    B.push()
    S2 = B.sb("S2", [128, 8, 2], BF16)
    BADA = B.sb("BADA", [2, DEPTH * 6 * D])
    MODR = B.sb("MODR", [2, 6 * D])
    WA = [B.sb("WA%d" % i, [128, 8, 512], BF16) for i in range(2)]
    lo, _ = COLS["c"]
    B.act(S2[:, :, 0], CC[:, lo:lo + 8], AF.Silu)
    lo, _ = COLS["cctx"]
    B.act(S2[:, :, 1], CC[:, lo:lo + 8], AF.Silu)
    B.dma("sp", BADA[:], bada_d[0].partition_broadcast(2))
    for l in range(depth):
        wv = wada_d[l].rearrange("(k p) f -> p k f", p=128)
        for fc in range(12):
            w = WA[fc % 2]
            B.dma("pool", w[:], wv[:, :, fc * 512:(fc + 1) * 512])
            p = P()
            B.mm(p[0:2, :], [(S2[:, k, :], w[:, k, :]) for k in range(8)])
            B.tt("dve", MODR[:, fc * 512:(fc + 1) * 512], p[0:2, :], BADA[:, l * 6144 + fc * 512: l * 6144 + (fc + 1) * 512], ALU.add)
        p = P()
        for f in range(48):
            B.tr(p[:, f * 2:(f + 1) * 2], MODR[0:2, f * 128:(f + 1) * 128], IDF[0:2, 0:2])
        B.copy("dve", MODC[:, l * 96:(l + 1) * 96], p[:, 0:96])
        lo1, _ = COLS["ln1"]
        lo2, _ = COLS["ln2"]
        for (AT_, jv, lo_) in ((A1, 1, lo1), (A2, 4, lo2)):
            src = MODC[:, l * 96 + jv * 16: l * 96 + (jv + 1) * 16].rearrange("p (k w) -> p k w", w=2)
            dst = AT_[:, l * 16:(l + 1) * 16].rearrange("p (k w) -> p k w", w=2)
            lnb = CC[:, lo_ + l * 8: lo_ + (l + 1) * 8].unsqueeze(2).to_broadcast([128, 8, 2])
            B.stt("dve", dst, src, 1.0, lnb, ALU.add, ALU.mult)
    B.pop()

    blocks = [(0, CTX)] + [(CTX + i * 512, 512) for i in range(8)]
    cur = None
    for l in range(depth):
        B.rotate_sems()
        last = (l == DEPTH - 1)
        perm = l >= 1
        if l == 0:
            xs_old, xs_new = xT_d, XS[0]
        else:
            xs_old, xs_new = XS[(l - 1) % 2], XS[l % 2]

        B.mark("L%d_p1" % l)
        B.push()
        HX = [B.sb("HX%d" % k, [128, T], BF16) for k in range(8)]
        B.push()
        XT = [B.sb("XT%d" % i, [128, SEQ]) for i in range(2)]
        ACC = B.sb("ACC", [128, SEQ])
        RSTD = B.sb("RSTD", [128, SEQ])
        XP = B.sb("XP", [128, SEQ])
        CACC = B.sb("CACC", [128, CTX])
        CRS = B.sb("CRS", [128, CTX])
        CT = B.sb("CT", [128, CTX])
        for k in range(8):
            xt = XT[k % 2]
            B.dma("sp", xt[:], xs_old[k], ikey=("XS", id(xs_old), k))
            if k == 0:
                B.act(ACC[:], xt[:], AF.Square)
                B.act(CACC[:], CX[:, k, :], AF.Square)
            else:
                B.act(XP[:], xt[:], AF.Square)
                B.tt("pool", ACC[:], ACC[:], XP[:], ALU.add)
                B.act(CT[:], CX[:, k, :], AF.Square)
                B.tt("dve", CACC[:], CACC[:], CT[:], ALU.add)
        for b8 in range(8):
            p = P()
            B.mm(p[:, :], [(ONESF[:], ACC[:, b8 * 512:(b8 + 1) * 512])])
            B.act(RSTD[:, b8 * 512:(b8 + 1) * 512], p[:, :], AF.Ln, scale=1.0 / D, bias=EPS)
        B.act(RSTD[:], RSTD[:], AF.Exp, scale=-0.5)
        p = P()
        B.mm(p[:, 0:CTX], [(ONESF[:], CACC[:])])
        B.act(CRS[:], p[:, 0:CTX], AF.Ln, scale=1.0 / D, bias=EPS)
        B.act(CRS[:], CRS[:], AF.Exp, scale=-0.5)
        for k in range(8):
            xt = XT[k % 2]
            B.dma("sp", xt[:], xs_old[k], ikey=("XS", id(xs_old), k))
            B.tt("dve", ACC[:], xt[:], RSTD[:], ALU.mult)
            hxo = HX[k][:, CTX:T]
            if perm:
                hxo = hxo.rearrange("p (a b) -> p b a", a=64, b=64)
                src = ACC[:].rearrange("p (a b) -> p a b", a=64)
            else:
                src = ACC[:]
            B.act(hxo, src, AF.Identity, scale=a1c(l, k, 0), bias=modc(l, 0, k, 0))
            if perm:
                B.copy("pool", XP[:].rearrange("p (a b) -> p b a", a=64, b=64), xt[:].rearrange("p (a b) -> p a b", a=64))
                B.dma("sp", xs_new[k], XP[:], okey=("XS", id(xs_new), k))
            else:
                B.dma("sp", xs_new[k], xt[:], okey=("XS", id(xs_new), k))
            B.tt("dve", CT[:], CX[:, k, :], CRS[:], ALU.mult)
            B.act(HX[k][:, 0:CTX], CT[:], AF.Identity, scale=a1c(l, k, 1), bias=modc(l, 0, k, 1))
            B.dma("sp", HXD[k], HX[k][:], okey=("HXD", k))
        B.pop()

        B.mark("L%d_p1c" % l)
        B.push()
        BT = B.sb("BT", [128, NTM])
        WT = B.sb("WT", [128, 8, NTM], BF16)
        WF = [B.sb("WF%d" % i, [128, 8, 512], BF16) for i in range(2)]
        STG = [B.sb("STG%d" % i, [128, T]) for i in range(2)]
        STT = [B.sb("STT%d" % i, [128, NTM]) for i in range(2)]
        wv = win_d[l].rearrange("(k p) f -> p k f", p=128)
        B.dma("sp", BT[:], btm_d[l].partition_broadcast(128))
        B.dma("pool", WT[:, :, 0:512], wv[:, :, 512:1024])
        B.dma("pool", WT[:, :, 512:1024], wv[:, :, 3072:3584])
        B.dma("pool", WT[:, :, 1024:1280], wv[:, :, 2816:3072])
        B.dma("pool", WT[:, :, 1280:1296], wv[:, :, 4096:4112])
        lo, _ = COLS["bfm"]
        B.ts("dve", BQ8[:, 0:2], CC[:, lo + l * 32 + 16: lo + l * 32 + 18], 0.125, ALU.mult)
        for g in range(8):
            w = WF[g % 2]
            B.dma("pool", w[:], wv[:, :, FM_GROUP_COL[g]:FM_GROUP_COL[g] + 512])
            for q in range(4):
                ft = g * 4 + q
                stg = STG[ft % 2]
                bias = col("bfm", l * 32 + ft)
                scale = None
                if ft < 8:
                    func = AF.Silu
                elif 20 <= ft < 24:
                    func = AF.Sigmoid
                else:
                    func = AF.Identity
                if ft in (16, 17):
                    scale = 0.125
                    bias = BQ8[:, ft - 16:ft - 15]
                for (c0, n) in blocks:
                    p = P()
                    B.mm(p[:, 0:n], [(w[:, k, q * 128:(q + 1) * 128], HX[k][:, c0:c0 + n]) for k in range(8)])
                    B.act(stg[:, c0:c0 + n], p[:, 0:n], func, bias=bias, scale=scale)
                B.dma("sp", PXF[ft], stg[:], okey=("PXF", ft))
        for i in range(NT):
            sl = slice(i * 128, (i + 1) * 128)
            pa, pb, pc = P(), P(), P()
            B.mm_multi([
                (pa[:, :], [(HX[k][:, sl], WT[:, k, 0:512]) for k in range(8)]),
                (pb[:, :], [(HX[k][:, sl], WT[:, k, 512:1024]) for k in range(8)]),
                (pc[:, 0:272], [(HX[k][:, sl], WT[:, k, 1024:1296]) for k in range(8)]),
            ])
            st = STT[i % 2]
            B.tt("dve", st[:, 0:512], pa[:, :], BT[:, 0:512], ALU.add)
            B.tt("dve", st[:, 512:1024], pb[:, :], BT[:, 512:1024], ALU.add)
            B.tt("dve", st[:, 1024:1296], pc[:, 0:272], BT[:, 1024:1296], ALU.add)
            B.dma("sp", PXT[i], st[:], okey=("PXT", i))
        B.pop()
        B.pop()

        for r4 in range(4):
            rs = slice(r4 * 256, (r4 + 1) * 256)
            B.dma("pool", WGbf[rs, :], win_d[l][rs, GATE_COL:GATE_COL + 3072], okey=("WGbf", r4))
            B.dma("pool", WObf[rs, :], wout_d[l][rs, :], okey=("WObf", r4))
            B.dma("pool", WFIbf[rs, :], wfi_d[l][rs, :], okey=("WFIbf", r4))
        wb2 = wbr_d[l].rearrange("n r f -> (n r) f")
        for r4 in range(3):
            B.dma("pool", WBbf[r4 * 512:(r4 + 1) * 512, :], wb2[r4 * 512:(r4 + 1) * 512, :], okey=("WBbf", r4))
        for r4 in range(4):
            rs = slice(r4 * 704, (r4 + 1) * 704)
            B.dma("pool", WFObf[rs, :], wfo_d[l][rs, :], okey=("WFObf", r4))
        PXT_ALL = [("PXT", i) for i in range(NT)]
        pxt_v = PXT.rearrange("i p c -> p i c")

        B.mark("L%d_lru" % l)
        HT = T // 2
        HB = [(0, HT), (HT, T)]
        SEGS = ((0, CTX), (CTX, T))
        B.push()
        LXs = [B.sb("LX%d" % i, [128, T]) for i in range(2)]
        GWs = [B.sb("GW%d" % i, [128, 4, 128]) for i in range(2)]
        LYs = [[B.sb("LY%d_%d" % (r, i), [128, HT]) for i in range(2)] for r in range(2)]
        XC = [B.sb("XC%d" % i, [128, HT]) for i in range(2)]
        HL = [B.sb("HL%d" % i, [128, HT]) for i in range(2)]
        RR = [B.sb("RR%d" % i, [128, HT]) for i in range(2)]
        II = [B.sb("II%d" % i, [128, HT]) for i in range(2)]
        AA = [B.sb("AA%d" % i, [128, HT]) for i in range(2)]
        OB = [B.sb("OB%d" % i, [128, HT], BF16) for i in range(2)]
        for r in range(2):
            B.memset("pool", GWs[r][:], 0.0)

        def load_lru(j):
            B.dma("sp", LXs[j % 2][:], PXF[24 + j], ikey=("PXF", 24 + j))
            for hs, (r0, r1) in enumerate(HB):
                B.dma("sp", LYs[j % 2][hs][:], PXF[28 + j][:, r0:r1], ikey=("PXF", 28 + j))
            for z in range(2):
                for g in range(2):
                    for h2 in range(2):
                        B.dma("sp", GWs[j % 2][h2 * 64:(h2 + 1) * 64, z * 2 + g, h2 * 64:(h2 + 1) * 64], lgw_d[l, z, g, 2 * j + h2])

        load_lru(0)
        for j in range(4):
            if j + 1 < 4:
                load_lru(j + 1)
            LX, GW, LY = LXs[j % 2], GWs[j % 2], LYs[j % 2]
            cw = lambda tap: col("convw", l * 16 + tap * 4 + j)
            for hs, (r0, r1) in enumerate(HB):
                B.ts("dve", XC[hs][:], LX[:, r0:r1], cw(2), ALU.mult, col("convb", l * 4 + j), ALU.add)
            for tap, o in ((0, -2), (1, -1), (3, 1)):
                for hs, (r0, r1) in enumerate(HB):
                    for (s0, s1) in SEGS:
                        d0 = max(r0, s0 + max(0, -o))
                        d1 = min(r1, s1 - max(0, o))
                        if d1 > d0:
                            B.stt("dve", XC[hs][:, d0 - r0:d1 - r0], LX[:, d0 + o:d1 + o], cw(tap), XC[hs][:, d0 - r0:d1 - r0], ALU.mult, ALU.add)
            for z in range(2):
                for hs, (r0, r1) in enumerate(HB):
                    c = 0
                    while c < HT:
                        n = min(512, HT - c)
                        p = P()
                        B.mm(p[:, 0:n], [(GW[:, z * 2 + 0, :], XC[hs][:, c:c + n])])
                        B.act(RR[hs][:, c:c + n], p[:, 0:n], AF.Sigmoid, bias=col("lgb", l * 16 + (z * 2 + 0) * 4 + j))
                        p = P()
                        B.mm(p[:, 0:n], [(GW[:, z * 2 + 1, :], XC[hs][:, c:c + n])])
                        B.act(II[hs][:, c:c + n], p[:, 0:n], AF.Sigmoid, bias=col("lgb", l * 16 + (z * 2 + 1) * 4 + j))
                        c += n
                ci = l * 8 + z * 4 + j
                for hs in range(2):
                    B.act(AA[hs][:], RR[hs][:], AF.Exp, scale=CST[:, ci:ci + 1])
                for hs in range(2):
                    B.act(RR[hs][:], RR[hs][:], AF.Exp, scale=CST2[:, ci:ci + 1])
                for hs in range(2):
                    B.act(RR[hs][:], RR[hs][:], AF.Sqrt, scale=-1.0, bias=1.0)
                for hs in range(2):
                    B.tt("dve", II[hs][:], II[hs][:], RR[hs][:], ALU.mult)
                for hs in range(2):
                    B.tt("dve" if hs == 0 else "pool", II[hs][:], II[hs][:], XC[hs][:], ALU.mult)
                if z == 0:
                    B.scan(RR[0][:, 0:CTX], AA[0][:, 0:CTX], II[0][:, 0:CTX], 0.0)
                    B.scan(RR[0][:, CTX:HT], AA[0][:, CTX:HT], II[0][:, CTX:HT], RR[0][:, CTX - 1:CTX])
                    B.scan(RR[1][:], AA[1][:], II[1][:], RR[0][:, HT - 1:HT])
                    for hs in range(2):
                        B.copy("act", HL[hs][:], RR[hs][:])
                else:
                    B.scan(RR[0][:, 0:CTX][:, ::-1], AA[0][:, 0:CTX][:, ::-1], II[0][:, 0:CTX][:, ::-1], 0.0)
                    B.scan(RR[1][:, ::-1], AA[1][:, ::-1], II[1][:, ::-1], RR[0][:, 0:1])
                    B.scan(RR[0][:, CTX:HT][:, ::-1], AA[0][:, CTX:HT][:, ::-1], II[0][:, CTX:HT][:, ::-1], RR[1][:, 0:1])
                    for hs in range(2):
                        B.tt("dve", HL[hs][:], HL[hs][:], RR[hs][:], ALU.add)
            for hs in range(2):
                B.act(AA[hs][:], LY[hs][:], AF.Square)
            for hs in range(2):
                B.ts("dve", AA[hs][:], AA[hs][:], 0.044715 * 1.5957691216, ALU.mult, 1.5957691216, ALU.add)
                B.tt("dve", AA[hs][:], AA[hs][:], LY[hs][:], ALU.mult)
            for hs in range(2):
                B.act(AA[hs][:], AA[hs][:], AF.Sigmoid)
            for hs, (r0, r1) in enumerate(HB):
                B.tt("dve" if hs == 0 else "pool", AA[hs][:], AA[hs][:], LY[hs][:], ALU.mult)
                B.tt("dve", OB[hs][:], AA[hs][:], HL[hs][:], ALU.mult)
                B.dma("sp", BR[8 + j][:, r0:r1], OB[hs][:], okey=("BR", 8 + j))
                if debug:
                    B.tt("dve", II[hs][:], AA[hs][:], HL[hs][:], ALU.mult)
                    B.dma("sp", BRdbg[8 + j][:, r0:r1], II[hs][:])
        B.pop()

        B.mark("L%d_mlstm" % l)
        B.push()
        GT = B.sb("GT", [128, NT, 16])
        LF = B.sb("LF", [128, NT, 8])
        BB = B.sb("BBm", [128, NT, 8])
        BTOT = B.sb("BTOT", [128, NT, 8])
        WP = B.sb("WPm", [128, NT, 8])
        WS = B.sb("WSm", [128, NT, 8])
        EN = B.sb("ENm", [128, NT, 8])
        EBT = B.sb("EBT", [128, NT, 8])
        TM1 = B.sb("TM1", [128, NT, 8])
        TM2 = B.sb("TM2", [128, NT, 8])
        B.dma("sp", GT[:], pxt_v[:, :, 1280:1296], ikey=PXT_ALL)
        softplus_negabs(B, LF[:], GT[:, :, 8:16], TM1[:], TM2[:])
        B.ts("dve", TM1[:], GT[:, :, 8:16], 0.0, ALU.min)
        B.tt("dve", LF[:], TM1[:], LF[:], ALU.subtract)
        p = P()
        pv = p[:, 0:NT * 8].rearrange("p (i c) -> p i c", c=8)
        for i in range(NT):
            B.mm(pv[:, i, 0:4], [(TRIF[:], LF[:, i, 0:4])])
            B.mm(pv[:, i, 4:8], [(TRIB[:], LF[:, i, 4:8])])
        B.copy("dve", BB[:], pv)
        p = P()
        B.mm(p[:, 0:NT * 8], [(ONESF[:], LF[:].rearrange("p i c -> p (i c)"))])
        B.copy("dve", BTOT[:], p[:, 0:NT * 8].rearrange("p (i c) -> p i c", c=8))
        B.tt("dve", TM1[:], GT[:, :, 0:8], BB[:], ALU.subtract)
        B.act(WP[:], TM1[:], AF.Exp)
        B.tt("dve", TM1[:], TM1[:], BTOT[:], ALU.add)
        B.act(WS[:], TM1[:], AF.Exp)
        B.act(EN[:], BB[:], AF.Exp, scale=-1.0)
        B.act(EBT[:], BTOT[:], AF.Exp)
        ord_f = list(range(NT))
        ord_b = [1, 0] + list(range(NT - 1, 1, -1))
        HH = B.sb("HH", [128, NT, 132])
        TMPH = B.sb("TMPH", [128, NT, 132])
        KWA = [B.sb("KWA%d" % z, [128, NT, 64], BF16) for z in range(2)]
        DEN = [B.sb("DENm%d" % z, [128, NT]) for z in range(2)]
        OBm = B.sb("OBm", [128, T], BF16)
        CS = [[B.sb("CSm%d_%d" % (z, r), [64, 132]) for r in range(4)] for z in range(2)]
        CB = [[B.sb("CBm%d_%d" % (z, r), [64, 132], BF16) for r in range(4)] for z in range(2)]
        PT = [[B.sb("PTm%d_%d" % (z, r), [128, 128], BF16) for r in range(3)] for z in range(2)]
        KW = [[B.sb("KWm%d_%d" % (z, r), [128, 64], BF16) for r in range(3)] for z in range(2)]
        RC = [[B.sb("RCm%d_%d" % (z, r), [128, 2]) for r in range(3)] for z in range(2)]
        SS = B.sb("SSm", [128, NT])
        MIN_ = [dict(QT=B.sb("QTm%d" % r, [64, T], BF16), KT=B.sb("KTm%d" % r, [64, T], BF16),
                     KTK=B.sb("KTK%d" % r, [128, NT, 64]), VA=B.sb("VA%d" % r, [128, NT, 132], BF16),
                     SOG=B.sb("SOG%d" % r, [128, T])) for r in range(2)]
        for r in range(2):
            B.memset("pool", MIN_[r]["VA"][:, :, 128:132], 1.0)

        def load_head(h):
            m = MIN_[h % 2]
            r0 = (h % 2) * 64
            B.dma("pool", m["QT"][:], PXF[16 + h // 2][r0:r0 + 64, :], ikey=("PXF", 16 + h // 2))
            B.dma("pool", m["KT"][:], PXF[18 + h // 2][r0:r0 + 64, :], ikey=("PXF", 18 + h // 2))
            B.dma("sp", m["KTK"][:], pxt_v[:, :, 1024 + h * 64:1024 + (h + 1) * 64], ikey=PXT_ALL)
            B.dma("pool", m["VA"][:, :, 0:128], pxt_v[:, :, 512 + h * 128:512 + (h + 1) * 128], ikey=PXT_ALL)
            B.dma("sp", m["SOG"][:], PXF[20 + h], ikey=("PXF", 20 + h))

        load_head(0)
        for h in range(4):
            if h + 1 < 4:
                load_head(h + 1)
            m = MIN_[h % 2]
            QT, KT, KTK, VA, SOG = m["QT"], m["KT"], m["KTK"], m["VA"], m["SOG"]
            for z in range(2):
                B.memset("pool", CS[z][0][:], 0.0)
                B.memset("pool", CB[z][0][:], 0.0)
            def emit_pt(step, z):
                i = (ord_f if z == 0 else ord_b)[step]
                sl = slice(i * 128, (i + 1) * 128)
                zh = z * 4 + h
                p = P()
                B.mm(p[:, 0:128], [(KT[:, sl], QT[:, sl])])
                B.stt("dve", PT[z][step % 3][:], p[:, 0:128], WP[:, i, zh:zh + 1], (TRIF if z == 0 else TRIB)[:], ALU.mult, ALU.mult)

            for z in range(2):
                zh = z * 4 + h
                B.tt("dve", KWA[z][:], KTK[:], WS[:, :, zh].unsqueeze(2).to_broadcast([128, NT, 64]), ALU.mult)
                emit_pt(0, z)
            for step in range(NT):
                for z in range(2):
                    i = (ord_f if z == 0 else ord_b)[step]
                    sl = slice(i * 128, (i + 1) * 128)
                    zh = z * 4 + h
                    pd = P()
                    B.mm(pd[0:64, 0:132], [(KWA[z][:, i, :], VA[:, i, :])])
                    B.stt("dve", CS[z][(step + 1) % 4][:], CS[z][step % 4][:], EBT[0:64, i, zh:zh + 1], pd[0:64, 0:132], ALU.mult, ALU.add)
                    B.copy("act", CB[z][(step + 1) % 4][:], CS[z][(step + 1) % 4][:])
                    po = P()
                    B.mm(po[:, 0:132], [(PT[z][step % 3][:], VA[:, i, :]), (QT[:, sl], CB[z][step % 4][:])])
                    B.copy("act", (HH if z == 0 else TMPH)[:, i, :], po[:, 0:132])
                    if step + 1 < NT:
                        emit_pt(step + 1, z)
            for z in range(2):
                zh = z * 4 + h
                Hz = HH if z == 0 else TMPH
                B.stt("dve", DEN[z][:], Hz[:, :, 128], -1.0, Hz[:, :, 128], ALU.mult, ALU.max)
                B.tt("dve", DEN[z][:], DEN[z][:], EN[:, :, zh], ALU.max)
                B.act(DEN[z][:], DEN[z][:], AF.Ln)
                B.act(DEN[z][:], DEN[z][:], AF.Exp, scale=-1.0)
                B.tt("dve", Hz[:, :, 0:128], Hz[:, :, 0:128], DEN[z][:].unsqueeze(2).to_broadcast([128, NT, 128]), ALU.mult)
            B.tt("dve", HH[:, :, 0:128], HH[:, :, 0:128], TMPH[:, :, 0:128], ALU.add)
            B.tt("pool", TMPH[:, :, 0:128], HH[:, :, 0:128], HH[:, :, 0:128], ALU.mult)
            B.reduce(SS[:], TMPH[:, :, 0:128])
            B.act(SS[:], SS[:], AF.Ln, scale=1.0 / 128, bias=EPS)
            B.act(SS[:], SS[:], AF.Exp, scale=-0.5)
            B.tt("dve", HH[:, :, 0:128], HH[:, :, 0:128], SS[:].unsqueeze(2).to_broadcast([128, NT, 128]), ALU.mult)
            for i4 in range(0, NT, 4):
                p = P()
                nn = min(4, NT - i4)
                for u in range(nn):
                    B.tr(p[:, u * 128:(u + 1) * 128], HH[:, i4 + u, 0:128], IDF[:])
                B.stt("dve", OBm[:, i4 * 128:(i4 + nn) * 128], p[:, 0:nn * 128], col("mln", l * 4 + h), SOG[:, i4 * 128:(i4 + nn) * 128], ALU.mult, ALU.mult)
            B.dma("sp", BR[4 + h], OBm[:], okey=("BR", 4 + h))
            if debug:
                B.copy("dve", TMPH[:].rearrange("p i c -> p (i c)")[:, 0:T], OBm[:])
                B.dma("sp", BRdbg[4 + h], TMPH[:].rearrange("p i c -> p (i c)")[:, 0:T])
        B.pop()

        B.mark("L%d_hgrn" % l)
        NCH = T // 32
        B.push()
        RMASK = B.sb("RMASK", [128, T])
        B.memset("pool", RMASK[:], 1.0)
        B.memset("pool", RMASK[:].rearrange("p (c t) -> p c t", t=32)[:, :, 0], 0.0)
        for h in range(4):
            B.push()
            QS = B.sb("QS", [128, T])
            VT = B.sb("VTh", [128, NT, 128], BF16)
            OA = B.sb("OA", [128, T])
            FFh = [B.sb("FF%d" % i, [128, T // 2]) for i in range(2)]
            KKh = [B.sb("KK%d" % i, [128, T // 2]) for i in range(2)]
            EEh = [B.sb("EE%d" % i, [128, T // 2]) for i in range(2)]
            QT2 = B.sb("QT2", [128, T], BF16)
            KT2 = B.sb("KT2", [128, T], BF16)
            KTO = B.sb("KTO", [128, NT, 128], BF16)
            BMID = B.sb("BMID", [128, NCH])
            BLS = B.sb("BLS", [128, NCH])
            EM = B.sb("EMh", [128, NCH])
            EL2 = B.sb("EL2", [128, NCH])
            ELM = B.sb("ELM", [128, NCH])
            RING = 6
            SR = [B.sb("SR%d" % r, [128, 128]) for r in range(RING)]
            PDS = [None, None, None]
            SPR = [B.sb("SPR%d" % r, [128, 128], BF16) for r in range(RING)]
            ATR = [B.sb("ATR%d" % r, [128, 128], BF16) for r in range(3)]
            cidx = 0
            B.dma("sp", QS[:], PXF[h], ikey=("PXF", h))
            B.dma("pool", VT[:], pxt_v[:, :, h * 128:(h + 1) * 128], ikey=PXT_ALL)
            VTM = [B.sb("VTM%d" % q, [128, NT, 128], BF16) for q in range(4)]
            for q in range(4):
                B.ts("dve", VTM[q][:], VT[:], RM[:, q:q + 1], ALU.mult)
            for z in range(2):
                zh = l * 8 + z * 4 + h
                mid = 15 if z == 0 else 16
                lastp = 31 if z == 0 else 0
                HT = T // 2
                NCH2 = NCH // 2

                def steps(hs):
                    r = slice(hs * HT, (hs + 1) * HT)
                    cr = slice(hs * NCH2, (hs + 1) * NCH2)
                    F_, K_, E_ = FFh[hs], KKh[hs], EEh[hs]
                    ev = E_[:].rearrange("p (c t) -> p c t", t=32)
                    fv = F_[:].rearrange("p (c t) -> p c t", t=32)
                    KT3 = E_[:].bitcast(BF16)[:, 0:HT]
                    yield lambda: B.dma("sp", F_[:], PXF[8 + z * 4 + h][:, r], ikey=("PXF", 8 + z * 4 + h))
                    yield lambda: B.act(F_[:], F_[:], AF.Sigmoid)
                    yield lambda: B.ts("dve", F_[:], F_[:], OML[:, zh:zh + 1], ALU.mult, LB[:, zh:zh + 1], ALU.add)
                    yield lambda: B.act(K_[:], F_[:], AF.Identity, scale=-1.0, bias=1.0)
                    yield lambda: B.act(F_[:], F_[:], AF.Ln)
                    yield lambda: B.scan(E_[:], RMASK[:, r], F_[:], 0.0)
                    if z == 1:
                        yield lambda: B.copy("dve", BLS[:, cr], ev[:, :, 31])
                        yield lambda: B.tt("dve", ev, BLS[:, cr].unsqueeze(2).to_broadcast([128, NCH2, 32]), ev, ALU.subtract)
                        yield lambda: B.tt("dve", E_[:], E_[:], F_[:], ALU.add)
                    yield lambda: B.copy("pool", BMID[:, cr], ev[:, :, mid])
                    yield lambda: B.copy("pool", BLS[:, cr], ev[:, :, lastp])
                    yield lambda: B.act(EM[:, cr], BMID[:, cr], AF.Exp)
                    yield lambda: B.act(EL2[:, cr], BLS[:, cr], AF.Exp)
                    yield lambda: B.tt("pool", BLS[:, cr], BLS[:, cr], BMID[:, cr], ALU.subtract)
                    yield lambda: B.act(ELM[:, cr], BLS[:, cr], AF.Exp)
                    yield lambda: B.tt("dve", ev, ev, BMID[:, cr].unsqueeze(2).to_broadcast([128, NCH2, 32]), ALU.subtract)
                    yield lambda: B.act(F_[:], E_[:], AF.Exp)
                    yield lambda: B.tt("dve", QT2[:, r], QS[:, r], F_[:], ALU.mult)
                    yield lambda: B.act(F_[:], E_[:], AF.Exp, scale=-1.0)
                    yield lambda: B.tt("dve", KT2[:, r], K_[:], F_[:], ALU.mult)
                    yield lambda: B.tt("dve", fv, fv, ELM[:, cr].unsqueeze(2).to_broadcast([128, NCH2, 32]), ALU.mult)
                    yield lambda: B.tt("dve", KT3, K_[:], F_[:], ALU.mult)
                    nth = NT // 2
                    for i4 in range(0, nth, 4):
                        nn = min(4, nth - i4)

                        def trs(i4=i4, nn=nn):
                            for u in range(nn):
                                B.tr(PB[:, u * 128:(u + 1) * 128], KT3[:, (i4 + u) * 128:(i4 + u + 1) * 128], IDB[:])
                            B.copy("act", KTO[:, hs * nth + i4:hs * nth + i4 + nn, :], PB[:, 0:nn * 128].rearrange("p (u k) -> p u k", k=128))
                        yield trs

                gens = [steps(0), steps(1)]
                live = [True, True]
                while any(live):
                    for hs in range(2):
                        if live[hs]:
                            try:
                                next(gens[hs])()
                            except StopIteration:
                                live[hs] = False
                B.memset("pool", SR[0][:], 0.0)
                order = ord_f if z == 0 else ord_b
                cidx = 0

                PAS = [None, None, None]

                def emit_pre_pe(ti):
                    i = order[ti]
                    sl = slice(i * 128, (i + 1) * 128)
                    pa = P()
                    B.mm(pa[:, 0:128], [(KT2[:, sl], QT2[:, sl])])
                    PAS[ti % 3] = pa
                    pd = P()
                    B.mm_multi([(pd[:, hf * 128:(hf + 1) * 128], [(KTO[:, i, :], VTM[hf][:, i, :])]) for hf in range(4)])
                    PDS[ti % 3] = pd

                def emit_mask(ti):
                    B.tt("dve", ATR[ti % 3][:], PAS[ti % 3][:, 0:128], (MHF if z == 0 else MHB)[:], ALU.mult)

                emit_pre_pe(0)
                emit_mask(0)
                pending = None
                for ti, i in enumerate(order):
                    sl = slice(i * 128, (i + 1) * 128)
                    if ti + 1 < NT:
                        emit_pre_pe(ti + 1)
                    ATb = ATR[ti % 3]
                    po = P()
                    halves = (0, 1, 2, 3) if z == 0 else (3, 2, 1, 0)
                    for idx, hf in enumerate(halves):
                        c = 4 * i + hf
                        lo = hf * 32
                        scur, snext = SR[cidx % RING], SR[(cidx + 1) % RING]
                        spb = SPR[cidx % RING]
                        B.stt("dve", snext[:], scur[:], EL2[:, c:c + 1], PDS[ti % 3][:, hf * 128:(hf + 1) * 128], ALU.mult, ALU.add)
                        if idx == 0 and ti + 1 < NT:
                            emit_mask(ti + 1)
                        if idx == 1 and pending is not None:
                            B.tt("dve", OA[:, pending[0]], OA[:, pending[0]], pending[1][:, 0:128], ALU.add)
                            pending = None
                        B.act(spb[:], scur[:], AF.Identity, scale=EM[:, c:c + 1])
                        B.mm(po[:, lo:lo + 32], [(VT[:, i, :], ATb[:, lo:lo + 32]), (spb[:], QT2[:, c * 32:(c + 1) * 32])])
                        cidx += 1
                    if z == 0:
                        B.copy("act", OA[:, sl], po[:, 0:128])
                    else:
                        pending = (sl, po)
                if pending is not None:
                    B.tt("dve", OA[:, pending[0]], OA[:, pending[0]], pending[1][:, 0:128], ALU.add)
                    pending = None
            HT = T // 2
            for hs in range(2):
                r0 = hs * HT
                B.act(EEh[hs][:], OA[:, r0:r0 + HT], AF.Square)
                B.dma("sp", KKh[hs][:], PXF[4 + h][:, r0:r0 + HT], ikey=("PXF", 4 + h))
            for hs in range(2):
                r0 = hs * HT
                c = 0
                while c < HT:
                    n = min(512, HT - c)
                    p = P()
                    B.mm(p[:, 0:n], [(ONESF[:], EEh[hs][:, c:c + n])])
                    B.act(FFh[hs][:, c:c + n], p[:, 0:n], AF.Ln, scale=1.0 / 128, bias=EPS)
                    c += n
                B.act(FFh[hs][:], FFh[hs][:], AF.Exp, scale=-0.5)
            OBh = QT2
            for hs in range(2):
                r0 = hs * HT
                B.tt("dve", OA[:, r0:r0 + HT], OA[:, r0:r0 + HT], FFh[hs][:], ALU.mult)
                B.stt("dve", OBh[:, r0:r0 + HT], OA[:, r0:r0 + HT], col("hgn", l * 4 + h), KKh[hs][:], ALU.mult, ALU.mult)
            B.dma("sp", BR[h], OBh[:], okey=("BR", h))
            if debug:
                B.copy("dve", OA[:], OBh[:])
                B.dma("sp", BRdbg[h], OA[:])
            B.pop()
        B.pop()

        B.mark("L%d_p3" % l)
        xsv = xs_new.rearrange("k p t -> p k t")
        hxv = HXD.rearrange("k p t -> p k t")
        brv = BR.rearrange("k p t -> p k t")
        yv = y_d.rearrange("k p t -> p k t")
        XSK = [("XS", id(xs_new), k) for k in range(8)]
        wv = win_d[l].rearrange("(k p) f -> p k f", p=128)
        B.push()
        WGr = B.sb("WGr", [128, 8, 3072], BF16)
        WBr = B.sb("WBr", [128, 12, 1024], BF16)
        WOr = B.sb("WOr", [128, 8, 1024], BF16)
        XBs = [B.sb("XB%d" % i, [128, 8, 512]) for i in range(2)]
        HXBs = [B.sb("HXB%d" % i, [128, 8, 512], BF16) for i in range(2)]
        BRBs = [B.sb("BRB%d" % i, [128, 12, 512], BF16) for i in range(2)]
        MG = B.sb("MG", [128, 8, 512], BF16)
        GS = [B.sb("GS%d" % i, [128, 512]) for i in range(3)]
        TMP3 = [B.sb("TMP3%d" % i, [128, 512]) for i in range(2)]
        ACC3 = B.sb("ACC3", [128, 512])
        wgv = WGbf.rearrange("(k p) f -> p k f", p=128)
        wbv = WBbf.rearrange("(q p) f -> p q f", p=128)
        wov = WObf.rearrange("(k p) f -> p k f", p=128)
        K4 = lambda nm: [(nm, r4) for r4 in range(4)]
        for nb in range(3):
            for hh in range(2):
                B.dma("sp", WGr[:, hh * 4:(hh + 1) * 4, nb * 1024:(nb + 1) * 1024],
                      wgv[:, hh * 4:(hh + 1) * 4, nb * 1024:(nb + 1) * 1024], ikey=K4("WGbf"))
            B.dma("sp", WBr[:, nb * 4:(nb + 1) * 4, :], wbv[:, nb * 4:(nb + 1) * 4, :], ikey=[("WBbf", r4) for r4 in range(3)])
        for hh in range(2):
            B.dma("sp", WOr[:, hh * 4:(hh + 1) * 4, :], wov[:, hh * 4:(hh + 1) * 4, :], ikey=K4("WObf"))
        blist = [(bi, c0, n) for bi, (c0, n) in enumerate(blocks) if not (bi == 0 and last)]

        def load3a(bi, c0, n):
            if bi != 0:
                B.dma("sp", XBs[bi % 2][:], xsv[:, :, c0 - CTX:c0 - CTX + n], ikey=XSK)
            B.dma("sp", HXBs[bi % 2][:, :, 0:n], hxv[:, :, c0:c0 + n], ikey=[("HXD", k) for k in range(8)])
            B.dma("sp", BRBs[bi % 2][:, :, 0:n], brv[:, :, c0:c0 + n], ikey=[("BR", k) for k in range(12)])

        load3a(*blist[0])
        for bpos, (bi, c0, n) in enumerate(blist):
            isctx = bi == 0
            if bpos + 1 < len(blist):
                load3a(*blist[bpos + 1])
            w_ = 1 if isctx else 0
            HXB, BRB = HXBs[bi % 2], BRBs[bi % 2]
            xb = CX if isctx else XBs[bi % 2]
            for j in range(8):
                js = slice(j * 128, (j + 1) * 128)
                for nb in range(3):
                    pg = P()
                    B.mm(pg[:, 0:n], [(WGr[:, k, nb * 1024 + j * 128: nb * 1024 + (j + 1) * 128], HXB[:, k, 0:n]) for k in range(8)])
                    gs = GS[nb]
                    B.act(gs[:, 0:n], pg[:, 0:n], AF.Sigmoid, bias=col("bgate", l * 24 + nb * 8 + j))
                    pb = P()
                    B.mm(pb[:, 0:n], [(WBr[:, nb * 4 + k, js], BRB[:, nb * 4 + k, 0:n]) for k in range(4)])
                    if nb == 0:
                        B.tt("dve", ACC3[:, 0:n], gs[:, 0:n], pb[:, 0:n], ALU.mult)
                    elif nb == 1:
                        B.tt("dve", TMP3[0][:, 0:n], gs[:, 0:n], pb[:, 0:n], ALU.mult)
                        B.tt("pool", ACC3[:, 0:n], ACC3[:, 0:n], TMP3[0][:, 0:n], ALU.add)
                    else:
                        B.tt("dve", TMP3[1][:, 0:n], gs[:, 0:n], pb[:, 0:n], ALU.mult)
                        B.tt("pool", MG[:, j, 0:n], ACC3[:, 0:n], TMP3[1][:, 0:n], ALU.add)
            for j in range(8):
                js = slice(j * 128, (j + 1) * 128)
                py = P()
                B.mm(py[:, 0:n], [(WOr[:, k, js], MG[:, k, 0:n]) for k in range(8)])
                B.stt("dve", xb[:, j, 0:n], py[:, 0:n], modc(l, 2, j, w_), xb[:, j, 0:n], ALU.mult, ALU.add)
            if not isctx:
                B.dma("sp", xsv[:, :, c0 - CTX:c0 - CTX + n], xb[:], okey=XSK)
            for k in range(8):
                B.act(MG[:, k, 0:n], xb[:, k, 0:n], AF.Square)
            pss = P()
            B.mm(pss[:, 0:n], [(ONESB[:], MG[:, k, 0:n]) for k in range(8)])
            B.act(ACC3[:, 0:n], pss[:, 0:n], AF.Ln, scale=1.0 / D, bias=EPS)
            B.act(ACC3[:, 0:n], ACC3[:, 0:n], AF.Exp, scale=-0.5)
            for k in range(8):
                tm = TMP3[k % 2]
                B.tt("dve", tm[:, 0:n], xb[:, k, 0:n], ACC3[:, 0:n], ALU.mult)
                B.act(HXB[:, k, 0:n], tm[:, 0:n], AF.Identity, scale=a2c(l, k, w_), bias=modc(l, 3, k, w_))
            B.dma("sp", hxv[:, :, c0:c0 + n], HXB[:, :, 0:n], okey=[("HXD", k) for k in range(8)], ikey=HXB[:])
        B.pop()
        B.mark("L%d_p3b" % l)
        B.push()
        WFIr = B.sb("WFIr", [128, 8, 2 * FFN], BF16)
        WFOr = B.sb("WFOr", [128, 22, 1024], BF16)
        NB2 = 384
        XB2 = [B.sb("XB2%d" % i, [128, 8, NB2]) for i in range(2)]
        H2s = [B.sb("H2%d" % i, [128, 8, NB2], BF16) for i in range(2)]
        ACTB = B.sb("ACTB", [128, 22, NB2], BF16)
        SQB = ACTB
        GS2 = [B.sb("GS2%d" % i, [128, NB2]) for i in range(3)]
        RS = B.sb("RS", [128, NB2])
        wfiv = WFIbf.rearrange("(k p) f -> p k f", p=128)
        wfov = WFObf.rearrange("(m p) f -> p m f", p=128)
        for pc_ in range(11):
            B.dma("sp", WFIr[:, :, pc_ * 512:(pc_ + 1) * 512], wfiv[:, :, pc_ * 512:(pc_ + 1) * 512], ikey=K4("WFIbf"))
        B.dma("sp", WFOr[:, 0:11, :], wfov[:, 0:11, :], ikey=K4("WFObf"))
        B.dma("sp", WFOr[:, 11:22, :], wfov[:, 11:22, :], ikey=K4("WFObf"))
        blocks2 = [(0, CTX)] + [(CTX + i * NB2, NB2) for i in range(SEQ // NB2)]
        if SEQ % NB2:
            blocks2.append((CTX + (SEQ // NB2) * NB2, SEQ % NB2))
        blist2 = [(bi, c0, n) for bi, (c0, n) in enumerate(blocks2) if not (bi == 0 and last)]

        def load3b(bi, c0, n):
            if bi != 0:
                B.dma("sp", XB2[bi % 2][:, :, 0:n], xsv[:, :, c0 - CTX:c0 - CTX + n], ikey=XSK)
            B.dma("sp", H2s[bi % 2][:, :, 0:n], hxv[:, :, c0:c0 + n], ikey=[("HXD", k) for k in range(8)])

        load3b(*blist2[0])
        for bpos, (bi, c0, n) in enumerate(blist2):
            isctx = bi == 0
            if bpos + 1 < len(blist2):
                load3b(*blist2[bpos + 1])
            w_ = 1 if isctx else 0
            xb = CX if isctx else XB2[bi % 2]
            H2 = H2s[bi % 2]
            for m in range(22):
                pg = P()
                B.mm(pg[:, 0:n], [(WFIr[:, k, m * 128:(m + 1) * 128], H2[:, k, 0:n]) for k in range(8)])
                pu = P()
                B.mm(pu[:, 0:n], [(WFIr[:, k, FFN + m * 128:FFN + (m + 1) * 128], H2[:, k, 0:n]) for k in range(8)])
                gs = GS2[m % 3]
                B.act(gs[:, 0:n], pg[:, 0:n], AF.Silu)
                B.tt("dve", ACTB[:, m, 0:n], gs[:, 0:n], pu[:, 0:n], ALU.mult)
            for j in range(8):
                py = P()
                B.mm(py[:, 0:n], [(WFOr[:, m, j * 128:(j + 1) * 128], ACTB[:, m, 0:n]) for m in range(22)])
                B.stt("dve", xb[:, j, 0:n], py[:, 0:n], modc(l, 5, j, w_), xb[:, j, 0:n], ALU.mult, ALU.add)
            if isctx:
                continue
            if l == depth - 1:
                for k in range(8):
                    B.act(SQB[:, k, 0:n], xb[:, k, 0:n], AF.Square)
                pss = P()
                B.mm(pss[:, 0:n], [(ONESB[:], SQB[:, k, 0:n]) for k in range(8)])
                B.act(RS[:, 0:n], pss[:, 0:n], AF.Ln, scale=1.0 / D, bias=EPS)
                B.act(RS[:, 0:n], RS[:, 0:n], AF.Exp, scale=-0.5)
                for k in range(8):
                    B.stt("dve", xb[:, k, 0:n], xb[:, k, 0:n], col("fnorm", k), RS[:, 0:n], ALU.mult, ALU.mult)
                B.dma("sp", yv[:, :, c0 - CTX:c0 - CTX + n], xb[:, :, 0:n], okey=("Y", bi))
            else:
                B.dma("sp", xsv[:, :, c0 - CTX:c0 - CTX + n], xb[:, :, 0:n], okey=XSK)
        B.pop()

    B.mark("end")
    B.barrier()
    return B


def _cols_for(inp, b):
    c = np.zeros((128, NCOLS), np.float32)

    def put(name, arr):
        o, w = COLS[name]
        arr = np.asarray(arr, np.float32)
        assert arr.shape == (128, w), (name, arr.shape, w)
        c[:, o:o + w] = arr

    def colform(v):
        v = np.asarray(v, np.float32)
        lead = v.shape[:-1]
        n = v.shape[-1] // 128
        return np.moveaxis(v.reshape(*lead, n, 128), -1, 0)

    put("c", colform(inp["c"][b]))
    put("cctx", colform(inp["c_ctx"]))
    put("fnorm", colform(inp["final_norm"]))
    put("ln1", colform(inp["ln1"]).reshape(128, 32))
    put("ln2", colform(inp["ln2"]).reshape(128, 32))
    b_in = np.asarray(inp["b_in"], np.float32)
    fm_cols = np.concatenate([np.arange(g, g + 512) for g in FM_GROUP_COL])
    put("bfm", colform(b_in[:, fm_cols]).reshape(128, 4 * 32))
    put("bgate", colform(b_in[:, GATE_COL:GATE_COL + 3072]).reshape(128, 4 * 24))
    put("lbraw", colform(inp["hg_lb_raw"]).reshape(128, 32))
    put("hgn", colform(inp["hg_norm"]).reshape(128, 16))
    put("mln", colform(inp["ml_norm"]).reshape(128, 16))
    put("convw", colform(inp["conv_w"]).reshape(128, 64))
    put("convb", colform(inp["conv_b"]).reshape(128, 16))
    put("lgb", colform(inp["lru_gate_b"]).reshape(128, 64))
    put("lam", colform(inp["lru_lambda"]).reshape(128, 32))
    return c


def make_in_maps(inp, cores):
    inp = {k: np.asarray(v) for k, v in inp.items()}
    b_in = inp["b_in"].astype(np.float32)
    tm_cols = np.concatenate([np.arange(512, 1024), np.arange(3072, 3584), np.arange(2816, 3072), np.arange(4096, 4112)])
    shared = {
        "btm": np.ascontiguousarray(b_in[:, tm_cols]),
        "b_ada": np.ascontiguousarray(inp["b_ada"].astype(np.float32).reshape(1, -1)),
        "w_ada": inp["w_ada"], "w_in": inp["w_in"], "w_branch": inp["w_branch"], "w_out": inp["w_out"],
        "w_ffn_in": inp["w_ffn_in"], "w_ffn_out": inp["w_ffn_out"], "lru_gate_w": inp["lru_gate_w"],
    }
    maps = []
    for b in cores:
        m = dict(shared)
        m["xT"] = np.ascontiguousarray(inp["x"][b].T).reshape(8, 128, SEQ)
        m["cxT"] = np.ascontiguousarray(inp["ctx"][b].T).reshape(8, 128, CTX)
        m["cols"] = _cols_for(inp, b)
        maps.append(m)
    return maps


def unpack_out(yT, depth=DEPTH):
    y = np.asarray(yT).reshape(D, SEQ)
    if (depth - 1) % 2 == 1:
        y = y.reshape(D, 64, 64).transpose(0, 2, 1).reshape(D, SEQ)
    return np.ascontiguousarray(y.T)


def kernel(**inputs):
    B = build(DEPTH, False)
    maps = make_in_maps(inputs, list(range(8)))
    res = run_bass_kernel_spmd(B.nc, maps, core_ids=list(range(8)))
    out = np.stack([unpack_out(r["yT"]) for r in res.results], axis=0)
    return out.astype(np.float32)
```

```python
import numpy as np
from contextlib import ExitStack
import concourse.bass as bass
import concourse.mybir as mybir
from concourse.bass_utils import run_bass_kernel_spmd

F32 = mybir.dt.float32
BF16 = mybir.dt.bfloat16
AF = mybir.ActivationFunctionType
ALU = mybir.AluOpType
AX = mybir.AxisListType

D = 1024
SEQ = 4096
CTX = 256
T = SEQ + CTX
NT = T // 128
DEPTH = 4
N_IN = 8208
FFN = 2816
EPS = 1e-6
NFM = 32
NTM = 1296
FM_GROUP_COL = [0, 1024, 1536, 2048, 2560, 3584, 4112, 4624]
GATE_COL = 5136

COLS = {}
_o = 0
for _n, _w in [("c", 8), ("cctx", 8), ("fnorm", 8), ("ln1", 32), ("ln2", 32), ("bfm", 4 * 32), ("bgate", 4 * 24),
               ("lbraw", 32), ("hgn", 16), ("mln", 16), ("convw", 64), ("convb", 16), ("lgb", 64), ("lam", 32)]:
    COLS[_n] = (_o, _w)
    _o += _w
NCOLS = _o


class Builder:
    NDSEM = 24

    def __init__(self):
        self.nc = bass.Bass("TRN2", target_bir_lowering=False)
        self.es = ExitStack()
        nc = self.nc
        self.eng = {"pe": nc.tensor, "act": nc.scalar, "dve": nc.vector, "pool": nc.gpsimd, "sp": nc.sync}
        self.esem = {e: self.es.enter_context(nc.semaphore("se_" + e)) for e in self.eng}
        self.ecnt = {e: 0 for e in self.eng}
        self.seen = {e: {} for e in self.eng}
        self.dsem = [self.es.enter_context(nc.semaphore("sd%d" % i)) for i in range(self.NDSEM)]
        self.dcum = [0] * self.NDSEM
        self.dnext = 0
        self.buf = {}
        self.scopes = []

    def push(self):
        st = ExitStack()
        self.scopes.append(st)
        return st

    def pop(self):
        self.barrier()
        self.scopes.pop().close()

    def _stack(self):
        return self.scopes[-1] if self.scopes else self.es

    def sb(self, name, shape, dt=F32):
        self.nuid = getattr(self, "nuid", 0) + 1
        name = "%s_%d" % (name, self.nuid)
        return self._stack().enter_context(self.nc.sbuf_tensor(name, list(shape), dt))

    def ps(self, name, shape, dt=F32):
        return self.es.enter_context(self.nc.psum_tensor(name, list(shape), dt))

    def dram(self, name, shape, dt=F32, kind="Internal"):
        return self.nc.dram_tensor(name, list(shape), dt, kind=kind).ap()

    @staticmethod
    def keys(x):
        if x is None or isinstance(x, (int, float)):
            return []
        if isinstance(x, list):
            r = []
            for y in x:
                r += Builder.keys(y)
            return r
        if isinstance(x, (tuple, str)):
            return [x]
        return [x.name]

    def _deps(self, ins, outs):
        deps = []
        for k in self.keys(ins):
            st = self.buf.get(k)
            if st and st["w"]:
                deps.append(st["w"])
        for k in self.keys(outs):
            st = self.buf.get(k)
            if st:
                if st["w"]:
                    deps.append(st["w"])
                deps.extend(st["r"].values())
        return deps

    def _wait(self, e, deps):
        seen = self.seen[e]
        need = {}
        for (sid, sem, val) in deps:
            if seen.get(sid, 0) >= val:
                continue
            if sid not in need or need[sid][1] < val:
                need[sid] = (sem, val)
        for sid, (sem, val) in need.items():
            self.eng[e].wait_ge(sem, val)
            seen[sid] = val

    def _record(self, tok, ins, outs):
        for k in self.keys(outs):
            self.buf[k] = {"w": tok, "r": {}}
        for k in self.keys(ins):
            st = self.buf.setdefault(k, {"w": None, "r": {}})
            st["r"][tok[0]] = tok

    def op(self, e, fn, outs, ins):
        self._wait(e, self._deps(ins, outs))
        inst = fn()
        inst.then_inc(self.esem[e], 1)
        self.ecnt[e] += 1
        tok = ("e_" + e + getattr(self, "sid_suffix", ""), self.esem[e], self.ecnt[e])
        self._record(tok, ins, outs)
        return tok

    def dma(self, q, out, in_, okey=None, ikey=None, **kw):
        ok = okey if okey is not None else out
        ik = ikey if ikey is not None else in_
        i = self.dnext
        self.dnext = (self.dnext + 1) % self.NDSEM
        deps = self._deps([ik], [ok])
        if self.dcum[i] > 0:
            deps.append(("d%d" % i, self.dsem[i], self.dcum[i]))
        self._wait(q, deps)
        self.eng[q].dma_start(out=out, in_=in_, **kw).then_inc(self.dsem[i], 16)
        self.dcum[i] += 16
        tok = ("d%d" % i, self.dsem[i], self.dcum[i])
        self._record(tok, [ik], [ok])
        return tok

    def rotate_sems(self):
        self.barrier()
        self.gen = getattr(self, "gen", 0) + 1
        for e in self.eng:
            self.esem[e] = self.es.enter_context(self.nc.semaphore("se%d_%s" % (self.gen, e)))
            self.ecnt[e] = 0
        self.sid_suffix = "_g%d" % self.gen

    def barrier(self):
        for e in self.eng:
            deps = [("e_" + e2 + getattr(self, "sid_suffix", ""), self.esem[e2], self.ecnt[e2]) for e2 in self.eng if e2 != e and self.ecnt[e2] > 0]
            deps += [("d%d" % i, self.dsem[i], self.dcum[i]) for i in range(self.NDSEM) if self.dcum[i] > 0]
            self._wait(e, deps)

    def mark(self, name):
        if not hasattr(self, "marks"):
            self.marks = []
        self.marks.append((name, getattr(self, "npe", 0)))

    def mm(self, out, pairs):
        self.npe = getattr(self, "npe", 0) + len(pairs)
        ins = []
        for l, r in pairs:
            ins += [l, r]
        n = len(pairs)

        def fn():
            inst = None
            for i, (l, r) in enumerate(pairs):
                inst = self.nc.tensor.matmul(out, lhsT=l, rhs=r, start=(i == 0), stop=(i == n - 1))
            return inst
        return self.op("pe", fn, [out], ins)

    def mm_multi(self, groups):
        self.npe = getattr(self, "npe", 0) + sum(len(p) for _, p in groups)
        ins, outs = [], []
        for out, pairs in groups:
            outs.append(out)
            for l, r in pairs:
                ins += [l, r]

        def fn():
            inst = None
            for out, pairs in groups:
                n = len(pairs)
                for i, (l, r) in enumerate(pairs):
                    inst = self.nc.tensor.matmul(out, lhsT=l, rhs=r, start=(i == 0), stop=(i == n - 1))
            return inst
        return self.op("pe", fn, outs, ins)

    def tr(self, out, in_, ident):
        self.npe = getattr(self, "npe", 0) + 1
        return self.op("pe", lambda: self.nc.tensor.transpose(out, in_, ident), [out], [in_, ident])

    def act(self, out, in_, func, bias=None, scale=None):
        kw = {}
        if bias is not None:
            kw["bias"] = bias
        if scale is not None:
            kw["scale"] = scale
        return self.op("act", lambda: self.nc.scalar.activation(out=out, in_=in_, func=func, **kw), [out], [in_, bias, scale])

    def tt(self, e, out, in0, in1, op):
        return self.op(e, lambda: self.eng[e].tensor_tensor(out=out, in0=in0, in1=in1, op=op), [out], [in0, in1])

    def ts(self, e, out, in0, s1, op0, s2=None, op1=None):
        if op1 is None:
            return self.op(e, lambda: self.eng[e].tensor_scalar(out=out, in0=in0, scalar1=s1, scalar2=None, op0=op0), [out], [in0, s1])
        return self.op(e, lambda: self.eng[e].tensor_scalar(out=out, in0=in0, scalar1=s1, scalar2=s2, op0=op0, op1=op1), [out], [in0, s1, s2])

    def stt(self, e, out, in0, scalar, in1, op0, op1):
        return self.op(e, lambda: self.eng[e].scalar_tensor_tensor(out=out, in0=in0, scalar=scalar, in1=in1, op0=op0, op1=op1), [out], [in0, scalar, in1])

    def copy(self, e, out, in_):
        if e == "act":
            return self.op(e, lambda: self.nc.scalar.copy(out=out, in_=in_), [out], [in_])
        return self.op(e, lambda: self.eng[e].tensor_copy(out=out, in_=in_), [out], [in_])

    def memset(self, e, out, val):
        return self.op(e, lambda: self.eng[e].memset(out, val), [out], [])

    def scan(self, out, d0, d1, init, e="dve"):
        return self.op(e, lambda: self.eng[e].tensor_tensor_scan(out=out, data0=d0, data1=d1, initial=init, op0=ALU.mult, op1=ALU.add), [out], [d0, d1, init])

    def recip(self, out, in_, e="dve"):
        return self.op(e, lambda: self.eng[e].reciprocal(out=out, in_=in_), [out], [in_])

    def reduce(self, out, in_, e="dve"):
        return self.op(e, lambda: self.eng[e].tensor_reduce(out=out, in_=in_, axis=AX.X, op=ALU.add), [out], [in_])

    def aselect(self, out, in_, pattern, cmp, fill, base, cm):
        return self.op("pool", lambda: self.nc.gpsimd.affine_select(out=out, in_=in_, pattern=pattern, compare_op=cmp, fill=fill, base=base, channel_multiplier=cm), [out], [in_])


def softplus_negabs(B, out, x, tmp1, tmp2):
    B.stt("dve", tmp1, x, -1.0, x, ALU.mult, ALU.max)
    B.act(tmp1, tmp1, AF.Exp, scale=-1.0)
    B.ts("dve", tmp2, tmp1, 2.0, ALU.add)
    B.recip(tmp2, tmp2)
    B.tt("dve", tmp1, tmp1, tmp2, ALU.mult)
    B.tt("dve", tmp2, tmp1, tmp1, ALU.mult)
    B.ts("dve", out, tmp2, 1.0 / 11.0, ALU.mult, 1.0 / 9.0, ALU.add)
    for cst in (1.0 / 7.0, 1.0 / 5.0, 1.0 / 3.0, 1.0):
        B.tt("dve", out, out, tmp2, ALU.mult)
        B.ts("dve", out, out, cst, ALU.add)
    B.tt("dve", out, out, tmp1, ALU.mult)
    B.ts("dve", out, out, 2.0, ALU.mult)


def build(depth=DEPTH, debug=False):
    B = Builder()
    nc = B.nc
    EI = "ExternalInput"
    xT_d = B.dram("xT", [8, 128, SEQ], F32, EI)
    cxT_d = B.dram("cxT", [8, 128, CTX], F32, EI)
    cols_d = B.dram("cols", [128, NCOLS], F32, EI)
    btm_d = B.dram("btm", [DEPTH, NTM], F32, EI)
    bada_d = B.dram("b_ada", [1, DEPTH * 6 * D], F32, EI)
    wada_d = B.dram("w_ada", [DEPTH, D, 6 * D], F32, EI)
    win_d = B.dram("w_in", [DEPTH, D, N_IN], F32, EI)
    wbr_d = B.dram("w_branch", [DEPTH, 3, 512, D], F32, EI)
    wout_d = B.dram("w_out", [DEPTH, D, D], F32, EI)
    wfi_d = B.dram("w_ffn_in", [DEPTH, D, 2 * FFN], F32, EI)
    wfo_d = B.dram("w_ffn_out", [DEPTH, FFN, D], F32, EI)
    lgw_d = B.dram("lru_gate_w", [DEPTH, 2, 2, 8, 64, 64], F32, EI)
    y_d = B.dram("yT", [8, 128, SEQ], F32, "ExternalOutput")
    dk = "ExternalOutput" if debug else "Internal"
    XS = [B.dram("XS0", [8, 128, SEQ], F32, dk), B.dram("XS1", [8, 128, SEQ], F32, dk)]
    HXD = B.dram("HXD", [8, 128, T], BF16, "Internal")
    PXF = B.dram("PXF", [NFM, 128, T], F32, dk)
    PXT = B.dram("PXT", [NT, 128, NTM], F32, dk)
    BR = B.dram("BR", [12, 128, T], BF16, "Internal")
    BRdbg = B.dram("BRdbg", [12, 128, T], F32, "ExternalOutput") if debug else None
    WGbf = B.dram("WGbf", [D, 3072], BF16, "Internal")
    WBbf = B.dram("WBbf", [1536, D], BF16, "Internal")
    WObf = B.dram("WObf", [D, D], BF16, "Internal")
    WFIbf = B.dram("WFIbf", [D, 2 * FFN], BF16, "Internal")
    WFObf = B.dram("WFObf", [FFN, D], BF16, "Internal")

    PS = [B.ps("P%d" % i, [128, 512], F32) for i in range(7)]
    PB = B.ps("PB", [128, 1024], BF16)
    psrot = [0]

    def P():
        psrot[0] = (psrot[0] + 1) % 7
        return PS[psrot[0]]

    CC = B.sb("CC", [128, NCOLS])
    IDF = B.sb("IDF", [128, 128])
    IDB = B.sb("IDB", [128, 128], BF16)
    ONESF = B.sb("ONESF", [128, 128])
    ONESB = B.sb("ONESB", [128, 128], BF16)
    TRIF = B.sb("TRIF", [128, 128])
    TRIB = B.sb("TRIB", [128, 128])
    MHF = B.sb("MHF", [128, 128])
    MHB = B.sb("MHB", [128, 128])
    MODC = B.sb("MODC", [128, DEPTH * 96])
    A1 = B.sb("A1", [128, DEPTH * 16])
    A2 = B.sb("A2", [128, DEPTH * 16])
    LB = B.sb("LB", [128, 32])
    OML = B.sb("OML", [128, 32])
    CST = B.sb("CST", [128, 32])
    CST2 = B.sb("CST2", [128, 32])
    BQ8 = B.sb("BQ8", [128, 8])
    CX = B.sb("CX", [128, 8, CTX])
    SMT = [B.sb("SMT%d" % i, [128, 32]) for i in range(4)]

    def col(name, i):
        o = COLS[name][0] + i
        return CC[:, o:o + 1]

    def modc(l, j, k, w):
        o = ((l * 48 + j * 8 + k) * 2 + w)
        return MODC[:, o:o + 1]

    def a1c(l, k, w):
        o = (l * 8 + k) * 2 + w
        return A1[:, o:o + 1]

    def a2c(l, k, w):
        o = (l * 8 + k) * 2 + w
        return A2[:, o:o + 1]

    B.dma("sp", CC[:], cols_d[:, :])
    B.dma("sp", CX[:], cxT_d.rearrange("k p t -> p k t"))
    B.memset("pool", ONESF[:], 1.0)
    B.memset("pool", ONESB[:], 1.0)
    B.aselect(IDF[:], ONESF[:], [[-1, 128]], ALU.is_equal, 0.0, 0, 1)
    B.copy("pool", IDB[:], IDF[:])
    B.aselect(TRIF[:], ONESF[:], [[1, 128]], ALU.is_ge, 0.0, 0, -1)
    B.aselect(TRIB[:], ONESF[:], [[-1, 128]], ALU.is_ge, 0.0, 0, 1)
    RM = B.sb("RM", [128, 4])
    BD = B.sb("BD", [128, 128])
    for q in range(4):
        B.aselect(RM[:, q:q + 1], ONESF[:, 0:1], [[0, 1]], ALU.is_ge, 0.0, -32 * q, 1)
        B.aselect(RM[:, q:q + 1], RM[:, q:q + 1], [[0, 1]], ALU.is_ge, 0.0, 32 * q + 31, -1)
        B.copy("pool", BD[:, q * 32:(q + 1) * 32], RM[:, q:q + 1].to_broadcast([128, 32]))
    MASK32 = B.sb("MASK32", [128, 32])
    B.memset("pool", MASK32[:], 1.0)
    B.memset("pool", MASK32[:, 0:1], 0.0)
    B.tt("pool", MHF[:], TRIF[:], BD[:], ALU.mult)
    B.tt("pool", MHB[:], TRIB[:], BD[:], ALU.mult)

    lo, _ = COLS["lbraw"]
    E_ = SMT[0]
    B.act(E_[:], CC[:, lo:lo + 32], AF.Exp)
    S_ = SMT[1]
    B.tt("dve", S_[:, 0:8], E_[:, 0:8], E_[:, 8:16], ALU.add)
    B.tt("dve", S_[:, 0:8], S_[:, 0:8], E_[:, 16:24], ALU.add)
    B.tt("dve", S_[:, 0:8], S_[:, 0:8], E_[:, 24:32], ALU.add)
    B.recip(S_[:, 0:8], S_[:, 0:8])
    for l in range(1, 4):
        B.tt("dve", E_[:, l * 8:(l + 1) * 8], E_[:, l * 8:(l + 1) * 8], S_[:, 0:8], ALU.mult)
    B.memset("dve", LB[:, 0:8], 0.0)
    B.copy("dve", LB[:, 8:16], E_[:, 8:16])
    B.tt("dve", LB[:, 16:24], LB[:, 8:16], E_[:, 16:24], ALU.add)
    B.tt("dve", LB[:, 24:32], LB[:, 16:24], E_[:, 24:32], ALU.add)
    B.ts("dve", OML[:], LB[:], -1.0, ALU.mult, 1.0, ALU.add)
    lo, _ = COLS["lam"]
    softplus_negabs(B, SMT[0][:], CC[:, lo:lo + 32], SMT[1][:], SMT[2][:])
    B.ts("dve", SMT[1][:], CC[:, lo:lo + 32], -1.0, ALU.mult, 0.0, ALU.max)
    B.tt("dve", SMT[0][:], SMT[0][:], SMT[1][:], ALU.add)
    B.ts("dve", CST[:], SMT[0][:], -8.0, ALU.mult)
    B.ts("dve", CST2[:], SMT[0][:], -16.0, ALU.mult)

    B.push()
    S2 = B.sb("S2", [128, 8, 2], BF16)
    BADA = B.sb("BADA", [2, DEPTH * 6 * D])
    MODR = B.sb("MODR", [2, 6 * D])
    WA = [B.sb("WA%d" % i, [128, 8, 512], BF16) for i in range(2)]
    lo, _ = COLS["c"]
    B.act(S2[:, :, 0], CC[:, lo:lo + 8], AF.Silu)
    lo, _ = COLS["cctx"]
    B.act(S2[:, :, 1], CC[:, lo:lo + 8], AF.Silu)
    B.dma("sp", BADA[:], bada_d[0].partition_broadcast(2))
    for l in range(depth):
        wv = wada_d[l].rearrange("(k p) f -> p k f", p=128)
        for fc in range(12):
            w = WA[fc % 2]
            B.dma("pool", w[:], wv[:, :, fc * 512:(fc + 1) * 512])
            p = P()
            B.mm(p[0:2, :], [(S2[:, k, :], w[:, k, :]) for k in range(8)])
            B.tt("dve", MODR[:, fc * 512:(fc + 1) * 512], p[0:2, :], BADA[:, l * 6144 + fc * 512: l * 6144 + (fc + 1) * 512], ALU.add)
        p = P()
        for f in range(48):
            B.tr(p[:, f * 2:(f + 1) * 2], MODR[0:2, f * 128:(f + 1) * 128], IDF[0:2, 0:2])
        B.copy("dve", MODC[:, l * 96:(l + 1) * 96], p[:, 0:96])
        lo1, _ = COLS["ln1"]
        lo2, _ = COLS["ln2"]
        for (AT_, jv, lo_) in ((A1, 1, lo1), (A2, 4, lo2)):
            src = MODC[:, l * 96 + jv * 16: l * 96 + (jv + 1) * 16].rearrange("p (k w) -> p k w", w=2)
            dst = AT_[:, l * 16:(l + 1) * 16].rearrange("p (k w) -> p k w", w=2)
            lnb = CC[:, lo_ + l * 8: lo_ + (l + 1) * 8].unsqueeze(2).to_broadcast([128, 8, 2])
            B.stt("dve", dst, src, 1.0, lnb, ALU.add, ALU.mult)
    B.pop()

    blocks = [(0, CTX)] + [(CTX + i * 512, 512) for i in range(8)]
    cur = None
    for l in range(depth):
        B.rotate_sems()
        last = (l == DEPTH - 1)
        perm = l >= 1
        if l == 0:
            xs_old, xs_new = xT_d, XS[0]
        else:
            xs_old, xs_new = XS[(l - 1) % 2], XS[l % 2]

        B.mark("L%d_p1" % l)
        B.push()
        HX = [B.sb("HX%d" % k, [128, T], BF16) for k in range(8)]
        B.push()
        XT = [B.sb("XT%d" % i, [128, SEQ]) for i in range(2)]
        ACC = B.sb("ACC", [128, SEQ])
        RSTD = B.sb("RSTD", [128, SEQ])
        XP = B.sb("XP", [128, SEQ])
        CACC = B.sb("CACC", [128, CTX])
        CRS = B.sb("CRS", [128, CTX])
        CT = B.sb("CT", [128, CTX])
        for k in range(8):
            xt = XT[k % 2]
            B.dma("sp", xt[:], xs_old[k], ikey=("XS", id(xs_old), k))
            if k == 0:
                B.act(ACC[:], xt[:], AF.Square)
                B.act(CACC[:], CX[:, k, :], AF.Square)
            else:
                B.act(XP[:], xt[:], AF.Square)
                B.tt("dve", ACC[:], ACC[:], XP[:], ALU.add)
                B.act(CT[:], CX[:, k, :], AF.Square)
                B.tt("dve", CACC[:], CACC[:], CT[:], ALU.add)
        for b8 in range(8):
            p = P()
            B.mm(p[:, :], [(ONESF[:], ACC[:, b8 * 512:(b8 + 1) * 512])])
            B.act(RSTD[:, b8 * 512:(b8 + 1) * 512], p[:, :], AF.Ln, scale=1.0 / D, bias=EPS)
        B.act(RSTD[:], RSTD[:], AF.Exp, scale=-0.5)
        p = P()
        B.mm(p[:, 0:CTX], [(ONESF[:], CACC[:])])
        B.act(CRS[:], p[:, 0:CTX], AF.Ln, scale=1.0 / D, bias=EPS)
        B.act(CRS[:], CRS[:], AF.Exp, scale=-0.5)
        for k in range(8):
            xt = XT[k % 2]
            B.dma("sp", xt[:], xs_old[k], ikey=("XS", id(xs_old), k))
            B.tt("dve", ACC[:], xt[:], RSTD[:], ALU.mult)
            hxo = HX[k][:, CTX:T]
            if perm:
                hxo = hxo.rearrange("p (a b) -> p b a", a=64, b=64)
                src = ACC[:].rearrange("p (a b) -> p a b", a=64)
            else:
                src = ACC[:]
            B.act(hxo, src, AF.Identity, scale=a1c(l, k, 0), bias=modc(l, 0, k, 0))
            if perm:
                B.copy("dve", XP[:].rearrange("p (a b) -> p b a", a=64, b=64), xt[:].rearrange("p (a b) -> p a b", a=64))
                B.dma("sp", xs_new[k], XP[:], okey=("XS", id(xs_new), k))
            else:
                B.dma("sp", xs_new[k], xt[:], okey=("XS", id(xs_new), k))
            B.tt("dve", CT[:], CX[:, k, :], CRS[:], ALU.mult)
            B.act(HX[k][:, 0:CTX], CT[:], AF.Identity, scale=a1c(l, k, 1), bias=modc(l, 0, k, 1))
            B.dma("sp", HXD[k], HX[k][:], okey=("HXD", k))
        B.pop()

        B.mark("L%d_p1c" % l)
        B.push()
        BT = B.sb("BT", [128, NTM])
        WT = B.sb("WT", [128, 8, NTM], BF16)
        WF = [B.sb("WF%d" % i, [128, 8, 512], BF16) for i in range(2)]
        STG = [B.sb("STG%d" % i, [128, T]) for i in range(2)]
        STT = [B.sb("STT%d" % i, [128, NTM]) for i in range(2)]
        wv = win_d[l].rearrange("(k p) f -> p k f", p=128)
        B.dma("sp", BT[:], btm_d[l].partition_broadcast(128))
        B.dma("pool", WT[:, :, 0:512], wv[:, :, 512:1024])
        B.dma("pool", WT[:, :, 512:1024], wv[:, :, 3072:3584])
        B.dma("pool", WT[:, :, 1024:1280], wv[:, :, 2816:3072])
        B.dma("pool", WT[:, :, 1280:1296], wv[:, :, 4096:4112])
        lo, _ = COLS["bfm"]
        B.ts("dve", BQ8[:, 0:2], CC[:, lo + l * 32 + 16: lo + l * 32 + 18], 0.125, ALU.mult)
        for g in range(8):
            w = WF[g % 2]
            B.dma("pool", w[:], wv[:, :, FM_GROUP_COL[g]:FM_GROUP_COL[g] + 512])
            for q in range(4):
                ft = g * 4 + q
                stg = STG[ft % 2]
                bias = col("bfm", l * 32 + ft)
                scale = None
                if ft < 8:
                    func = AF.Silu
                elif 20 <= ft < 24:
                    func = AF.Sigmoid
                else:
                    func = AF.Identity
                if ft in (16, 17):
                    scale = 0.125
                    bias = BQ8[:, ft - 16:ft - 15]
                for (c0, n) in blocks:
                    p = P()
                    B.mm(p[:, 0:n], [(w[:, k, q * 128:(q + 1) * 128], HX[k][:, c0:c0 + n]) for k in range(8)])
                    B.act(stg[:, c0:c0 + n], p[:, 0:n], func, bias=bias, scale=scale)
                B.dma("sp", PXF[ft], stg[:], okey=("PXF", ft))
        for i in range(NT):
            sl = slice(i * 128, (i + 1) * 128)
            pa, pb, pc = P(), P(), P()
            B.mm_multi([
                (pa[:, :], [(HX[k][:, sl], WT[:, k, 0:512]) for k in range(8)]),
                (pb[:, :], [(HX[k][:, sl], WT[:, k, 512:1024]) for k in range(8)]),
                (pc[:, 0:272], [(HX[k][:, sl], WT[:, k, 1024:1296]) for k in range(8)]),
            ])
            st = STT[i % 2]
            B.tt("dve", st[:, 0:512], pa[:, :], BT[:, 0:512], ALU.add)
            B.tt("dve", st[:, 512:1024], pb[:, :], BT[:, 512:1024], ALU.add)
            B.tt("dve", st[:, 1024:1296], pc[:, 0:272], BT[:, 1024:1296], ALU.add)
            B.dma("sp", PXT[i], st[:], okey=("PXT", i))
        B.pop()
        B.pop()

        for r4 in range(4):
            rs = slice(r4 * 256, (r4 + 1) * 256)
            B.dma("pool", WGbf[rs, :], win_d[l][rs, GATE_COL:GATE_COL + 3072], okey=("WGbf", r4))
            B.dma("pool", WObf[rs, :], wout_d[l][rs, :], okey=("WObf", r4))
            B.dma("pool", WFIbf[rs, :], wfi_d[l][rs, :], okey=("WFIbf", r4))
        wb2 = wbr_d[l].rearrange("n r f -> (n r) f")
        for r4 in range(3):
            B.dma("pool", WBbf[r4 * 512:(r4 + 1) * 512, :], wb2[r4 * 512:(r4 + 1) * 512, :], okey=("WBbf", r4))
        for r4 in range(4):
            rs = slice(r4 * 704, (r4 + 1) * 704)
            B.dma("pool", WFObf[rs, :], wfo_d[l][rs, :], okey=("WFObf", r4))
        PXT_ALL = [("PXT", i) for i in range(NT)]
        pxt_v = PXT.rearrange("i p c -> p i c")

        B.mark("L%d_lru" % l)
        HT = T // 2
        HB = [(0, HT), (HT, T)]
        SEGS = ((0, CTX), (CTX, T))
        B.push()
        LXs = [B.sb("LX%d" % i, [128, T]) for i in range(2)]
        GWs = [B.sb("GW%d" % i, [128, 4, 128]) for i in range(2)]
        LYs = [[B.sb("LY%d_%d" % (r, i), [128, HT]) for i in range(2)] for r in range(2)]
        XC = [B.sb("XC%d" % i, [128, HT]) for i in range(2)]
        HL = [B.sb("HL%d" % i, [128, HT]) for i in range(2)]
        RR = [B.sb("RR%d" % i, [128, HT]) for i in range(2)]
        II = [B.sb("II%d" % i, [128, HT]) for i in range(2)]
        AA = [B.sb("AA%d" % i, [128, HT]) for i in range(2)]
        OB = [B.sb("OB%d" % i, [128, HT], BF16) for i in range(2)]
        for r in range(2):
            B.memset("pool", GWs[r][:], 0.0)

        def load_lru(j):
            B.dma("sp", LXs[j % 2][:], PXF[24 + j], ikey=("PXF", 24 + j))
            for hs, (r0, r1) in enumerate(HB):
                B.dma("sp", LYs[j % 2][hs][:], PXF[28 + j][:, r0:r1], ikey=("PXF", 28 + j))
            for z in range(2):
                for g in range(2):
                    for h2 in range(2):
                        B.dma("sp", GWs[j % 2][h2 * 64:(h2 + 1) * 64, z * 2 + g, h2 * 64:(h2 + 1) * 64], lgw_d[l, z, g, 2 * j + h2])

        load_lru(0)
        for j in range(4):
            if j + 1 < 4:
                load_lru(j + 1)
            LX, GW, LY = LXs[j % 2], GWs[j % 2], LYs[j % 2]
            cw = lambda tap: col("convw", l * 16 + tap * 4 + j)
            for hs, (r0, r1) in enumerate(HB):
                B.ts("dve", XC[hs][:], LX[:, r0:r1], cw(2), ALU.mult, col("convb", l * 4 + j), ALU.add)
            for tap, o in ((0, -2), (1, -1), (3, 1)):
                for hs, (r0, r1) in enumerate(HB):
                    for (s0, s1) in SEGS:
                        d0 = max(r0, s0 + max(0, -o))
                        d1 = min(r1, s1 - max(0, o))
                        if d1 > d0:
                            B.stt("dve", XC[hs][:, d0 - r0:d1 - r0], LX[:, d0 + o:d1 + o], cw(tap), XC[hs][:, d0 - r0:d1 - r0], ALU.mult, ALU.add)
            for z in range(2):
                for hs, (r0, r1) in enumerate(HB):
                    c = 0
                    while c < HT:
                        n = min(512, HT - c)
                        p = P()
                        B.mm(p[:, 0:n], [(GW[:, z * 2 + 0, :], XC[hs][:, c:c + n])])
                        B.act(RR[hs][:, c:c + n], p[:, 0:n], AF.Sigmoid, bias=col("lgb", l * 16 + (z * 2 + 0) * 4 + j))
                        p = P()
                        B.mm(p[:, 0:n], [(GW[:, z * 2 + 1, :], XC[hs][:, c:c + n])])
                        B.act(II[hs][:, c:c + n], p[:, 0:n], AF.Sigmoid, bias=col("lgb", l * 16 + (z * 2 + 1) * 4 + j))
                        c += n
                ci = l * 8 + z * 4 + j
                for hs in range(2):
                    B.act(AA[hs][:], RR[hs][:], AF.Exp, scale=CST[:, ci:ci + 1])
                for hs in range(2):
                    B.act(RR[hs][:], RR[hs][:], AF.Exp, scale=CST2[:, ci:ci + 1])
                for hs in range(2):
                    B.act(RR[hs][:], RR[hs][:], AF.Sqrt, scale=-1.0, bias=1.0)
                for hs in range(2):
                    B.tt("dve", II[hs][:], II[hs][:], RR[hs][:], ALU.mult)
                for hs in range(2):
                    B.tt("dve", II[hs][:], II[hs][:], XC[hs][:], ALU.mult)
                if z == 0:
                    B.scan(RR[0][:, 0:CTX], AA[0][:, 0:CTX], II[0][:, 0:CTX], 0.0)
                    B.scan(RR[0][:, CTX:HT], AA[0][:, CTX:HT], II[0][:, CTX:HT], RR[0][:, CTX - 1:CTX])
                    B.scan(RR[1][:], AA[1][:], II[1][:], RR[0][:, HT - 1:HT])
                    for hs in range(2):
                        B.copy("act", HL[hs][:], RR[hs][:])
                else:
                    B.scan(RR[0][:, 0:CTX][:, ::-1], AA[0][:, 0:CTX][:, ::-1], II[0][:, 0:CTX][:, ::-1], 0.0)
                    B.scan(RR[1][:, ::-1], AA[1][:, ::-1], II[1][:, ::-1], RR[0][:, 0:1])
                    B.scan(RR[0][:, CTX:HT][:, ::-1], AA[0][:, CTX:HT][:, ::-1], II[0][:, CTX:HT][:, ::-1], RR[1][:, 0:1])
                    for hs in range(2):
                        B.tt("dve", HL[hs][:], HL[hs][:], RR[hs][:], ALU.add)
            for hs in range(2):
                B.act(AA[hs][:], LY[hs][:], AF.Square)
            for hs in range(2):
                B.ts("dve", AA[hs][:], AA[hs][:], 0.044715 * 1.5957691216, ALU.mult, 1.5957691216, ALU.add)
                B.tt("dve", AA[hs][:], AA[hs][:], LY[hs][:], ALU.mult)
            for hs in range(2):
                B.act(AA[hs][:], AA[hs][:], AF.Sigmoid)
            for hs, (r0, r1) in enumerate(HB):
                B.tt("dve", AA[hs][:], AA[hs][:], LY[hs][:], ALU.mult)
                B.tt("dve", OB[hs][:], AA[hs][:], HL[hs][:], ALU.mult)
                B.dma("sp", BR[8 + j][:, r0:r1], OB[hs][:], okey=("BR", 8 + j))
                if debug:
                    B.tt("dve", II[hs][:], AA[hs][:], HL[hs][:], ALU.mult)
                    B.dma("sp", BRdbg[8 + j][:, r0:r1], II[hs][:])
        B.pop()

        B.mark("L%d_mlstm" % l)
        B.push()
        GT = B.sb("GT", [128, NT, 16])
        LF = B.sb("LF", [128, NT, 8])
        BB = B.sb("BBm", [128, NT, 8])
        BTOT = B.sb("BTOT", [128, NT, 8])
        WP = B.sb("WPm", [128, NT, 8])
        WS = B.sb("WSm", [128, NT, 8])
        EN = B.sb("ENm", [128, NT, 8])
        EBT = B.sb("EBT", [128, NT, 8])
        TM1 = B.sb("TM1", [128, NT, 8])
        TM2 = B.sb("TM2", [128, NT, 8])
        B.dma("sp", GT[:], pxt_v[:, :, 1280:1296], ikey=PXT_ALL)
        softplus_negabs(B, LF[:], GT[:, :, 8:16], TM1[:], TM2[:])
        B.ts("dve", TM1[:], GT[:, :, 8:16], 0.0, ALU.min)
        B.tt("dve", LF[:], TM1[:], LF[:], ALU.subtract)
        p = P()
        pv = p[:, 0:NT * 8].rearrange("p (i c) -> p i c", c=8)
        for i in range(NT):
            B.mm(pv[:, i, 0:4], [(TRIF[:], LF[:, i, 0:4])])
            B.mm(pv[:, i, 4:8], [(TRIB[:], LF[:, i, 4:8])])
        B.copy("dve", BB[:], pv)
        p = P()
        B.mm(p[:, 0:NT * 8], [(ONESF[:], LF[:].rearrange("p i c -> p (i c)"))])
        B.copy("dve", BTOT[:], p[:, 0:NT * 8].rearrange("p (i c) -> p i c", c=8))
        B.tt("dve", TM1[:], GT[:, :, 0:8], BB[:], ALU.subtract)
        B.act(WP[:], TM1[:], AF.Exp)
        B.tt("dve", TM1[:], TM1[:], BTOT[:], ALU.add)
        B.act(WS[:], TM1[:], AF.Exp)
        B.act(EN[:], BB[:], AF.Exp, scale=-1.0)
        B.act(EBT[:], BTOT[:], AF.Exp)
        ord_f = list(range(NT))
        ord_b = [1, 0] + list(range(NT - 1, 1, -1))
        HH = B.sb("HH", [128, NT, 132])
        TMPH = B.sb("TMPH", [128, NT, 132])
        KWA = [B.sb("KWA%d" % z, [128, NT, 64], BF16) for z in range(2)]
        DEN = [B.sb("DENm%d" % z, [128, NT]) for z in range(2)]
        OBm = B.sb("OBm", [128, T], BF16)
        CS = [[B.sb("CSm%d_%d" % (z, r), [64, 132]) for r in range(4)] for z in range(2)]
        CB = [[B.sb("CBm%d_%d" % (z, r), [64, 132], BF16) for r in range(4)] for z in range(2)]
        PT = [[B.sb("PTm%d_%d" % (z, r), [128, 128], BF16) for r in range(3)] for z in range(2)]
        KW = [[B.sb("KWm%d_%d" % (z, r), [128, 64], BF16) for r in range(3)] for z in range(2)]
        RC = [[B.sb("RCm%d_%d" % (z, r), [128, 2]) for r in range(3)] for z in range(2)]
        SS = B.sb("SSm", [128, NT])
        MIN_ = [dict(QT=B.sb("QTm%d" % r, [64, T], BF16), KT=B.sb("KTm%d" % r, [64, T], BF16),
                     KTK=B.sb("KTK%d" % r, [128, NT, 64]), VA=B.sb("VA%d" % r, [128, NT, 132], BF16),
                     SOG=B.sb("SOG%d" % r, [128, T])) for r in range(2)]
        for r in range(2):
            B.memset("pool", MIN_[r]["VA"][:, :, 128:132], 1.0)

        def load_head(h):
            m = MIN_[h % 2]
            r0 = (h % 2) * 64
            B.dma("pool", m["QT"][:], PXF[16 + h // 2][r0:r0 + 64, :], ikey=("PXF", 16 + h // 2))
            B.dma("pool", m["KT"][:], PXF[18 + h // 2][r0:r0 + 64, :], ikey=("PXF", 18 + h // 2))
            B.dma("sp", m["KTK"][:], pxt_v[:, :, 1024 + h * 64:1024 + (h + 1) * 64], ikey=PXT_ALL)
            B.dma("pool", m["VA"][:, :, 0:128], pxt_v[:, :, 512 + h * 128:512 + (h + 1) * 128], ikey=PXT_ALL)
            B.dma("sp", m["SOG"][:], PXF[20 + h], ikey=("PXF", 20 + h))

        load_head(0)
        for h in range(4):
            if h + 1 < 4:
                load_head(h + 1)
            m = MIN_[h % 2]
            QT, KT, KTK, VA, SOG = m["QT"], m["KT"], m["KTK"], m["VA"], m["SOG"]
            for z in range(2):
                B.memset("pool", CS[z][0][:], 0.0)
                B.memset("pool", CB[z][0][:], 0.0)
            def emit_pt(step, z):
                i = (ord_f if z == 0 else ord_b)[step]
                sl = slice(i * 128, (i + 1) * 128)
                zh = z * 4 + h
                p = P()
                B.mm(p[:, 0:128], [(KT[:, sl], QT[:, sl])])
                B.stt("dve", PT[z][step % 3][:], p[:, 0:128], WP[:, i, zh:zh + 1], (TRIF if z == 0 else TRIB)[:], ALU.mult, ALU.mult)

            for z in range(2):
                zh = z * 4 + h
                B.tt("dve", KWA[z][:], KTK[:], WS[:, :, zh].unsqueeze(2).to_broadcast([128, NT, 64]), ALU.mult)
                emit_pt(0, z)
            for step in range(NT):
                for z in range(2):
                    i = (ord_f if z == 0 else ord_b)[step]
                    sl = slice(i * 128, (i + 1) * 128)
                    zh = z * 4 + h
                    pd = P()
                    B.mm(pd[0:64, 0:132], [(KWA[z][:, i, :], VA[:, i, :])])
                    B.stt("dve", CS[z][(step + 1) % 4][:], CS[z][step % 4][:], EBT[0:64, i, zh:zh + 1], pd[0:64, 0:132], ALU.mult, ALU.add)
                    B.copy("act", CB[z][(step + 1) % 4][:], CS[z][(step + 1) % 4][:])
                    po = P()
                    B.mm(po[:, 0:132], [(PT[z][step % 3][:], VA[:, i, :]), (QT[:, sl], CB[z][step % 4][:])])
                    B.copy("act", (HH if z == 0 else TMPH)[:, i, :], po[:, 0:132])
                    if step + 1 < NT:
                        emit_pt(step + 1, z)
            for z in range(2):
                zh = z * 4 + h
                Hz = HH if z == 0 else TMPH
                B.stt("dve", DEN[z][:], Hz[:, :, 128], -1.0, Hz[:, :, 128], ALU.mult, ALU.max)
                B.tt("dve", DEN[z][:], DEN[z][:], EN[:, :, zh], ALU.max)
                B.act(DEN[z][:], DEN[z][:], AF.Ln)
                B.act(DEN[z][:], DEN[z][:], AF.Exp, scale=-1.0)
                B.tt("dve", Hz[:, :, 0:128], Hz[:, :, 0:128], DEN[z][:].unsqueeze(2).to_broadcast([128, NT, 128]), ALU.mult)
            B.tt("dve", HH[:, :, 0:128], HH[:, :, 0:128], TMPH[:, :, 0:128], ALU.add)
            B.tt("dve", TMPH[:, :, 0:128], HH[:, :, 0:128], HH[:, :, 0:128], ALU.mult)
            B.reduce(SS[:], TMPH[:, :, 0:128])
            B.act(SS[:], SS[:], AF.Ln, scale=1.0 / 128, bias=EPS)
            B.act(SS[:], SS[:], AF.Exp, scale=-0.5)
            B.tt("dve", HH[:, :, 0:128], HH[:, :, 0:128], SS[:].unsqueeze(2).to_broadcast([128, NT, 128]), ALU.mult)
            for i4 in range(0, NT, 4):
                p = P()
                nn = min(4, NT - i4)
                for u in range(nn):
                    B.tr(p[:, u * 128:(u + 1) * 128], HH[:, i4 + u, 0:128], IDF[:])
                B.stt("dve", OBm[:, i4 * 128:(i4 + nn) * 128], p[:, 0:nn * 128], col("mln", l * 4 + h), SOG[:, i4 * 128:(i4 + nn) * 128], ALU.mult, ALU.mult)
            B.dma("sp", BR[4 + h], OBm[:], okey=("BR", 4 + h))
            if debug:
                B.copy("dve", TMPH[:].rearrange("p i c -> p (i c)")[:, 0:T], OBm[:])
                B.dma("sp", BRdbg[4 + h], TMPH[:].rearrange("p i c -> p (i c)")[:, 0:T])
        B.pop()

        B.mark("L%d_hgrn" % l)
        NCH = T // 32
        B.push()
        RMASK = B.sb("RMASK", [128, T])
        B.memset("pool", RMASK[:], 1.0)
        B.memset("pool", RMASK[:].rearrange("p (c t) -> p c t", t=32)[:, :, 0], 0.0)
        for h in range(4):
            B.push()
            QS = B.sb("QS", [128, T])
            VT = B.sb("VTh", [128, NT, 128], BF16)
            OA = B.sb("OA", [128, T])
            FFh = [B.sb("FF%d" % i, [128, T // 2]) for i in range(2)]
            KKh = [B.sb("KK%d" % i, [128, T // 2]) for i in range(2)]
            EEh = [B.sb("EE%d" % i, [128, T // 2]) for i in range(2)]
            QT2 = B.sb("QT2", [128, T], BF16)
            KT2 = B.sb("KT2", [128, T], BF16)
            KTO = B.sb("KTO", [128, NT, 128], BF16)
            BMID = B.sb("BMID", [128, NCH])
            BLS = B.sb("BLS", [128, NCH])
            EM = B.sb("EMh", [128, NCH])
            EL2 = B.sb("EL2", [128, NCH])
            ELM = B.sb("ELM", [128, NCH])
            RING = 6
            SR = [B.sb("SR%d" % r, [128, 128]) for r in range(RING)]
            PDS = [None, None, None]
            SPR = [B.sb("SPR%d" % r, [128, 128], BF16) for r in range(RING)]
            ATR = [B.sb("ATR%d" % r, [128, 128], BF16) for r in range(3)]
            cidx = 0
            B.dma("sp", QS[:], PXF[h], ikey=("PXF", h))
            B.dma("pool", VT[:], pxt_v[:, :, h * 128:(h + 1) * 128], ikey=PXT_ALL)
            VTM = [B.sb("VTM%d" % q, [128, NT, 128], BF16) for q in range(4)]
            for q in range(4):
                B.ts("dve", VTM[q][:], VT[:], RM[:, q:q + 1], ALU.mult)
            for z in range(2):
                zh = l * 8 + z * 4 + h
                mid = 15 if z == 0 else 16
                lastp = 31 if z == 0 else 0
                HT = T // 2
                NCH2 = NCH // 2

                def steps(hs):
                    r = slice(hs * HT, (hs + 1) * HT)
                    cr = slice(hs * NCH2, (hs + 1) * NCH2)
                    F_, K_, E_ = FFh[hs], KKh[hs], EEh[hs]
                    ev = E_[:].rearrange("p (c t) -> p c t", t=32)
                    fv = F_[:].rearrange("p (c t) -> p c t", t=32)
                    KT3 = E_[:].bitcast(BF16)[:, 0:HT]
                    yield lambda: B.dma("sp", F_[:], PXF[8 + z * 4 + h][:, r], ikey=("PXF", 8 + z * 4 + h))
                    yield lambda: B.act(F_[:], F_[:], AF.Sigmoid)
                    yield lambda: B.ts("dve", F_[:], F_[:], OML[:, zh:zh + 1], ALU.mult, LB[:, zh:zh + 1], ALU.add)
                    yield lambda: B.act(K_[:], F_[:], AF.Identity, scale=-1.0, bias=1.0)
                    yield lambda: B.act(F_[:], F_[:], AF.Ln)
                    yield lambda: B.scan(E_[:], RMASK[:, r], F_[:], 0.0)
                    if z == 1:
                        yield lambda: B.copy("dve", BLS[:, cr], ev[:, :, 31])
                        yield lambda: B.tt("dve", ev, BLS[:, cr].unsqueeze(2).to_broadcast([128, NCH2, 32]), ev, ALU.subtract)
                        yield lambda: B.tt("dve", E_[:], E_[:], F_[:], ALU.add)
                    yield lambda: B.copy("dve", BMID[:, cr], ev[:, :, mid])
                    yield lambda: B.copy("dve", BLS[:, cr], ev[:, :, lastp])
                    yield lambda: B.act(EM[:, cr], BMID[:, cr], AF.Exp)
                    yield lambda: B.act(EL2[:, cr], BLS[:, cr], AF.Exp)
                    yield lambda: B.tt("dve", BLS[:, cr], BLS[:, cr], BMID[:, cr], ALU.subtract)
                    yield lambda: B.act(ELM[:, cr], BLS[:, cr], AF.Exp)
                    yield lambda: B.tt("dve", ev, ev, BMID[:, cr].unsqueeze(2).to_broadcast([128, NCH2, 32]), ALU.subtract)
                    yield lambda: B.act(F_[:], E_[:], AF.Exp)
                    yield lambda: B.tt("dve", QT2[:, r], QS[:, r], F_[:], ALU.mult)
                    yield lambda: B.act(F_[:], E_[:], AF.Exp, scale=-1.0)
                    yield lambda: B.tt("dve", KT2[:, r], K_[:], F_[:], ALU.mult)
                    yield lambda: B.tt("dve", fv, fv, ELM[:, cr].unsqueeze(2).to_broadcast([128, NCH2, 32]), ALU.mult)
                    yield lambda: B.tt("dve", KT3, K_[:], F_[:], ALU.mult)
                    nth = NT // 2
                    for i4 in range(0, nth, 4):
                        nn = min(4, nth - i4)

                        def trs(i4=i4, nn=nn):
                            for u in range(nn):
                                B.tr(PB[:, u * 128:(u + 1) * 128], KT3[:, (i4 + u) * 128:(i4 + u + 1) * 128], IDB[:])
                            B.copy("act", KTO[:, hs * nth + i4:hs * nth + i4 + nn, :], PB[:, 0:nn * 128].rearrange("p (u k) -> p u k", k=128))
                        yield trs

                gens = [steps(0), steps(1)]
                live = [True, True]
                while any(live):
                    for hs in range(2):
                        if live[hs]:
                            try:
                                next(gens[hs])()
                            except StopIteration:
                                live[hs] = False
                B.memset("pool", SR[0][:], 0.0)
                order = ord_f if z == 0 else ord_b
                cidx = 0

                PAS = [None, None, None]

                def emit_pre_pe(ti):
                    i = order[ti]
                    sl = slice(i * 128, (i + 1) * 128)
                    pa = P()
                    B.mm(pa[:, 0:128], [(KT2[:, sl], QT2[:, sl])])
                    PAS[ti % 3] = pa
                    pd = P()
                    B.mm_multi([(pd[:, hf * 128:(hf + 1) * 128], [(KTO[:, i, :], VTM[hf][:, i, :])]) for hf in range(4)])
                    PDS[ti % 3] = pd

                def emit_mask(ti):
                    B.tt("dve", ATR[ti % 3][:], PAS[ti % 3][:, 0:128], (MHF if z == 0 else MHB)[:], ALU.mult)

                emit_pre_pe(0)
                emit_mask(0)
                pending = None
                for ti, i in enumerate(order):
                    sl = slice(i * 128, (i + 1) * 128)
                    if ti + 1 < NT:
                        emit_pre_pe(ti + 1)
                    ATb = ATR[ti % 3]
                    po = P()
                    halves = (0, 1, 2, 3) if z == 0 else (3, 2, 1, 0)
                    for idx, hf in enumerate(halves):
                        c = 4 * i + hf
                        lo = hf * 32
                        scur, snext = SR[cidx % RING], SR[(cidx + 1) % RING]
                        spb = SPR[cidx % RING]
                        B.stt("dve", snext[:], scur[:], EL2[:, c:c + 1], PDS[ti % 3][:, hf * 128:(hf + 1) * 128], ALU.mult, ALU.add)
                        if idx == 0 and ti + 1 < NT:
                            emit_mask(ti + 1)
                        if idx == 1 and pending is not None:
                            B.tt("dve", OA[:, pending[0]], OA[:, pending[0]], pending[1][:, 0:128], ALU.add)
                            pending = None
                        B.act(spb[:], scur[:], AF.Identity, scale=EM[:, c:c + 1])
                        B.mm(po[:, lo:lo + 32], [(VT[:, i, :], ATb[:, lo:lo + 32]), (spb[:], QT2[:, c * 32:(c + 1) * 32])])
                        cidx += 1
                    if z == 0:
                        B.copy("act", OA[:, sl], po[:, 0:128])
                    else:
                        pending = (sl, po)
                if pending is not None:
                    B.tt("dve", OA[:, pending[0]], OA[:, pending[0]], pending[1][:, 0:128], ALU.add)
                    pending = None
            HT = T // 2
            for hs in range(2):
                r0 = hs * HT
                B.act(EEh[hs][:], OA[:, r0:r0 + HT], AF.Square)
                B.dma("sp", KKh[hs][:], PXF[4 + h][:, r0:r0 + HT], ikey=("PXF", 4 + h))
            for hs in range(2):
                r0 = hs * HT
                c = 0
                while c < HT:
                    n = min(512, HT - c)
                    p = P()
                    B.mm(p[:, 0:n], [(ONESF[:], EEh[hs][:, c:c + n])])
                    B.act(FFh[hs][:, c:c + n], p[:, 0:n], AF.Ln, scale=1.0 / 128, bias=EPS)
                    c += n
                B.act(FFh[hs][:], FFh[hs][:], AF.Exp, scale=-0.5)
            OBh = QT2
            for hs in range(2):
                r0 = hs * HT
                B.tt("dve", OA[:, r0:r0 + HT], OA[:, r0:r0 + HT], FFh[hs][:], ALU.mult)
                B.stt("dve", OBh[:, r0:r0 + HT], OA[:, r0:r0 + HT], col("hgn", l * 4 + h), KKh[hs][:], ALU.mult, ALU.mult)
            B.dma("sp", BR[h], OBh[:], okey=("BR", h))
            if debug:
                B.copy("dve", OA[:], OBh[:])
                B.dma("sp", BRdbg[h], OA[:])
            B.pop()
        B.pop()

        B.mark("L%d_p3" % l)
        xsv = xs_new.rearrange("k p t -> p k t")
        hxv = HXD.rearrange("k p t -> p k t")
        brv = BR.rearrange("k p t -> p k t")
        yv = y_d.rearrange("k p t -> p k t")
        XSK = [("XS", id(xs_new), k) for k in range(8)]
        wv = win_d[l].rearrange("(k p) f -> p k f", p=128)
        B.push()
        WGr = B.sb("WGr", [128, 8, 3072], BF16)
        WBr = B.sb("WBr", [128, 12, 1024], BF16)
        WOr = B.sb("WOr", [128, 8, 1024], BF16)
        XBs = [B.sb("XB%d" % i, [128, 8, 512]) for i in range(2)]
        HXBs = [B.sb("HXB%d" % i, [128, 8, 512], BF16) for i in range(2)]
        BRBs = [B.sb("BRB%d" % i, [128, 12, 512], BF16) for i in range(2)]
        MG = B.sb("MG", [128, 8, 512], BF16)
        GS = [B.sb("GS%d" % i, [128, 512]) for i in range(3)]
        TMP3 = [B.sb("TMP3%d" % i, [128, 512]) for i in range(2)]
        ACC3 = B.sb("ACC3", [128, 512])
        wgv = WGbf.rearrange("(k p) f -> p k f", p=128)
        wbv = WBbf.rearrange("(q p) f -> p q f", p=128)
        wov = WObf.rearrange("(k p) f -> p k f", p=128)
        K4 = lambda nm: [(nm, r4) for r4 in range(4)]
        for nb in range(3):
            for hh in range(2):
                B.dma("sp", WGr[:, hh * 4:(hh + 1) * 4, nb * 1024:(nb + 1) * 1024],
                      wgv[:, hh * 4:(hh + 1) * 4, nb * 1024:(nb + 1) * 1024], ikey=K4("WGbf"))
            B.dma("sp", WBr[:, nb * 4:(nb + 1) * 4, :], wbv[:, nb * 4:(nb + 1) * 4, :], ikey=[("WBbf", r4) for r4 in range(3)])
        for hh in range(2):
            B.dma("sp", WOr[:, hh * 4:(hh + 1) * 4, :], wov[:, hh * 4:(hh + 1) * 4, :], ikey=K4("WObf"))
        blist = [(bi, c0, n) for bi, (c0, n) in enumerate(blocks) if not (bi == 0 and last)]

        def load3a(bi, c0, n):
            if bi != 0:
                B.dma("sp", XBs[bi % 2][:], xsv[:, :, c0 - CTX:c0 - CTX + n], ikey=XSK)
            B.dma("sp", HXBs[bi % 2][:, :, 0:n], hxv[:, :, c0:c0 + n], ikey=[("HXD", k) for k in range(8)])
            B.dma("sp", BRBs[bi % 2][:, :, 0:n], brv[:, :, c0:c0 + n], ikey=[("BR", k) for k in range(12)])

        load3a(*blist[0])
        for bpos, (bi, c0, n) in enumerate(blist):
            isctx = bi == 0
            if bpos + 1 < len(blist):
                load3a(*blist[bpos + 1])
            w_ = 1 if isctx else 0
            HXB, BRB = HXBs[bi % 2], BRBs[bi % 2]
            xb = CX if isctx else XBs[bi % 2]
            for j in range(8):
                js = slice(j * 128, (j + 1) * 128)
                for nb in range(3):
                    pg = P()
                    B.mm(pg[:, 0:n], [(WGr[:, k, nb * 1024 + j * 128: nb * 1024 + (j + 1) * 128], HXB[:, k, 0:n]) for k in range(8)])
                    gs = GS[nb]
                    B.act(gs[:, 0:n], pg[:, 0:n], AF.Sigmoid, bias=col("bgate", l * 24 + nb * 8 + j))
                    pb = P()
                    B.mm(pb[:, 0:n], [(WBr[:, nb * 4 + k, js], BRB[:, nb * 4 + k, 0:n]) for k in range(4)])
                    if nb == 0:
                        B.tt("dve", ACC3[:, 0:n], gs[:, 0:n], pb[:, 0:n], ALU.mult)
                    elif nb == 1:
                        B.tt("dve", TMP3[0][:, 0:n], gs[:, 0:n], pb[:, 0:n], ALU.mult)
                        B.tt("pool", ACC3[:, 0:n], ACC3[:, 0:n], TMP3[0][:, 0:n], ALU.add)
                    else:
                        B.tt("dve", TMP3[1][:, 0:n], gs[:, 0:n], pb[:, 0:n], ALU.mult)
                        B.tt("pool", MG[:, j, 0:n], ACC3[:, 0:n], TMP3[1][:, 0:n], ALU.add)
            for j in range(8):
                js = slice(j * 128, (j + 1) * 128)
                py = P()
                B.mm(py[:, 0:n], [(WOr[:, k, js], MG[:, k, 0:n]) for k in range(8)])
                B.stt("dve", xb[:, j, 0:n], py[:, 0:n], modc(l, 2, j, w_), xb[:, j, 0:n], ALU.mult, ALU.add)
            if not isctx:
                B.dma("sp", xsv[:, :, c0 - CTX:c0 - CTX + n], xb[:], okey=XSK)
            for k in range(8):
                B.act(MG[:, k, 0:n], xb[:, k, 0:n], AF.Square)
            pss = P()
            B.mm(pss[:, 0:n], [(ONESB[:], MG[:, k, 0:n]) for k in range(8)])
            B.act(ACC3[:, 0:n], pss[:, 0:n], AF.Ln, scale=1.0 / D, bias=EPS)
            B.act(ACC3[:, 0:n], ACC3[:, 0:n], AF.Exp, scale=-0.5)
            for k in range(8):
                tm = TMP3[k % 2]
                B.tt("dve", tm[:, 0:n], xb[:, k, 0:n], ACC3[:, 0:n], ALU.mult)
                B.act(HXB[:, k, 0:n], tm[:, 0:n], AF.Identity, scale=a2c(l, k, w_), bias=modc(l, 3, k, w_))
            B.dma("sp", hxv[:, :, c0:c0 + n], HXB[:, :, 0:n], okey=[("HXD", k) for k in range(8)], ikey=HXB[:])
        B.pop()
        B.mark("L%d_p3b" % l)
        B.push()
        WFIr = B.sb("WFIr", [128, 8, 2 * FFN], BF16)
        WFOr = B.sb("WFOr", [128, 22, 1024], BF16)
        NB2 = 384
        XB2 = [B.sb("XB2%d" % i, [128, 8, NB2]) for i in range(2)]
        H2s = [B.sb("H2%d" % i, [128, 8, NB2], BF16) for i in range(2)]
        ACTB = B.sb("ACTB", [128, 22, NB2], BF16)
        SQB = ACTB
        GS2 = [B.sb("GS2%d" % i, [128, NB2]) for i in range(3)]
        RS = B.sb("RS", [128, NB2])
        wfiv = WFIbf.rearrange("(k p) f -> p k f", p=128)
        wfov = WFObf.rearrange("(m p) f -> p m f", p=128)
        for pc_ in range(11):
            B.dma("sp", WFIr[:, :, pc_ * 512:(pc_ + 1) * 512], wfiv[:, :, pc_ * 512:(pc_ + 1) * 512], ikey=K4("WFIbf"))
        B.dma("sp", WFOr[:, 0:11, :], wfov[:, 0:11, :], ikey=K4("WFObf"))
        B.dma("sp", WFOr[:, 11:22, :], wfov[:, 11:22, :], ikey=K4("WFObf"))
        blocks2 = [(0, CTX)] + [(CTX + i * NB2, NB2) for i in range(SEQ // NB2)]
        if SEQ % NB2:
            blocks2.append((CTX + (SEQ // NB2) * NB2, SEQ % NB2))
        blist2 = [(bi, c0, n) for bi, (c0, n) in enumerate(blocks2) if not (bi == 0 and last)]

        def load3b(bi, c0, n):
            if bi != 0:
                B.dma("sp", XB2[bi % 2][:, :, 0:n], xsv[:, :, c0 - CTX:c0 - CTX + n], ikey=XSK)
            B.dma("sp", H2s[bi % 2][:, :, 0:n], hxv[:, :, c0:c0 + n], ikey=[("HXD", k) for k in range(8)])

        load3b(*blist2[0])
        for bpos, (bi, c0, n) in enumerate(blist2):
            isctx = bi == 0
            if bpos + 1 < len(blist2):
                load3b(*blist2[bpos + 1])
            w_ = 1 if isctx else 0
            xb = CX if isctx else XB2[bi % 2]
            H2 = H2s[bi % 2]
            for m in range(22):
                pg = P()
                B.mm(pg[:, 0:n], [(WFIr[:, k, m * 128:(m + 1) * 128], H2[:, k, 0:n]) for k in range(8)])
                pu = P()
                B.mm(pu[:, 0:n], [(WFIr[:, k, FFN + m * 128:FFN + (m + 1) * 128], H2[:, k, 0:n]) for k in range(8)])
                gs = GS2[m % 3]
                B.act(gs[:, 0:n], pg[:, 0:n], AF.Silu)
                B.tt("dve", ACTB[:, m, 0:n], gs[:, 0:n], pu[:, 0:n], ALU.mult)
            for j in range(8):
                py = P()
                B.mm(py[:, 0:n], [(WFOr[:, m, j * 128:(j + 1) * 128], ACTB[:, m, 0:n]) for m in range(22)])
                B.stt("dve", xb[:, j, 0:n], py[:, 0:n], modc(l, 5, j, w_), xb[:, j, 0:n], ALU.mult, ALU.add)
            if isctx:
                continue
            if l == depth - 1:
                for k in range(8):
                    B.act(SQB[:, k, 0:n], xb[:, k, 0:n], AF.Square)
                pss = P()
                B.mm(pss[:, 0:n], [(ONESB[:], SQB[:, k, 0:n]) for k in range(8)])
                B.act(RS[:, 0:n], pss[:, 0:n], AF.Ln, scale=1.0 / D, bias=EPS)
                B.act(RS[:, 0:n], RS[:, 0:n], AF.Exp, scale=-0.5)
                for k in range(8):
                    B.stt("dve", xb[:, k, 0:n], xb[:, k, 0:n], col("fnorm", k), RS[:, 0:n], ALU.mult, ALU.mult)
                B.dma("sp", yv[:, :, c0 - CTX:c0 - CTX + n], xb[:, :, 0:n], okey=("Y", bi))
            else:
                B.dma("sp", xsv[:, :, c0 - CTX:c0 - CTX + n], xb[:, :, 0:n], okey=XSK)
        B.pop()

    B.mark("end")
    B.barrier()
    return B


def _cols_for(inp, b):
    c = np.zeros((128, NCOLS), np.float32)

    def put(name, arr):
        o, w = COLS[name]
        arr = np.asarray(arr, np.float32)
        assert arr.shape == (128, w), (name, arr.shape, w)
        c[:, o:o + w] = arr

    def colform(v):
        v = np.asarray(v, np.float32)
        lead = v.shape[:-1]
        n = v.shape[-1] // 128
        return np.moveaxis(v.reshape(*lead, n, 128), -1, 0)

    put("c", colform(inp["c"][b]))
    put("cctx", colform(inp["c_ctx"]))
    put("fnorm", colform(inp["final_norm"]))
    put("ln1", colform(inp["ln1"]).reshape(128, 32))
    put("ln2", colform(inp["ln2"]).reshape(128, 32))
    b_in = np.asarray(inp["b_in"], np.float32)
    fm_cols = np.concatenate([np.arange(g, g + 512) for g in FM_GROUP_COL])
    put("bfm", colform(b_in[:, fm_cols]).reshape(128, 4 * 32))
    put("bgate", colform(b_in[:, GATE_COL:GATE_COL + 3072]).reshape(128, 4 * 24))
    put("lbraw", colform(inp["hg_lb_raw"]).reshape(128, 32))
    put("hgn", colform(inp["hg_norm"]).reshape(128, 16))
    put("mln", colform(inp["ml_norm"]).reshape(128, 16))
    put("convw", colform(inp["conv_w"]).reshape(128, 64))
    put("convb", colform(inp["conv_b"]).reshape(128, 16))
    put("lgb", colform(inp["lru_gate_b"]).reshape(128, 64))
    put("lam", colform(inp["lru_lambda"]).reshape(128, 32))
    return c


def make_in_maps(inp, cores):
    inp = {k: np.asarray(v) for k, v in inp.items()}
    b_in = inp["b_in"].astype(np.float32)
    tm_cols = np.concatenate([np.arange(512, 1024), np.arange(3072, 3584), np.arange(2816, 3072), np.arange(4096, 4112)])
    shared = {
        "btm": np.ascontiguousarray(b_in[:, tm_cols]),
        "b_ada": np.ascontiguousarray(inp["b_ada"].astype(np.float32).reshape(1, -1)),
        "w_ada": inp["w_ada"], "w_in": inp["w_in"], "w_branch": inp["w_branch"], "w_out": inp["w_out"],
        "w_ffn_in": inp["w_ffn_in"], "w_ffn_out": inp["w_ffn_out"], "lru_gate_w": inp["lru_gate_w"],
    }
    maps = []
    for b in cores:
        m = dict(shared)
        m["xT"] = np.ascontiguousarray(inp["x"][b].T).reshape(8, 128, SEQ)
        m["cxT"] = np.ascontiguousarray(inp["ctx"][b].T).reshape(8, 128, CTX)
        m["cols"] = _cols_for(inp, b)
        maps.append(m)
    return maps


def unpack_out(yT, depth=DEPTH):
    y = np.asarray(yT).reshape(D, SEQ)
    if (depth - 1) % 2 == 1:
        y = y.reshape(D, 64, 64).transpose(0, 2, 1).reshape(D, SEQ)
    return np.ascontiguousarray(y.T)


def kernel(**inputs):
    B = build(DEPTH, False)
    maps = make_in_maps(inputs, list(range(8)))
    res = run_bass_kernel_spmd(B.nc, maps, core_ids=list(range(8)))
    out = np.stack([unpack_out(r["yT"]) for r in res.results], axis=0)
    return out.astype(np.float32)
```
